# Optimizing a Trainium2 kernel written in Bass

```python
import functools
import jax, jax.numpy as jnp
from jax import lax
import numpy as np

D_MODEL = 1024
BATCH = 2
SEQ = 8192
DEPTH = 4
DEC_BATCH = 32
DEC_SEQ = 1
PAST_LEN = 8192
PAGE_SIZE = 128

SSD_HEAD_DIM = 64
SSD_INNER = D_MODEL
SSD_HEADS = SSD_INNER // SSD_HEAD_DIM
SSD_GROUPS = 2
SSD_REP = SSD_HEADS // SSD_GROUPS
SSD_STATE = 128
SSD_CONV = 4
SSD_CONV_DIM = SSD_INNER + 2 * SSD_GROUPS * SSD_STATE
SSD_CHUNK = 128
CF_DIM = D_MODEL
CF_KERNEL = 31
ATTN_HEADS = 16
ATTN_HEAD_DIM = D_MODEL // ATTN_HEADS
ATTN_KV_HEADS = 4
ATTN_REP = ATTN_HEADS // ATTN_KV_HEADS
ATTN_WIDTH = ATTN_HEADS * ATTN_HEAD_DIM
KV_WIDTH = ATTN_KV_HEADS * ATTN_HEAD_DIM
IDX_HEADS = 8
IDX_DIM = 64
TOPK_MAX = 256
Q_BLOCK = 128

EVEN_SPLITS = (SSD_INNER, SSD_INNER + SSD_CONV_DIM, SSD_INNER + SSD_CONV_DIM + SSD_HEADS,
               SSD_INNER + SSD_CONV_DIM + SSD_HEADS + 2 * CF_DIM)
EVEN_IN = SSD_INNER + SSD_CONV_DIM + SSD_HEADS + 3 * CF_DIM
ODD_SPLITS = (ATTN_WIDTH, ATTN_WIDTH + KV_WIDTH, ATTN_WIDTH + 2 * KV_WIDTH,
              ATTN_WIDTH + 2 * KV_WIDTH + IDX_HEADS * IDX_DIM,
              ATTN_WIDTH + 2 * KV_WIDTH + IDX_HEADS * IDX_DIM + IDX_DIM,
              ATTN_WIDTH + 2 * KV_WIDTH + IDX_HEADS * IDX_DIM + IDX_DIM + IDX_HEADS)
ODD_IN = 2 * ATTN_WIDTH + 2 * KV_WIDTH + IDX_HEADS * IDX_DIM + IDX_DIM + IDX_HEADS
N_EVEN = (DEPTH + 1) // 2
N_ODD = DEPTH // 2

ALPHA = (2 * DEPTH) ** 0.25
BETA = (8 * DEPTH) ** -0.25
EPS = 1e-5

kernel_name = 'hybrid_ssd_conformer_dsa_step'


def layer_norm(x, g, b):
    xf = x.astype(jnp.float32)
    mu = jnp.mean(xf, -1, keepdims=True)
    var = jnp.mean(jnp.square(xf - mu), -1, keepdims=True)
    return ((xf - mu) * lax.rsqrt(var + EPS) * g + b).astype(x.dtype)


def causal_dwconv(x_ext, w, b):
    c = x_ext.shape[-1]
    y = lax.conv_general_dilated(x_ext, w.astype(x_ext.dtype)[:, None, :], window_strides=(1,),
                                 padding='VALID', dimension_numbers=('NWC', 'WIO', 'NWC'),
                                 feature_group_count=c)
    return y + b.astype(x_ext.dtype)


def ssd_chunked(xs, dt, a, bm, cm):
    f32 = jnp.float32
    b, l, g, r, p = xs.shape
    n = bm.shape[-1]
    q = SSD_CHUNK
    c = l // q
    X = (xs.astype(f32) * dt[..., None]).reshape(b, c, q, g, r, p)
    a_cum = jnp.cumsum((dt * a).reshape(b, c, q, g, r), axis=2)
    Bc = bm.astype(f32).reshape(b, c, q, g, n)
    Cc = cm.astype(f32).reshape(b, c, q, g, n)
    diff = a_cum[:, :, :, None] - a_cum[:, :, None, :]
    causal = jnp.tril(jnp.ones((q, q), bool))[:, :, None, None]
    decay = jnp.exp(jnp.where(causal, diff, -jnp.inf))
    cb = jnp.einsum('bctgn,bcsgn->bctsg', Cc, Bc)
    y_diag = jnp.einsum('bctsg,bctsgr,bcsgrp->bctgrp', cb, decay, X)
    to_end = jnp.exp(a_cum[:, :, -1:] - a_cum)
    chunk_states = jnp.einsum('bcsgn,bcsgr,bcsgrp->bcgrpn', Bc, to_end, X)
    chunk_decay = jnp.exp(a_cum[:, :, -1])

    def step(state, inp):
        dec, st = inp
        return state * dec[..., None, None] + st, state

    init = jnp.zeros((b, g, r, p, n), f32)
    final, prev = lax.scan(step, init, (jnp.moveaxis(chunk_decay, 1, 0), jnp.moveaxis(chunk_states, 1, 0)))
    prev = jnp.moveaxis(prev, 0, 1)
    y_off = jnp.einsum('bctgn,bcgrpn,bctgr->bctgrp', Cc, prev, jnp.exp(a_cum))
    return (y_diag + y_off).reshape(b, l, g, r, p), final


def ssd_recurrent(xs, dt, a, bm, cm, state):
    f32 = jnp.float32
    b = xs.shape[0]
    s0 = state.astype(f32).reshape(b, SSD_GROUPS, SSD_REP, SSD_HEAD_DIM, SSD_STATE)

    def step(s, inp):
        x_t, dt_t, b_t, c_t = inp
        s = s * jnp.exp(dt_t * a)[..., None, None] + jnp.einsum('bgrp,bgn->bgrpn', x_t * dt_t[..., None], b_t)
        return s, jnp.einsum('bgrpn,bgn->bgrp', s, c_t)

    seq_first = lambda t: jnp.moveaxis(t.astype(f32), 1, 0)
    final, y = lax.scan(step, s0, (seq_first(xs), seq_first(dt), seq_first(bm), seq_first(cm)))
    return jnp.moveaxis(y, 0, 1), final


def even_layer(x, ssd_ctx, cf_ctx, ssd_scan, w_in, conv_w, conv_b, dt_bias, a_log, d_skip, norm_g,
               cf_w, cf_b, cf_g, cf_beta, w_out, ln_g, ln_b):
    f32 = jnp.float32
    b, l, _ = x.shape
    z_a, xbc, dt, glu_in, z_b = jnp.split(x @ w_in, EVEN_SPLITS, axis=-1)
    xbc_ext = jnp.concatenate([ssd_ctx.astype(x.dtype), xbc], axis=1)
    xbc_c = jax.nn.silu(causal_dwconv(xbc_ext, conv_w, conv_b))
    xs, bm, cm = jnp.split(xbc_c, (SSD_INNER, SSD_INNER + SSD_GROUPS * SSD_STATE), axis=-1)
    xs = xs.reshape(b, l, SSD_GROUPS, SSD_REP, SSD_HEAD_DIM)
    bm = bm.reshape(b, l, SSD_GROUPS, SSD_STATE)
    cm = cm.reshape(b, l, SSD_GROUPS, SSD_STATE)
    dt = jax.nn.softplus((dt + dt_bias).astype(f32)).reshape(b, l, SSD_GROUPS, SSD_REP)
    a = -jnp.exp(a_log.astype(f32)).reshape(SSD_GROUPS, SSD_REP)
    y, ssm_final = ssd_scan(xs, dt, a, bm, cm)
    y = y + d_skip.astype(f32).reshape(SSD_GROUPS, SSD_REP)[..., None] * xs.astype(f32)
    y = y.reshape(b, l, SSD_INNER) * jax.nn.silu(z_a.astype(f32))
    y = y.reshape(b, l, SSD_GROUPS, SSD_INNER // SSD_GROUPS)
    y = y * lax.rsqrt(jnp.mean(jnp.square(y), -1, keepdims=True) + EPS)
    y_a = (y.reshape(b, l, SSD_INNER) * norm_g).astype(x.dtype)
    u_val, u_gate = jnp.split(glu_in, 2, axis=-1)
    u = u_val * jax.nn.sigmoid(u_gate)
    u_ext = jnp.concatenate([cf_ctx.astype(x.dtype), u], axis=1)
    v = jax.nn.silu(layer_norm(causal_dwconv(u_ext, cf_w, cf_b), cf_g, cf_beta))
    y_b = v * jax.nn.silu(z_b)
    out = jnp.concatenate([y_a, y_b], axis=-1) @ w_out
    x_new = layer_norm(ALPHA * x + out, ln_g, ln_b)
    return (x_new, ssm_final.reshape(b, SSD_HEADS, SSD_HEAD_DIM, SSD_STATE).astype(x.dtype),
            xbc_ext[:, -(SSD_CONV - 1):], u_ext[:, -(CF_KERNEL - 1):])


def odd_project(x, w_in):
    b, l, _ = x.shape
    q, k, v, qi, ki, wi, z = jnp.split(x @ w_in, ODD_SPLITS, axis=-1)
    q = q.reshape(b, l, ATTN_KV_HEADS, ATTN_REP, ATTN_HEAD_DIM)
    k = k.reshape(b, l, ATTN_KV_HEADS, ATTN_HEAD_DIM)
    v = v.reshape(b, l, ATTN_KV_HEADS, ATTN_HEAD_DIM)
    qi = qi.reshape(b, l, IDX_HEADS, IDX_DIM)
    return q, k, v, qi, ki, wi, z


def index_scores(qi, wi, ki):
    s = jnp.einsum('bthd,bsd->bths', qi, ki, preferred_element_type=jnp.float32) * IDX_DIM ** -0.5
    return jnp.einsum('bths,bth->bts', jax.nn.relu(s), wi.astype(jnp.float32) * IDX_HEADS ** -0.5)


def dsa_attend(q, qi, wi, q_pos, ki_all, gather_kv, topk):
    scores = index_scores(qi, wi, ki_all)
    key_pos = jnp.arange(ki_all.shape[1])
    admissible = key_pos[None, :] <= q_pos[:, None]
    scores = jnp.where(admissible[None], scores, -jnp.inf)
    _, idx = lax.top_k(scores, topk)
    valid = idx <= q_pos[None, :, None]
    k_sel, v_sel = gather_kv(idx)
    logits = jnp.einsum('btgrd,btkgd->btgrk', q, k_sel, preferred_element_type=jnp.float32) * ATTN_HEAD_DIM ** -0.5
    logits = jnp.where(valid[:, :, None, None, :], logits, -jnp.inf)
    p = jax.nn.softmax(logits, axis=-1)
    return jnp.einsum('btgrk,btkgd->btgrd', p.astype(v_sel.dtype), v_sel)


def odd_out(x, o, z, w_out, ln_g, ln_b):
    b, l, _ = x.shape
    out = (o.reshape(b, l, ATTN_WIDTH) * jax.nn.silu(z)) @ w_out
    return layer_norm(ALPHA * x + out, ln_g, ln_b)


def odd_layer_prompt(x, w_in, w_out, ln_g, ln_b):
    b, l, _ = x.shape
    q, k, v, qi, ki, wi, z = odd_project(x, w_in)
    topk = min(TOPK_MAX, l // 4)
    take = jax.vmap(lambda arr, i: arr[i])

    def gather_kv(idx):
        return take(k, idx), take(v, idx)

    def block(i):
        start = i * Q_BLOCK
        sl = lambda t: lax.dynamic_slice_in_dim(t, start, Q_BLOCK, axis=1)
        pos = start + jnp.arange(Q_BLOCK)
        return dsa_attend(sl(q), sl(qi), sl(wi), pos, ki, gather_kv, topk)

    o = lax.map(block, jnp.arange(l // Q_BLOCK))
    o = jnp.moveaxis(o, 0, 1)
    return odd_out(x, o, z, w_out, ln_g, ln_b), k, v, ki


def odd_layer_sample(x, cache_k, cache_v, cache_ki, page_table, w_in, w_out, ln_g, ln_b):
    b, t, _ = x.shape
    q, k, v, qi, ki, wi, z = odd_project(x, w_in)
    past = page_table.shape[1] * PAGE_SIZE
    topk = min(TOPK_MAX, (past + t) // 4)
    ki_past = cache_ki[page_table].reshape(b, past, IDX_DIM)
    ki_all = jnp.concatenate([ki_past.astype(ki.dtype), ki], axis=1)
    take = jax.vmap(lambda arr, i: arr[i])

    def gather_kv(idx):
        in_past = idx < past
        pidx = jnp.minimum(idx, past - 1)
        phys = jnp.take_along_axis(page_table, pidx.reshape(b, -1) // PAGE_SIZE, axis=1).reshape(idx.shape)
        off = pidx % PAGE_SIZE
        nidx = jnp.clip(idx - past, 0, t - 1)

        def sel(cache, new):
            return jnp.where(in_past[..., None, None], cache[phys, off].astype(new.dtype), take(new, nidx))

        return sel(cache_k, k), sel(cache_v, v)

    pos = past + jnp.arange(t)
    o = dsa_attend(q, qi, wi, pos, ki_all, gather_kv, topk)
    return odd_out(x, o, z, w_out, ln_g, ln_b), k, v, ki


def setup_inputs(seed: int = 0) -> dict:
    key = jax.random.key(seed)
    keys = iter(jax.random.split(key, 40))

    def nrm(shape, scale=1.0):
        return jax.random.normal(next(keys), shape, jnp.float32) * scale

    n_pages = PAST_LEN // PAGE_SIZE
    n_pool = (DEC_BATCH * n_pages * 5) // 4
    ssm_shape = (DEC_BATCH, SSD_HEADS, SSD_HEAD_DIM, SSD_STATE)
    ssdconv_shape = (DEC_BATCH, SSD_CONV - 1, SSD_CONV_DIM)
    cfconv_shape = (DEC_BATCH, CF_KERNEL - 1, CF_DIM)
    kv_shape = (n_pool, PAGE_SIZE, ATTN_KV_HEADS, ATTN_HEAD_DIM)
    kidx_shape = (n_pool, PAGE_SIZE, IDX_DIM)
    inp = {}
    inp['x_prompt'] = nrm((BATCH, SEQ, D_MODEL))
    inp['x_sample'] = nrm((DEC_BATCH, DEC_SEQ, D_MODEL))
    inp['state_ssm_l0'] = nrm(ssm_shape, 0.5)
    inp['state_ssdconv_l0'] = nrm(ssdconv_shape)
    inp['state_cfconv_l0'] = nrm(cfconv_shape, 0.6)
    inp['cache_k_l1'] = nrm(kv_shape)
    inp['cache_v_l1'] = nrm(kv_shape)
    inp['cache_kidx_l1'] = nrm(kidx_shape)
    inp['state_ssm_l2'] = nrm(ssm_shape, 0.5)
    inp['state_ssdconv_l2'] = nrm(ssdconv_shape)
    inp['state_cfconv_l2'] = nrm(cfconv_shape, 0.6)
    inp['cache_k_l3'] = nrm(kv_shape)
    inp['cache_v_l3'] = nrm(kv_shape)
    inp['cache_kidx_l3'] = nrm(kidx_shape)
    perm = jax.random.permutation(next(keys), n_pool)[:DEC_BATCH * n_pages]
    inp['page_table'] = perm.reshape(DEC_BATCH, n_pages).astype(jnp.int32)
    inp['w_in_even'] = nrm((N_EVEN, D_MODEL, EVEN_IN), D_MODEL ** -0.5)
    inp['ssd_conv_w'] = nrm((N_EVEN, SSD_CONV, SSD_CONV_DIM), SSD_CONV ** -0.5)
    inp['ssd_conv_b'] = nrm((N_EVEN, SSD_CONV_DIM), 0.02)
    dt0 = jnp.exp(jax.random.uniform(next(keys), (N_EVEN, SSD_HEADS), jnp.float32,
                                     float(np.log(1e-3)), float(np.log(1e-1))))
    inp['ssd_dt_bias'] = dt0 + jnp.log(-jnp.expm1(-dt0))
    inp['ssd_a_log'] = jnp.log(jax.random.uniform(next(keys), (N_EVEN, SSD_HEADS), jnp.float32, 1.0, 16.0))
    inp['ssd_d'] = 1.0 + nrm((N_EVEN, SSD_HEADS), 0.1)
    inp['ssd_norm_g'] = 1.0 + nrm((N_EVEN, SSD_INNER), 0.05)
    inp['cf_dw_w'] = nrm((N_EVEN, CF_KERNEL, CF_DIM), CF_KERNEL ** -0.5)
    inp['cf_dw_b'] = nrm((N_EVEN, CF_DIM), 0.02)
    inp['cf_ln_g'] = 1.0 + nrm((N_EVEN, CF_DIM), 0.05)
    inp['cf_ln_b'] = nrm((N_EVEN, CF_DIM), 0.02)
    inp['w_out_even'] = nrm((N_EVEN, SSD_INNER + CF_DIM, D_MODEL), BETA * (SSD_INNER + CF_DIM) ** -0.5)
    inp['w_in_odd'] = nrm((N_ODD, D_MODEL, ODD_IN), D_MODEL ** -0.5)
    inp['w_out_odd'] = nrm((N_ODD, ATTN_WIDTH, D_MODEL), BETA * ATTN_WIDTH ** -0.5)
    inp['ln_g'] = 1.0 + nrm((DEPTH, D_MODEL), 0.05)
    inp['ln_b'] = nrm((DEPTH, D_MODEL), 0.02)
    return inp


def reference(x_prompt, x_sample,
              state_ssm_l0, state_ssdconv_l0, state_cfconv_l0,
              cache_k_l1, cache_v_l1, cache_kidx_l1,
              state_ssm_l2, state_ssdconv_l2, state_cfconv_l2,
              cache_k_l3, cache_v_l3, cache_kidx_l3,
              page_table,
              w_in_even, ssd_conv_w, ssd_conv_b, ssd_dt_bias, ssd_a_log, ssd_d, ssd_norm_g,
              cf_dw_w, cf_dw_b, cf_ln_g, cf_ln_b, w_out_even,
              w_in_odd, w_out_odd, ln_g, ln_b):
    ssm_states = (state_ssm_l0, state_ssm_l2)
    ssdconv_states = (state_ssdconv_l0, state_ssdconv_l2)
    cfconv_states = (state_cfconv_l0, state_cfconv_l2)
    k_caches = (cache_k_l1, cache_k_l3)
    v_caches = (cache_v_l1, cache_v_l3)
    kidx_caches = (cache_kidx_l1, cache_kidx_l3)
    yp, ys = x_prompt, x_sample
    new_state = []
    for layer in range(DEPTH):
        j = layer // 2
        if layer % 2 == 0:
            params = (w_in_even[j], ssd_conv_w[j], ssd_conv_b[j], ssd_dt_bias[j], ssd_a_log[j], ssd_d[j],
                      ssd_norm_g[j], cf_dw_w[j], cf_dw_b[j], cf_ln_g[j], cf_ln_b[j], w_out_even[j],
                      ln_g[layer], ln_b[layer])
            bp = yp.shape[0]
            ssd_ctx0 = jnp.zeros((bp, SSD_CONV - 1, SSD_CONV_DIM), yp.dtype)
            cf_ctx0 = jnp.zeros((bp, CF_KERNEL - 1, CF_DIM), yp.dtype)
            yp, ssm_p, sc_p, cc_p = even_layer(yp, ssd_ctx0, cf_ctx0, ssd_chunked, *params)
            ys, ssm_s, sc_s, cc_s = even_layer(ys, ssdconv_states[j], cfconv_states[j],
                                               functools.partial(ssd_recurrent, state=ssm_states[j]), *params)
            new_state += [ssm_p, ssm_s, sc_p, sc_s, cc_p, cc_s]
        else:
            params = (w_in_odd[j], w_out_odd[j], ln_g[layer], ln_b[layer])
            yp, k_p, v_p, ki_p = odd_layer_prompt(yp, *params)
            ys, k_s, v_s, ki_s = odd_layer_sample(ys, k_caches[j], v_caches[j], kidx_caches[j], page_table, *params)
            new_state += [k_p, k_s, v_p, v_s, ki_p, ki_s]
    return (yp, ys, *new_state)
```

```python
import contextlib
import os
import numpy as np
STAGE = float(os.environ.get("KSTAGE", "99"))
SKIP = set(os.environ.get("KSKIP", "").split(","))
SODD = float(os.environ.get("SODD", "99"))
import concourse.bass as bass
import concourse.mybir as mybir
from concourse.bass_utils import run_bass_kernel_spmd

F32 = mybir.dt.float32
BF16 = mybir.dt.bfloat16
I32 = mybir.dt.int32
U32 = mybir.dt.uint32
ALU = mybir.AluOpType
AF = mybir.ActivationFunctionType
AX = mybir.AxisListType

NDS = 24
D = 1024
EVEN_IN = 5648
ODD_IN = 3144
ALPHA = 8.0 ** 0.25
EPS = 1e-5
NEG = -1.0e5


class Res:
    __slots__ = ("name", "w", "r", "ex")

    def __init__(self, name=""):
        self.name = name
        self.w = None
        self.r = {}
        self.ex = False


class Sched:
    def __init__(self, nc, es):
        self.nc = nc
        self.eng = {"pe": nc.tensor, "dve": nc.vector, "act": nc.scalar, "pool": nc.gpsimd, "sp": nc.sync}
        self.semh = {}
        self.cnt = {}
        for k in self.eng:
            self.semh[k] = es.enter_context(nc.semaphore("s_" + k))
            self.cnt[k] = 0
        self.known = {k: {} for k in self.eng}
        self.dq = {}
        for q in ("sp", "pool", "act"):
            keys = []
            for i in range(NDS):
                key = "d_%s_%d" % (q, i)
                self.semh[key] = es.enter_context(nc.semaphore(key))
                self.cnt[key] = 0
                keys.append(key)
            self.dq[q] = [keys, 0]
        self.ninst = 0

    def _waits(self, e, reads, writes):
        need = {}
        for r in reads:
            if r.w is not None:
                k, v = r.w
                if need.get(k, 0) < v:
                    need[k] = v
        for w in writes:
            if w.w is not None:
                k, v = w.w
                if need.get(k, 0) < v:
                    need[k] = v
            for k, v in w.r.items():
                if need.get(k, 0) < v:
                    need[k] = v
        known = self.known[e]
        for k, v in need.items():
            if known.get(k, 0) >= v:
                continue
            self.eng[e].wait_ge(self.semh[k], v)
            known[k] = v

    def _commit(self, tok, reads, writes):
        k, v = tok
        for r in reads:
            if r.r.get(k, 0) < v:
                r.r[k] = v
        for w in writes:
            w.w = tok
            w.r = {}

    def op(self, e, fn, reads=(), writes=(), acc_writes=()):
        if e != "pe":
            exr = [r for r in reads if r.ex]
            if exr:
                writes = list(writes) + exr
                reads = [r for r in reads if not r.ex]
        self._waits(e, reads, writes)
        inst = fn()
        self.cnt[e] += 1
        inst.then_inc(self.semh[e], 1)
        self._commit((e, self.cnt[e]), reads, list(writes) + list(acc_writes))
        self.ninst += 1
        return inst

    def dma(self, q, fn, reads=(), writes=()):
        keys, nxt = self.dq[q]
        key = keys[nxt]
        self.dq[q][1] = (nxt + 1) % NDS
        self._waits(q, reads, writes)
        known = self.known[q]
        if known.get(key, 0) < self.cnt[key]:
            self.eng[q].wait_ge(self.semh[key], self.cnt[key])
            known[key] = self.cnt[key]
        inst = fn(self.eng[q])
        self.cnt[key] += 16
        inst.then_inc(self.semh[key], 16)
        self._commit((key, self.cnt[key]), reads, writes)
        self.ninst += 1
        return inst

    def barrier(self):
        for e in self.eng:
            known = self.known[e]
            for k, h in self.semh.items():
                v = self.cnt[k]
                if v == 0 or known.get(k, 0) >= v:
                    continue
                self.eng[e].wait_ge(h, v)
                known[k] = v

    def final_wait(self, e="sp"):
        known = self.known[e]
        for k, h in self.semh.items():
            v = self.cnt[k]
            if v == 0 or known.get(k, 0) >= v:
                continue
            self.eng[e].wait_ge(h, v)
            known[k] = v


class TT:
    def __init__(self, t, name=""):
        self.t = t
        self.res = Res(name)

    def __getitem__(self, idx):
        return self.t[idx]


class Ctx:
    pass


def build(L, layers=4, NS=4, NPOOL=2560, do_prompt=True, do_sample=True):
    NTI = L // 128
    NMT = L // 512
    nc = bass.Bass("TRN2", target_bir_lowering=False)
    C = Ctx()
    C.nc = nc

    def din(name, shape, dt=F32):
        return nc.dram_tensor(name, list(shape), dt, kind="ExternalInput").ap()

    def dout(name, shape, dt=F32):
        return nc.dram_tensor(name, list(shape), dt, kind="ExternalOutput").ap()

    def dscr(name, shape, dt=F32):
        return nc.dram_tensor(name, list(shape), dt, kind="Internal").ap()

    I = {}
    if do_sample:
        for j in (1, 0):
            I["cache_ki%d" % j] = din("cache_ki%d" % j, [NPOOL * 128, 64])
            I["cache_k%d" % j] = din("cache_k%d" % j, [NPOOL * 128, 256])
            I["cache_v%d" % j] = din("cache_v%d" % j, [NPOOL * 128, 256])
    I["x_prompt"] = din("x_prompt", [L, D])
    I["w_in_even"] = din("w_in_even", [2, D, EVEN_IN])
    I["ssd_conv_w"] = din("ssd_conv_w", [2, 4, 1536])
    I["ssd_conv_b"] = din("ssd_conv_b", [2, 1536])
    I["ssd_dt_bias"] = din("ssd_dt_bias", [2, 16])
    I["ssd_a_log"] = din("ssd_a_log", [2, 16])
    I["ssd_d"] = din("ssd_d", [2, 16])
    I["ssd_norm_g"] = din("ssd_norm_g", [2, 1024])
    I["cf_dw_w"] = din("cf_dw_w", [2, 31, 1024])
    I["cf_dw_b"] = din("cf_dw_b", [2, 1024])
    I["cf_ln_g"] = din("cf_ln_g", [2, 1024])
    I["cf_ln_b"] = din("cf_ln_b", [2, 1024])
    I["w_out_even"] = din("w_out_even", [2, 2048, D])
    I["w_in_odd"] = din("w_in_odd", [2, D, ODD_IN])
    I["w_out_odd"] = din("w_out_odd", [2, D, D])
    I["ln_g"] = din("ln_g", [4, D])
    I["ln_b"] = din("ln_b", [4, D])
    if do_sample:
        I["x_sample"] = din("x_sample", [NS, D])
        I["page_table"] = din("page_table", [1, NS * 64], I32)
        for j in range(2):
            I["state_ssm%d" % j] = din("state_ssm%d" % j, [NS, 1024, 128])
            I["state_ssdconv%d" % j] = din("state_ssdconv%d" % j, [NS, 3, 1536])
            I["state_cfconv%d" % j] = din("state_cfconv%d" % j, [NS, 30, 1024])

    O = {}
    O["y_prompt"] = dout("y_prompt", [L, D])
    for j in range(2):
        O["ssm_p%d" % j] = dout("ssm_p%d" % j, [1024, 128])
        O["sc_p%d" % j] = dout("sc_p%d" % j, [3, 1536])
        O["cc_p%d" % j] = dout("cc_p%d" % j, [30, 1024])
        O["k_p%d" % j] = dout("k_p%d" % j, [L, 256])
        O["v_p%d" % j] = dout("v_p%d" % j, [L, 256])
        O["ki_p%d" % j] = dout("ki_p%d" % j, [L, 64])

    if do_sample:
        O["y_s"] = dout("y_s", [NS, D])
        if SODD < 99:
            O["dbg_i"] = dout("dbg_i", [128, 256], I32)
            O["dbg_f"] = dout("dbg_f", [128, 4 * 65])
        for j in range(2):
            O["ssm_s%d" % j] = dout("ssm_s%d" % j, [NS, 1024, 128])
            O["sc_s%d" % j] = dout("sc_s%d" % j, [NS, 3, 1536])
            O["cc_s%d" % j] = dout("cc_s%d" % j, [NS, 30, 1024])
            O["k_s%d" % j] = dout("k_s%d" % j, [NS, 256])
            O["v_s%d" % j] = dout("v_s%d" % j, [NS, 256])
            O["ki_s%d" % j] = dout("ki_s%d" % j, [NS, 64])

    xtm = dscr("xtm", [L, D])
    xT = dscr("xT", [D, L], BF16)
    ycat = dscr("ycat", [2048, L], BF16)
    R_xtm = [Res("xtm%d" % i) for i in range(NTI)]
    R_xT = [Res("xT%d" % i) for i in range(NTI)]
    R_ycat = [Res("ycat%d" % i) for i in range(NTI)]
    C.qT_d = dscr("qT_d", [128, 8, L], BF16)
    C.kT_d = dscr("kT_d", [128, 2, L], BF16)
    C.qiT_d = dscr("qiT_d", [64, 8, L], BF16)
    C.kiT_d = dscr("kiT_d", [64, L], BF16)
    C.V1_d = dscr("V1_d", [L, 260], BF16)
    C.wi_d = dscr("wi_d", [L, 8])
    C.zs_d = dscr("zs_d", [L, 1024], BF16)
    C.R_odd = Res("odd_scratch")

    with contextlib.ExitStack() as es:
        S = Sched(nc, es)
        C.S = S

        C.uid = 0

        def sb(es_, name, shape, dt=F32):
            C.uid += 1
            name = "%s_u%d" % (name, C.uid)
            return TT(es_.enter_context(nc.sbuf_tensor(name, list(shape), dt)), name)

        def ps(es_, name, shape, dt=F32):
            t = TT(es_.enter_context(nc.psum_tensor(name, list(shape), dt)), name)
            t.res.ex = True
            return t

        V = lambda fn, r=(), w=(): S.op("dve", fn, r, w)
        A = lambda fn, r=(), w=(): S.op("act", fn, r, w)
        G = lambda fn, r=(), w=(): S.op("pool", fn, r, w)

        ones_f = sb(es, "ones_f", [128, 128])
        zeros_f = sb(es, "zeros_f", [128, 128])
        ones_bf = sb(es, "ones_bf", [128, 128], BF16)
        ident_f = sb(es, "ident_f", [128, 128])
        ident_bf = sb(es, "ident_bf", [128, 128], BF16)
        U_f = sb(es, "U_f", [128, 128])
        negm = sb(es, "negm", [128, 128])
        Esel = sb(es, "Esel", [16, 16, 128])
        G(lambda: nc.gpsimd.memset(ones_f[:], 1.0), w=[ones_f.res])
        G(lambda: nc.gpsimd.memset(zeros_f[:], 0.0), w=[zeros_f.res])
        G(lambda: nc.gpsimd.memset(ones_bf[:], 1.0), w=[ones_bf.res])
        G(lambda: nc.gpsimd.memset(Esel[:], 1.0), w=[Esel.res])
        G(lambda: nc.gpsimd.affine_select(out=ident_f[:], in_=ones_f[:], pattern=[[-1, 128]], compare_op=ALU.is_equal,
                                          fill=0.0, base=0, channel_multiplier=1), r=[ones_f.res], w=[ident_f.res])
        G(lambda: nc.gpsimd.tensor_copy(out=ident_bf[:], in_=ident_f[:]), r=[ident_f.res], w=[ident_bf.res])
        G(lambda: nc.gpsimd.affine_select(out=U_f[:], in_=ones_f[:], pattern=[[1, 128]], compare_op=ALU.is_ge,
                                          fill=0.0, base=0, channel_multiplier=-1), r=[ones_f.res], w=[U_f.res])
        G(lambda: nc.gpsimd.affine_select(out=negm[:], in_=zeros_f[:], pattern=[[1, 128]], compare_op=ALU.is_ge,
                                          fill=NEG, base=0, channel_multiplier=-1), r=[zeros_f.res], w=[negm.res])
        G(lambda: nc.gpsimd.affine_select(out=Esel[:], in_=Esel[:], pattern=[[-1, 16], [0, 128]], compare_op=ALU.is_equal,
                                          fill=0.0, base=0, channel_multiplier=1), r=[], w=[Esel.res])

        C.do_prompt, C.do_sample, C.NS = do_prompt, do_sample, NS
        if do_sample:
            bmk = sb(es, "bmk", [16, 4])
            negpage = sb(es, "negpage", [128, 1])
            iota_i = sb(es, "iota_i", [128, 1], I32)
            iota_f = sb(es, "iota_f", [128, 1])
            G(lambda: nc.gpsimd.affine_select(out=bmk[:], in_=ones_f[0:16, 0:4], pattern=[[-4, 4]], compare_op=ALU.is_ge,
                                              fill=0.0, base=0, channel_multiplier=1), r=[ones_f.res], w=[bmk.res])
            G(lambda: nc.gpsimd.affine_select(out=bmk[:], in_=bmk[:], pattern=[[4, 4]], compare_op=ALU.is_ge,
                                              fill=0.0, base=3, channel_multiplier=-1), r=[], w=[bmk.res])
            G(lambda: nc.gpsimd.affine_select(out=negpage[:], in_=zeros_f[:, 0:1], pattern=[[0, 1]], compare_op=ALU.is_ge,
                                              fill=-1.0e30, base=0, channel_multiplier=-1), r=[zeros_f.res],
              w=[negpage.res])
            G(lambda: nc.gpsimd.iota(iota_i[:], pattern=[[0, 1]], base=0, channel_multiplier=1), w=[iota_i.res])
            G(lambda: nc.gpsimd.tensor_copy(out=iota_f[:], in_=iota_i[:]), r=[iota_i.res], w=[iota_f.res])
            C.xs_s = sb(es, "xs_s", [NS, D])
            C.xsT = sb(es, "xsT", [128, 8, NS], BF16)
            C.ycT = sb(es, "ycT", [128, 16, NS], BF16)

        pf = [ps(es, "pf%d" % i, [128, 512]) for i in range(6)]
        pb = [ps(es, "pb%d" % i, [128, 1024], BF16) for i in range(2)]
        C.pfi = 0
        C.pbi = 0

        def PF(i=None):
            if i is not None:
                return pf[i]
            t = pf[C.pfi]
            C.pfi = (C.pfi + 1) % len(pf)
            return t

        def PB():
            t = pb[C.pbi]
            C.pbi = (C.pbi + 1) % len(pb)
            return t

        def mm(pt, out_ap, pairs, reads, first=True, start=True, stop=True, sgc=False):
            n = len(pairs)
            for i, (l, r) in enumerate(pairs):
                st_ = start if i == 0 else False
                sp_ = stop if i == n - 1 else False
                if i == 0 and first:
                    S.op("pe", lambda: nc.tensor.matmul(out_ap, lhsT=l, rhs=r, start=st_, stop=sp_,
                                                        skip_group_check=sgc),
                         reads=reads, writes=[pt.res])
                else:
                    S.op("pe", lambda: nc.tensor.matmul(out_ap, lhsT=l, rhs=r, start=st_, stop=sp_,
                                                        skip_group_check=sgc),
                         reads=reads, acc_writes=[pt.res])

        def tr(pt, out_ap, in_ap, reads, bf=True, first=True):
            idn = ident_bf if bf else ident_f
            kp = in_ap.shape[0]
            if first:
                S.op("pe", lambda: nc.tensor.transpose(out=out_ap, in_=in_ap, identity=idn[0:kp, 0:kp]),
                     reads=list(reads) + [idn.res], writes=[pt.res])
            else:
                S.op("pe", lambda: nc.tensor.transpose(out=out_ap, in_=in_ap, identity=idn[0:kp, 0:kp]),
                     reads=list(reads) + [idn.res], acc_writes=[pt.res])


        def load_weight_bf16(esl, w_ap, K, N, dst, name):
            KT = K // 128
            with contextlib.ExitStack() as es2:
                CW = 4096 // KT
                stg = [sb(es2, "%s_stg%d" % (name, i), [128, KT, CW]) for i in range(2)]
                wv = w_ap.rearrange("(k p) n -> p k n", p=128)
                nchunk = (N + CW - 1) // CW
                for c in range(nchunk):
                    c0 = c * CW
                    cw = min(CW, N - c0)
                    st = stg[c % 2]
                    S.dma("sp" if c % 2 == 0 else "pool",
                          lambda q: q.dma_start(out=st[:, :, 0:cw], in_=wv[:, :, c0:c0 + cw]), writes=[st.res])
                    if c % 2 == 0:
                        A(lambda: nc.scalar.copy(out=dst[:, :, c0:c0 + cw], in_=st[:, :, 0:cw]), r=[st.res], w=[dst.res])
                    else:
                        V(lambda: nc.vector.tensor_copy(out=dst[:, :, c0:c0 + cw], in_=st[:, :, 0:cw]), r=[st.res],
                          w=[dst.res])
                S.barrier()

        def prep_xT(x_src):
            with contextlib.ExitStack() as es2:
                xs = [sb(es2, "px%d" % i, [128, D]) for i in range(2)]
                xb = [sb(es2, "pxb%d" % i, [128, D], BF16) for i in range(2)]
                xt = [sb(es2, "pxt%d" % i, [128, 8, 128], BF16) for i in range(2)]
                for i in range(NTI):
                    a, b, c = xs[i % 2], xb[i % 2], xt[i % 2]
                    S.dma("sp", lambda q: q.dma_start(out=a[:], in_=x_src[i * 128:(i + 1) * 128, :]), writes=[a.res])
                    V(lambda: nc.vector.tensor_copy(out=b[:], in_=a[:]), r=[a.res], w=[b.res])
                    p = PB()
                    for k in range(8):
                        tr(p, p[:, k * 128:(k + 1) * 128], b[:, k * 128:(k + 1) * 128], [b.res], first=(k == 0))
                    A(lambda: nc.scalar.copy(out=c[:], in_=p[:].rearrange("p (k t) -> p k t", k=8)), r=[p.res], w=[c.res])
                    S.dma("pool", lambda q: q.dma_start(
                        out=xT.rearrange("(k p) t -> p k t", p=128)[:, :, i * 128:(i + 1) * 128], in_=c[:]),
                        reads=[c.res], writes=[R_xT[i]])
                S.barrier()

        def out_pass(l, KT, w_out_sb, x_src, is_last):
            with contextlib.ExitStack() as es2:
                yt = [sb(es2, "oy%d" % i, [128, KT, 128], BF16) for i in range(2)]
                xs = [sb(es2, "ox%d" % i, [128, D]) for i in range(2)]
                hb = [sb(es2, "oh%d" % i, [128, D]) for i in range(2)]
                xn = [sb(es2, "oxn%d" % i, [128, D]) for i in range(2)]
                xb = [sb(es2, "oxb%d" % i, [128, D], BF16) for i in range(2)]
                xt = [sb(es2, "oxt%d" % i, [128, 8, 128], BF16) for i in range(2)]
                st = [sb(es2, "ost%d" % i, [128, 2, 6]) for i in range(2)]
                mv = [sb(es2, "omv%d" % i, [128, 4]) for i in range(2)]
                dst = O["y_prompt"] if is_last else xtm
                lng = sb(es2, "lng", [128, D])
                lnb = sb(es2, "lnb", [128, D])
                S.dma("sp", lambda q: q.dma_start(out=lng[:], in_=I["ln_g"][l:l + 1, :].to_broadcast([128, D])),
                      writes=[lng.res])
                S.dma("sp", lambda q: q.dma_start(out=lnb[:], in_=I["ln_b"][l:l + 1, :].to_broadcast([128, D])),
                      writes=[lnb.res])
                if do_sample:
                    sample_out(C, l, KT, w_out_sb, lng, lnb, is_last)
                for i in range(NTI if do_prompt else 0):
                    b = i % 2
                    S.dma("sp", lambda q: q.dma_start(
                        out=yt[b][:], in_=ycat.rearrange("(k p) t -> p k t", p=128)[:, 0:KT, i * 128:(i + 1) * 128]),
                        reads=[R_ycat[i]], writes=[yt[b].res])
                    S.dma("sp", lambda q: q.dma_start(out=xs[b][:], in_=x_src[i * 128:(i + 1) * 128, :]),
                          reads=[R_xtm[i]], writes=[xs[b].res])
                    for hf in range(2):
                        p = PF()
                        mm(p, p[:], [(yt[b][:, k, :], w_out_sb[:, k, hf * 512:(hf + 1) * 512]) for k in range(KT)],
                           [yt[b].res, w_out_sb.res])
                        V(lambda: nc.vector.scalar_tensor_tensor(out=hb[b][:, hf * 512:(hf + 1) * 512],
                                                                 in0=xs[b][:, hf * 512:(hf + 1) * 512], scalar=ALPHA,
                                                                 in1=p[:], op0=ALU.mult, op1=ALU.add),
                          r=[xs[b].res, p.res], w=[hb[b].res])
                    for hf in range(2):
                        V(lambda: nc.vector.bn_stats(out=st[b][:, hf, :], in_=hb[b][:, hf * 512:(hf + 1) * 512]),
                          r=[hb[b].res], w=[st[b].res])
                    V(lambda: nc.vector.bn_aggr(out=mv[b][:, 0:2], in_=st[b][:].rearrange("p a b -> p (a b)")),
                      r=[st[b].res], w=[mv[b].res])
                    V(lambda: nc.vector.tensor_scalar(out=mv[b][:, 2:3], in0=mv[b][:, 1:2], scalar1=EPS, scalar2=None,
                                                      op0=ALU.add), r=[mv[b].res], w=[mv[b].res])
                    A(lambda: nc.scalar.activation(out=mv[b][:, 2:3], in_=mv[b][:, 2:3], func=AF.Sqrt),
                      r=[mv[b].res], w=[mv[b].res])
                    V(lambda: nc.vector.reciprocal(out=mv[b][:, 3:4], in_=mv[b][:, 2:3]), r=[mv[b].res], w=[mv[b].res])
                    V(lambda: nc.vector.tensor_scalar(out=xn[b][:], in0=hb[b][:], scalar1=mv[b][:, 0:1],
                                                      scalar2=mv[b][:, 3:4], op0=ALU.subtract, op1=ALU.mult),
                      r=[hb[b].res, mv[b].res], w=[xn[b].res])
                    G(lambda: nc.gpsimd.tensor_tensor(out=xn[b][:], in0=xn[b][:], in1=lng[:], op=ALU.mult),
                      r=[lng.res], w=[xn[b].res])
                    G(lambda: nc.gpsimd.tensor_tensor(out=xn[b][:], in0=xn[b][:], in1=lnb[:], op=ALU.add),
                      r=[lnb.res], w=[xn[b].res])
                    S.dma("pool", lambda q: q.dma_start(out=dst[i * 128:(i + 1) * 128, :], in_=xn[b][:]),
                          reads=[xn[b].res], writes=[R_xtm[i]])
                    if not is_last:
                        A(lambda: nc.scalar.copy(out=xb[b][:], in_=xn[b][:]), r=[xn[b].res], w=[xb[b].res])
                        p = PB()
                        for k in range(8):
                            tr(p, p[:, k * 128:(k + 1) * 128], xb[b][:, k * 128:(k + 1) * 128], [xb[b].res],
                               first=(k == 0))
                        A(lambda: nc.scalar.copy(out=xt[b][:], in_=p[:].rearrange("p (k t) -> p k t", k=8)),
                          r=[p.res], w=[xt[b].res])
                        S.dma("pool", lambda q: q.dma_start(
                            out=xT.rearrange("(k p) t -> p k t", p=128)[:, :, i * 128:(i + 1) * 128], in_=xt[b][:]),
                            reads=[xt[b].res], writes=[R_xT[i]])
                S.barrier()

        C.sb = sb
        C.PF = PF
        C.PB = PB
        C.mm = mm
        C.tr = tr
        C.V, C.A, C.G = V, A, G
        C.I, C.O = I, O
        C.consts = dict(ones_f=ones_f, zeros_f=zeros_f, ones_bf=ones_bf, ident_f=ident_f, ident_bf=ident_bf, U_f=U_f,
                        negm=negm, Esel=Esel)
        if do_sample:
            C.consts.update(bmk=bmk, negpage=negpage, iota_f=iota_f)
        C.xT, C.xtm, C.ycat = xT, xtm, ycat
        C.R_xT, C.R_xtm, C.R_ycat = R_xT, R_xtm, R_ycat
        C.L, C.NTI, C.NMT = L, NTI, NMT
        C.load_weight_bf16 = load_weight_bf16

        if do_prompt:
            prep_xT(I["x_prompt"])
        if do_sample:
            S.dma("sp", lambda q: q.dma_start(out=C.xs_s[:], in_=I["x_sample"][:, :]), writes=[C.xs_s.res])
            sample_xT(C)
        x_src = I["x_prompt"]
        for l in range(layers if STAGE > 0 else 0):
            j = l // 2
            with contextlib.ExitStack() as esl:
                if l % 2 == 0:
                    with contextlib.ExitStack() as esa:
                        even_pass_a(C, esa, j)
                        S.barrier()
                else:
                    with contextlib.ExitStack() as esa:
                        odd_pass_a(C, esa, j)
                        S.barrier()
                    if do_prompt:
                        with contextlib.ExitStack() as esa:
                            odd_pass_b(C, esa, j)
                            S.barrier()
                if STAGE < 5:
                    continue
                with contextlib.ExitStack() as eso:
                    KT = 16 if l % 2 == 0 else 8
                    w_out_sb = sb(eso, "w_out_sb", [128, KT, D], BF16)
                    load_weight_bf16(eso, (I["w_out_even"] if l % 2 == 0 else I["w_out_odd"])[j], KT * 128, D,
                                     w_out_sb, "wo")
                    out_pass(l, KT, w_out_sb, x_src, is_last=(l == layers - 1))
                x_src = xtm
        S.final_wait("sp")
        print("instructions:", S.ninst)
    return nc


def even_pass_a(C, esl, j):
    nc, S, sb, PF, PB, mm, tr, V, A, G, I, O = C.nc, C.S, C.sb, C.PF, C.PB, C.mm, C.tr, C.V, C.A, C.G, C.I, C.O
    K = C.consts
    L, NTI, NMT = C.L, C.NTI, C.NMT
    xT, ycat = C.xT, C.ycat
    w_in = sb(esl, "w_in_e", [128, 8, EVEN_IN], BF16)
    C.load_weight_bf16(esl, I["w_in_even"][j], D, EVEN_IN, w_in, "wi")
    es = esl
    cw4 = sb(es, "cw4", [128, 12, 4])
    cb4 = sb(es, "cb4", [128, 12])
    cw31 = sb(es, "cw31", [128, 8, 31])
    cb31 = sb(es, "cb31", [128, 8])
    cfg = sb(es, "cfg", [128, 8])
    cfb = sb(es, "cfb", [128, 8])
    dtb = sb(es, "dtb", [128, 16])
    a_bc = sb(es, "a_bc", [128, 16])
    d_bc = sb(es, "d_bc", [128, 16])
    ng_bc = sb(es, "ng_bc", [128, 1024])
    ngT = sb(es, "ngT", [128, 8])
    hp16 = sb(es, "hp16", [16, 3])
    with contextlib.ExitStack() as esp:
        pr1 = sb(esp, "pr1", [8, 1536])
        pr2 = sb(esp, "pr2", [35, 1024])
        S.dma("sp", lambda q: q.dma_start(out=pr2[34:35, :], in_=I["ssd_norm_g"][j:j + 1, :]), writes=[pr2.res])
        S.dma("sp", lambda q: q.dma_start(out=hp16[:, 0:1], in_=I["ssd_dt_bias"][j:j + 1, :].rearrange("o h -> h o")),
              writes=[hp16.res])
        S.dma("sp", lambda q: q.dma_start(out=hp16[:, 1:2], in_=I["ssd_a_log"][j:j + 1, :].rearrange("o h -> h o")),
              writes=[hp16.res])
        S.dma("sp", lambda q: q.dma_start(out=hp16[:, 2:3], in_=I["ssd_d"][j:j + 1, :].rearrange("o h -> h o")),
              writes=[hp16.res])
        S.dma("sp", lambda q: q.dma_start(out=pr1[0:4, :], in_=I["ssd_conv_w"][j]), writes=[pr1.res])
        S.dma("sp", lambda q: q.dma_start(out=pr1[4:5, :], in_=I["ssd_conv_b"][j:j + 1, :]), writes=[pr1.res])
        S.dma("sp", lambda q: q.dma_start(out=pr2[0:31, :], in_=I["cf_dw_w"][j]), writes=[pr2.res])
        S.dma("sp", lambda q: q.dma_start(out=pr2[31:32, :], in_=I["cf_dw_b"][j:j + 1, :]), writes=[pr2.res])
        S.dma("sp", lambda q: q.dma_start(out=pr2[32:33, :], in_=I["cf_ln_g"][j:j + 1, :]), writes=[pr2.res])
        S.dma("sp", lambda q: q.dma_start(out=pr2[33:34, :], in_=I["cf_ln_b"][j:j + 1, :]), writes=[pr2.res])
        for c in range(12):
            p = PF()
            tr(p, p[:, 0:5], pr1[0:5, c * 128:(c + 1) * 128], [pr1.res], bf=False)
            V(lambda: nc.vector.tensor_copy(out=cw4[:, c, :], in_=p[:, 0:4]), r=[p.res], w=[cw4.res])
            V(lambda: nc.vector.tensor_copy(out=cb4[:, c:c + 1], in_=p[:, 4:5]), r=[p.res], w=[cb4.res])
        for c in range(8):
            p = PF()
            tr(p, p[:, 0:35], pr2[0:35, c * 128:(c + 1) * 128], [pr2.res], bf=False)
            V(lambda: nc.vector.tensor_copy(out=ngT[:, c:c + 1], in_=p[:, 34:35]), r=[p.res], w=[ngT.res])
            V(lambda: nc.vector.tensor_copy(out=cw31[:, c, :], in_=p[:, 0:31]), r=[p.res], w=[cw31.res])
            V(lambda: nc.vector.tensor_copy(out=cb31[:, c:c + 1], in_=p[:, 31:32]), r=[p.res], w=[cb31.res])
            V(lambda: nc.vector.tensor_copy(out=cfg[:, c:c + 1], in_=p[:, 32:33]), r=[p.res], w=[cfg.res])
            V(lambda: nc.vector.tensor_copy(out=cfb[:, c:c + 1], in_=p[:, 33:34]), r=[p.res], w=[cfb.res])
        S.barrier()
    S.dma("sp", lambda q: q.dma_start(out=dtb[:], in_=I["ssd_dt_bias"][j:j + 1, :].to_broadcast([128, 16])),
          writes=[dtb.res])
    S.dma("sp", lambda q: q.dma_start(out=a_bc[:], in_=I["ssd_a_log"][j:j + 1, :].to_broadcast([128, 16])),
          writes=[a_bc.res])
    S.dma("sp", lambda q: q.dma_start(out=d_bc[:], in_=I["ssd_d"][j:j + 1, :].to_broadcast([128, 16])),
          writes=[d_bc.res])
    S.dma("sp", lambda q: q.dma_start(out=ng_bc[:], in_=I["ssd_norm_g"][j:j + 1, :].to_broadcast([128, 1024])),
          writes=[ng_bc.res])
    A(lambda: nc.scalar.activation(out=a_bc[:], in_=a_bc[:], func=AF.Exp), r=[], w=[a_bc.res])
    V(lambda: nc.vector.tensor_scalar(out=a_bc[:], in0=a_bc[:], scalar1=-1.0, scalar2=None, op0=ALU.mult), r=[],
      w=[a_bc.res])

    if C.do_sample:
        sample_even(C, j, w_in, dict(cw4=cw4, cb4=cb4, cw31=cw31, cb31=cb31, cfg=cfg, cfb=cfb, ngT=ngT, hp16=hp16))
    if STAGE < 2 or not C.do_prompt:
        return
    ST = sb(es, "ST", [128, 1024])
    sraw = sb(es, "sraw", [128, 12, 3])
    uraw = sb(es, "uraw", [128, 8, 30])
    esw = contextlib.ExitStack()
    es = esw
    xt_mt = [sb(es, "xt_mt%d" % i, [128, 8, 512], BF16) for i in range(1)]
    xroll = sb(es, "xroll", [128, 12, 515], BF16)
    uroll = sb(es, "uroll", [128, 8, 542], BF16)
    xbcT = sb(es, "xbcT", [128, 12, 512], BF16)
    zb = [sb(es, "zb%d" % i, [128, 512], BF16) for i in range(2)]
    ybT = [sb(es, "ybT%d" % i, [128, 512], BF16) for i in range(2)]
    yaT = [sb(es, "yaT%d" % i, [128, 8, 128], BF16) for i in range(2)]
    acc = [sb(es, "acc%d" % i, [128, 512]) for i in range(2)]
    sg = [sb(es, "sg%d" % i, [128, 512]) for i in range(2)]
    cvq = [sb(es, "cvq%d" % i, [128, 512], BF16) for i in range(2)]
    cvf = sb(es, "cvf", [128, 8, 512], BF16)
    mean_bc = sb(es, "mean_bc", [128, 512])
    rstd_bc = sb(es, "rstd_bc", [128, 512])
    tmp512 = sg
    STb = sb(es, "STb", [128, 1024], BF16)
    za = sb(es, "za", [128, 1024], BF16)
    dt = sb(es, "dt", [128, 16])
    dta = sb(es, "dta", [128, 16])
    acum = sb(es, "acum", [128, 16])
    nacum = sb(es, "nacum", [128, 16])
    ea = sb(es, "ea", [128, 16])
    toend = sb(es, "toend", [128, 16])
    cdec = sb(es, "cdec", [128, 16])
    acT = sb(es, "acT", [16, 128])
    xs_tm = sb(es, "xs_tm", [128, 1024], BF16)
    X = sb(es, "X", [128, 1024], BF16)
    Xw = sb(es, "Xw", [128, 1024], BF16)
    B_tm = sb(es, "B_tm", [128, 256], BF16)
    cbT = sb(es, "cbT", [128, 2, 128])
    decT = [sb(es, "decT0", [128, 4, 128])] * 2
    MT = [sb(es, "MT%d" % i, [128, 4, 128], BF16) for i in range(2)]
    yv = sb(es, "yv", [128, 1024])
    ysq = mean_bc
    rms = sb(es, "rms", [128, 4])
    yab = sb(es, "yab", [128, 1024], BF16)

    V(lambda: nc.vector.memset(xroll[:], 0.0), w=[xroll.res])
    V(lambda: nc.vector.memset(uroll[:], 0.0), w=[uroll.res])
    V(lambda: nc.vector.memset(ST[:], 0.0), w=[ST.res])
    V(lambda: nc.vector.memset(STb[:], 0.0), w=[STb.res])

    xTv = xT.rearrange("(k p) t -> p k t", p=128)
    ycv = ycat.rearrange("(k p) t -> p k t", p=128)

    for m in range(NMT):
        last = (m == NMT - 1)
        xt = xt_mt[0]
        S.dma("sp", lambda q: q.dma_start(out=xt[:], in_=xTv[:, :, m * 512:(m + 1) * 512]),
              reads=[C.R_xT[4 * m + i] for i in range(4)], writes=[xt.res])

        def fproj(col0):
            p = PF()
            mm(p, p[:], [(w_in[:, k, col0:col0 + 128], xt[:, k, :]) for k in range(8)], [w_in.res, xt.res])
            return p

        if STAGE < 2.1:
            continue
        for c in range(12):
            p = fproj(1024 + c * 128)
            if "xcopy" not in SKIP:
                A(lambda: nc.scalar.copy(out=xroll[:, c, 3:515], in_=p[:]), r=[p.res], w=[xroll.res])
            if last and "sraw" not in SKIP:
                V(lambda: nc.vector.tensor_copy(out=sraw[:, c, :], in_=p[:, 509:512]), r=[p.res], w=[sraw.res])
        if STAGE < 2.2:
            continue
        for c in range(8):
            pv = fproj(2576 + c * 128)
            pg = fproj(3600 + c * 128)
            s_ = sg[c % 2]
            A(lambda: nc.scalar.activation(out=s_[:], in_=pg[:], func=AF.Sigmoid), r=[pg.res], w=[s_.res])
            V(lambda: nc.vector.tensor_tensor(out=uroll[:, c, 30:542], in0=pv[:], in1=s_[:], op=ALU.mult),
              r=[pv.res, s_.res], w=[uroll.res])
            if last:
                V(lambda: nc.vector.tensor_tensor(out=uraw[:, c, :], in0=pv[:, 482:512], in1=s_[:, 482:512],
                                                  op=ALU.mult), r=[pv.res, s_.res], w=[uraw.res])
        if STAGE < 2.3:
            continue
        for c in range(12):
            a_ = acc[c % 2]
            V(lambda: nc.vector.tensor_scalar(out=a_[:], in0=xroll[:, c, 0:512], scalar1=cw4[:, c, 0:1],
                                              scalar2=cb4[:, c:c + 1], op0=ALU.mult, op1=ALU.add),
              r=[xroll.res, cw4.res, cb4.res], w=[a_.res])
            for k in range(1, 4):
                V(lambda: nc.vector.scalar_tensor_tensor(out=a_[:], in0=xroll[:, c, k:k + 512], scalar=cw4[:, c, k:k + 1],
                                                         in1=a_[:], op0=ALU.mult, op1=ALU.add),
                  r=[xroll.res], w=[a_.res])
            A(lambda: nc.scalar.activation(out=xbcT[:, c, :], in_=a_[:], func=AF.Silu), r=[a_.res], w=[xbcT.res])
        G(lambda: nc.gpsimd.tensor_copy(out=xroll[:, :, 0:3], in_=xroll[:, :, 512:515]), r=[], w=[xroll.res])
        if STAGE < 2.4:
            continue
        p1 = PF(4)
        p2 = PF(5)

        def cf_taps():
            for c in range(8):
                a_ = acc[c % 2]
                V(lambda: nc.vector.tensor_scalar(out=a_[:], in0=uroll[:, c, 0:512], scalar1=cw31[:, c, 0:1],
                                                  scalar2=cb31[:, c:c + 1], op0=ALU.mult, op1=ALU.add),
                  r=[uroll.res, cw31.res, cb31.res], w=[a_.res])
                yield
                for k in range(1, 31):
                    V(lambda: nc.vector.scalar_tensor_tensor(out=a_[:] if k < 30 else cvf[:, c, :],
                                                             in0=uroll[:, c, k:k + 512], scalar=cw31[:, c, k:k + 1],
                                                             in1=a_[:], op0=ALU.mult, op1=ALU.add),
                      r=[uroll.res, a_.res], w=[a_.res] if k < 30 else [cvf.res])
                    yield

        taps = cf_taps()

        def pump(n_):
            for _ in range(n_):
                if next(taps, "done") == "done":
                    break

        if STAGE < 2.5:
            continue
        for cc in range(4 if STAGE > 3 else 0):
            cs = slice(cc * 128, (cc + 1) * 128)
            for hf in range(2):
                p = PF()
                mm(p, p[:], [(xt[:, k, cs], w_in[:, k, hf * 512:(hf + 1) * 512]) for k in range(8)], [w_in.res, xt.res])
                A(lambda: nc.scalar.activation(out=za[:, hf * 512:(hf + 1) * 512], in_=p[:], func=AF.Silu), r=[p.res],
                  w=[za.res])
            p = PF()
            mm(p, p[:, 0:16], [(xt[:, k, cs], w_in[:, k, 2560:2576]) for k in range(8)], [w_in.res, xt.res])
            V(lambda: nc.vector.tensor_tensor(out=dt[:], in0=p[:, 0:16], in1=dtb[:], op=ALU.add), r=[p.res, dtb.res],
              w=[dt.res])
            A(lambda: nc.scalar.activation(out=dt[:], in_=dt[:], func=AF.Exp), r=[], w=[dt.res])
            A(lambda: nc.scalar.activation(out=dt[:], in_=dt[:], func=AF.Ln, bias=1.0), r=[], w=[dt.res])
            V(lambda: nc.vector.tensor_tensor(out=dta[:], in0=dt[:], in1=a_bc[:], op=ALU.mult), r=[dt.res, a_bc.res],
              w=[dta.res])
            p = PF()
            mm(p, p[:, 0:16], [(K["U_f"][:], dta[:])], [K["U_f"].res, dta.res])
            mm(p, p[:, 16:32], [(K["ones_f"][:], dta[:])], [K["ones_f"].res, dta.res], first=False)
            V(lambda: nc.vector.tensor_copy(out=acum[:], in_=p[:, 0:16]), r=[p.res], w=[acum.res])
            V(lambda: nc.vector.tensor_scalar(out=nacum[:], in0=p[:, 0:16], scalar1=-1.0, scalar2=None, op0=ALU.mult),
              r=[p.res], w=[nacum.res])
            A(lambda: nc.scalar.activation(out=ea[:], in_=p[:, 0:16], func=AF.Exp), r=[p.res], w=[ea.res])
            A(lambda: nc.scalar.activation(out=cdec[:], in_=p[:, 16:32], func=AF.Exp), r=[p.res], w=[cdec.res])
            V(lambda: nc.vector.tensor_tensor(out=toend[:], in0=p[:, 16:32], in1=acum[:], op=ALU.subtract),
              r=[p.res, acum.res], w=[toend.res])
            A(lambda: nc.scalar.activation(out=toend[:], in_=toend[:], func=AF.Exp), r=[], w=[toend.res])
            p = PF()
            tr(p, p[0:16, 0:128], acum[:], [acum.res], bf=False)
            V(lambda: nc.vector.tensor_copy(out=acT[:], in_=p[0:16, 0:128]), r=[p.res], w=[acT.res])
            pbt = PB()
            for f in range(8):
                tr(pbt, pbt[:, f * 128:(f + 1) * 128], xbcT[:, f, cs], [xbcT.res], first=(f == 0))
            A(lambda: nc.scalar.copy(out=xs_tm[:], in_=pbt[:]), r=[pbt.res], w=[xs_tm.res])
            pbt = PB()
            for g in range(2):
                tr(pbt, pbt[:, g * 128:(g + 1) * 128], xbcT[:, 8 + g, cs], [xbcT.res], first=(g == 0))
            A(lambda: nc.scalar.copy(out=B_tm[:], in_=pbt[:, 0:256]), r=[pbt.res], w=[B_tm.res])
            V(lambda: nc.vector.tensor_tensor(out=X[:].rearrange("p (h d) -> p h d", d=64),
                                              in0=xs_tm[:].rearrange("p (h d) -> p h d", d=64),
                                              in1=dt[:].unsqueeze(2).to_broadcast([128, 16, 64]), op=ALU.mult),
              r=[xs_tm.res, dt.res], w=[X.res])
            V(lambda: nc.vector.tensor_tensor(out=Xw[:].rearrange("p (h d) -> p h d", d=64),
                                              in0=X[:].rearrange("p (h d) -> p h d", d=64),
                                              in1=toend[:].unsqueeze(2).to_broadcast([128, 16, 64]), op=ALU.mult),
              r=[X.res, toend.res], w=[Xw.res])
            p = PF()
            for g in range(2):
                mm(p, p[:, g * 128:(g + 1) * 128], [(xbcT[:, 8 + g, cs], xbcT[:, 10 + g, cs])], [xbcT.res],
                   first=(g == 0))
            V(lambda: nc.vector.tensor_copy(out=cbT[:], in_=p[:, 0:256].rearrange("p (g t) -> p g t", g=2)), r=[p.res],
              w=[cbT.res])
            pyo = [PF(0), PF(1)]
            for g in range(2):
                mm(pyo[g], pyo[g][:], [(xbcT[:, 10 + g, cs], STb[:, g * 512:(g + 1) * 512])], [xbcT.res, STb.res])
            pyd = [PF(2), PF(3)]
            for rd_ in range(4):
                g = rd_ // 2
                pd = PF(4 + rd_ % 2)
                for jh in range(4):
                    h = rd_ * 4 + jh
                    mm(pd, pd[:, jh * 128:(jh + 1) * 128], [(K["Esel"][:, h, :], acT[:]), (K["ident_f"][:], K["negm"][:])],
                       [K["Esel"].res, acT.res, K["ident_f"].res, K["negm"].res], first=(jh == 0))
                dT = decT[rd_ % 2]
                for jh in range(4):
                    h = rd_ * 4 + jh
                    A(lambda: nc.scalar.activation(out=dT[:, jh, :], in_=pd[:, jh * 128:(jh + 1) * 128], func=AF.Exp,
                                                   bias=nacum[:, h:h + 1]), r=[pd.res, nacum.res], w=[dT.res])
                mt = MT[rd_ % 2]
                V(lambda: nc.vector.tensor_tensor(out=mt[:], in0=dT[:],
                                                  in1=cbT[:, g, :].unsqueeze(1).to_broadcast([128, 4, 128]), op=ALU.mult),
                  r=[dT.res, cbT.res], w=[mt.res])
                pump(16)
                for jh in range(4):
                    h = rd_ * 4 + jh
                    hh = h % 8
                    mm(pyd[g], pyd[g][:, hh * 64:(hh + 1) * 64], [(mt[:, jh, :], X[:, h * 64:(h + 1) * 64])],
                       [mt.res, X.res], first=(hh == 0))
            for g in range(2):
                gs = slice(g * 512, (g + 1) * 512)
                V(lambda: nc.vector.tensor_tensor(out=yv[:, gs].rearrange("p (h d) -> p h d", d=64),
                                                  in0=pyo[g][:].rearrange("p (h d) -> p h d", d=64),
                                                  in1=ea[:, g * 8:(g + 1) * 8].unsqueeze(2).to_broadcast([128, 8, 64]),
                                                  op=ALU.mult), r=[pyo[g].res, ea.res], w=[yv.res])
                V(lambda: nc.vector.tensor_tensor(out=yv[:, gs], in0=yv[:, gs], in1=pyd[g][:], op=ALU.add),
                  r=[pyd[g].res], w=[yv.res])
                t_ = tmp512[g]
                G(lambda: nc.gpsimd.tensor_tensor(out=t_[:].rearrange("p (h d) -> p h d", d=64),
                                                  in0=xs_tm[:, gs].rearrange("p (h d) -> p h d", d=64),
                                                  in1=d_bc[:, g * 8:(g + 1) * 8].unsqueeze(2).to_broadcast([128, 8, 64]),
                                                  op=ALU.mult), r=[xs_tm.res, d_bc.res], w=[t_.res])
                V(lambda: nc.vector.tensor_tensor(out=yv[:, gs], in0=yv[:, gs], in1=t_[:], op=ALU.add), r=[t_.res],
                  w=[yv.res])
                V(lambda: nc.vector.tensor_tensor(out=yv[:, gs], in0=yv[:, gs], in1=za[:, gs], op=ALU.mult),
                  r=[za.res], w=[yv.res])
                A(lambda: nc.scalar.activation(out=ysq[:], in_=yv[:, gs], func=AF.Square, accum_out=rms[:, g:g + 1]),
                  r=[yv.res], w=[ysq.res, rms.res])
            V(lambda: nc.vector.tensor_scalar(out=rms[:, 0:2], in0=rms[:, 0:2], scalar1=1.0 / 512, scalar2=EPS,
                                              op0=ALU.mult, op1=ALU.add), r=[], w=[rms.res])
            A(lambda: nc.scalar.activation(out=rms[:, 0:2], in_=rms[:, 0:2], func=AF.Sqrt), r=[], w=[rms.res])
            V(lambda: nc.vector.reciprocal(out=rms[:, 2:4], in_=rms[:, 0:2]), r=[], w=[rms.res])
            for g in range(2):
                gs = slice(g * 512, (g + 1) * 512)
                V(lambda: nc.vector.scalar_tensor_tensor(out=yab[:, gs], in0=yv[:, gs], scalar=rms[:, 2 + g:3 + g],
                                                         in1=ng_bc[:, gs], op0=ALU.mult, op1=ALU.mult),
                  r=[yv.res, rms.res, ng_bc.res], w=[yab.res])
            pbt = PB()
            for f in range(8):
                tr(pbt, pbt[:, f * 128:(f + 1) * 128], yab[:, f * 128:(f + 1) * 128], [yab.res], first=(f == 0))
            ya_ = yaT[cc % 2]
            A(lambda: nc.scalar.copy(out=ya_[:], in_=pbt[:].rearrange("p (k t) -> p k t", k=8)), r=[pbt.res],
              w=[ya_.res])
            S.dma("pool", lambda q: q.dma_start(out=ycv[:, 0:8, m * 512 + cc * 128:m * 512 + (cc + 1) * 128], in_=ya_[:]),
                  reads=[ya_.res], writes=[C.R_ycat[4 * m + cc]])
            for g in range(2):
                gs = slice(g * 512, (g + 1) * 512)
                p = PF()
                mm(p, p[:], [(B_tm[:, g * 128:(g + 1) * 128], Xw[:, gs])], [B_tm.res, Xw.res])
                V(lambda: nc.vector.tensor_tensor(out=ST[:, gs].rearrange("p (h d) -> p h d", d=64),
                                                  in0=ST[:, gs].rearrange("p (h d) -> p h d", d=64),
                                                  in1=cdec[:, g * 8:(g + 1) * 8].unsqueeze(2).to_broadcast([128, 8, 64]),
                                                  op=ALU.mult), r=[cdec.res], w=[ST.res])
                V(lambda: nc.vector.tensor_tensor(out=ST[:, gs], in0=ST[:, gs], in1=p[:], op=ALU.add), r=[p.res],
                  w=[ST.res])
            A(lambda: nc.scalar.copy(out=STb[:], in_=ST[:]), r=[ST.res], w=[STb.res])
        pump(10 ** 6)
        for c in range(8):
            cq_ = cvq[c % 2]
            A(lambda: nc.scalar.activation(out=cq_[:], in_=cvf[:, c, :], func=AF.Square), r=[cvf.res],
              w=[cq_.res])
            mm(p1, p1[:], [(K["ones_bf"][:], cvf[:, c, :])], [cvf.res, K["ones_bf"].res], first=(c == 0),
               start=(c == 0), stop=(c == 7))
            mm(p2, p2[:], [(K["ones_bf"][:], cq_[:])], [cq_.res, K["ones_bf"].res], first=(c == 0),
               start=(c == 0), stop=(c == 7))
        G(lambda: nc.gpsimd.tensor_copy(out=uroll[:, :, 0:30], in_=uroll[:, :, 512:542]), r=[], w=[uroll.res])

        t0, t1 = tmp512
        A(lambda: nc.scalar.activation(out=mean_bc[:], in_=p1[:], func=AF.Copy, scale=1.0 / 1024), r=[p1.res], w=[mean_bc.res])
        V(lambda: nc.vector.tensor_tensor(out=t0[:], in0=mean_bc[:], in1=mean_bc[:], op=ALU.mult), r=[mean_bc.res],
          w=[t0.res])
        V(lambda: nc.vector.scalar_tensor_tensor(out=t1[:], in0=p2[:], scalar=1.0 / 1024, in1=t0[:], op0=ALU.mult,
                                                 op1=ALU.subtract), r=[p2.res, t0.res], w=[t1.res])
        V(lambda: nc.vector.tensor_scalar(out=t1[:], in0=t1[:], scalar1=EPS, scalar2=None, op0=ALU.add), r=[],
          w=[t1.res])
        A(lambda: nc.scalar.activation(out=t1[:], in_=t1[:], func=AF.Sqrt), r=[], w=[t1.res])
        V(lambda: nc.vector.reciprocal(out=rstd_bc[:], in_=t1[:]), r=[t1.res], w=[rstd_bc.res])
        for c in range(8 if STAGE >= 2.6 else 0):
            a_ = acc[c % 2]
            V(lambda: nc.vector.tensor_tensor(out=a_[:], in0=cvf[:, c, :], in1=mean_bc[:], op=ALU.subtract),
              r=[cvf.res, mean_bc.res], w=[a_.res])
            G(lambda: nc.gpsimd.tensor_tensor(out=a_[:], in0=a_[:], in1=rstd_bc[:], op=ALU.mult), r=[rstd_bc.res],
              w=[a_.res])
            A(lambda: nc.scalar.activation(out=a_[:], in_=a_[:], func=AF.Silu, bias=cfb[:, c:c + 1],
                                           scale=cfg[:, c:c + 1]), r=[cfg.res, cfb.res], w=[a_.res])
            pz = fproj(4624 + c * 128)
            zb_, yb_ = zb[c % 2], ybT[c % 2]
            A(lambda: nc.scalar.activation(out=zb_[:], in_=pz[:], func=AF.Silu), r=[pz.res], w=[zb_.res])
            G(lambda: nc.gpsimd.tensor_tensor(out=yb_[:], in0=a_[:], in1=zb_[:], op=ALU.mult),
              r=[a_.res, zb_.res], w=[yb_.res])
            S.dma("pool", lambda q: q.dma_start(out=ycv[:, 8 + c, m * 512:(m + 1) * 512], in_=yb_[:]),
                  reads=[yb_.res], writes=[])

    S.barrier()
    esw.close()
    es = esl
    if "outs" in SKIP:
        return
    sT_out = sb(es, "sT_out", [128, 8, 128])
    sc_o = sb(es, "sc_o", [3, 1536])
    cc_o = sb(es, "cc_o", [30, 1024])
    for f in range(8):
        p = PF()
        tr(p, p[:, 0:128], ST[:, f * 128:(f + 1) * 128], [ST.res], bf=False)
        V(lambda: nc.vector.tensor_copy(out=sT_out[:, f, :], in_=p[:, 0:128]), r=[p.res], w=[sT_out.res])
    S.dma("sp", lambda q: q.dma_start(out=O["ssm_p%d" % j].rearrange("(f p) n -> p f n", p=128), in_=sT_out[:]),
          reads=[sT_out.res])
    for c in range(12):
        p = PF()
        tr(p, p[0:3, 0:128], sraw[:, c, :], [sraw.res], bf=False)
        V(lambda: nc.vector.tensor_copy(out=sc_o[0:3, c * 128:(c + 1) * 128], in_=p[0:3, 0:128]), r=[p.res],
          w=[sc_o.res])
    for c in range(8):
        p = PF()
        tr(p, p[0:30, 0:128], uraw[:, c, :], [uraw.res], bf=False)
        V(lambda: nc.vector.tensor_copy(out=cc_o[0:30, c * 128:(c + 1) * 128], in_=p[0:30, 0:128]), r=[p.res],
          w=[cc_o.res])
    S.dma("sp", lambda q: q.dma_start(out=O["sc_p%d" % j], in_=sc_o[0:3, :]), reads=[sc_o.res])
    S.dma("sp", lambda q: q.dma_start(out=O["cc_p%d" % j], in_=cc_o[0:30, :]), reads=[cc_o.res])


def odd_pass_a(C, esl, j):
    nc, S, sb, PF, PB, mm, tr, V, A, G, I, O = C.nc, C.S, C.sb, C.PF, C.PB, C.mm, C.tr, C.V, C.A, C.G, C.I, C.O
    L, NTI, NMT = C.L, C.NTI, C.NMT
    es = esl
    w_in = sb(es, "w_in_o", [128, 8, ODD_IN], BF16)
    C.load_weight_bf16(esl, I["w_in_odd"][j], D, ODD_IN, w_in, "wio")
    if C.do_sample:
        sample_odd(C, j, w_in)
    if not C.do_prompt:
        return
    xt_mt = [sb(es, "oxt_mt%d" % i, [128, 8, 512], BF16) for i in range(2)]
    stg = [sb(es, "ostg%d" % i, [128, 512], BF16) for i in range(4)]
    kvf = [sb(es, "kvf%d" % i, [128, 512]) for i in range(2)]
    kif = [sb(es, "kif%d" % i, [128, 72]) for i in range(2)]
    v1 = [sb(es, "v1_%d" % i, [128, 4, 65], BF16) for i in range(2)]
    zs = [sb(es, "zs%d" % i, [128, 1024], BF16) for i in range(2)]
    for t in v1:
        V(lambda: nc.vector.memset(t[:], 1.0), w=[t.res])
    xTv = C.xT.rearrange("(k p) t -> p k t", p=128)
    WSC = (8.0 ** -0.5) * (64.0 ** -0.5)
    si = 0
    for m in range(NMT):
        xt = xt_mt[m % 2]
        ts = slice(m * 512, (m + 1) * 512)
        S.dma("sp", lambda q: q.dma_start(out=xt[:], in_=xTv[:, :, ts]),
              reads=[C.R_xT[4 * m + i] for i in range(4)], writes=[xt.res])

        def fproj(col0, M=128):
            p = PF()
            mm(p, p[0:M, :], [(w_in[:, k, col0:col0 + M], xt[:, k, :]) for k in range(8)], [w_in.res, xt.res])
            return p

        def evac(p, M=128):
            nonlocal si
            s_ = stg[si % 4]
            si += 1
            if si % 2 == 0:
                A(lambda: nc.scalar.copy(out=s_[0:M, :], in_=p[0:M, :]), r=[p.res], w=[s_.res])
            else:
                V(lambda: nc.vector.tensor_copy(out=s_[0:M, :], in_=p[0:M, :]), r=[p.res], w=[s_.res])
            return s_

        for c in range(8):
            s_ = evac(fproj(c * 128))
            for e in range(2):
                h = 2 * c + e
                S.dma("pool", lambda q: q.dma_start(out=C.qT_d[(h // 8) * 64:(h // 8) * 64 + 64, h % 8, ts],
                                                    in_=s_[e * 64:(e + 1) * 64, :]), reads=[s_.res], writes=[C.R_odd])
        for c in range(2):
            s_ = evac(fproj(1024 + c * 128))
            for e in range(2):
                S.dma("pool", lambda q: q.dma_start(out=C.kT_d[c * 64:(c + 1) * 64, e, ts], in_=s_[e * 64:(e + 1) * 64, :]),
                      reads=[s_.res], writes=[C.R_odd])
        for c in range(4):
            s_ = evac(fproj(1536 + c * 128))
            for e in range(2):
                S.dma("pool", lambda q: q.dma_start(out=C.qiT_d[:, 2 * c + e, ts], in_=s_[e * 64:(e + 1) * 64, :]),
                      reads=[s_.res], writes=[C.R_odd])
        s_ = evac(fproj(2048, 64), 64)
        S.dma("pool", lambda q: q.dma_start(out=C.kiT_d[:, ts], in_=s_[0:64, :]), reads=[s_.res], writes=[C.R_odd])
        for cc in range(4):
            cs = slice(cc * 128, (cc + 1) * 128)
            t0 = m * 512 + cc * 128
            rows = slice(t0, t0 + 128)
            b = cc % 2
            p = PF()
            mm(p, p[:], [(xt[:, k, cs], w_in[:, k, 1024:1536]) for k in range(8)], [w_in.res, xt.res])
            V(lambda: nc.vector.tensor_copy(out=kvf[b][:], in_=p[:]), r=[p.res], w=[kvf[b].res])
            A(lambda: nc.scalar.copy(out=v1[b][:, :, 0:64], in_=p[:, 256:512].rearrange("p (g d) -> p g d", d=64)),
              r=[p.res], w=[v1[b].res])
            S.dma("sp", lambda q: q.dma_start(out=O["k_p%d" % j][rows, :], in_=kvf[b][:, 0:256]), reads=[kvf[b].res])
            S.dma("sp", lambda q: q.dma_start(out=O["v_p%d" % j][rows, :], in_=kvf[b][:, 256:512]), reads=[kvf[b].res])
            S.dma("pool", lambda q: q.dma_start(out=C.V1_d[rows, :], in_=v1[b][:].rearrange("p g d -> p (g d)")),
                  reads=[v1[b].res], writes=[C.R_odd])
            p = PF()
            mm(p, p[:, 0:72], [(xt[:, k, cs], w_in[:, k, 2048:2120]) for k in range(8)], [w_in.res, xt.res])
            V(lambda: nc.vector.tensor_copy(out=kif[b][:, 0:64], in_=p[:, 0:64]), r=[p.res], w=[kif[b].res])
            V(lambda: nc.vector.tensor_scalar(out=kif[b][:, 64:72], in0=p[:, 64:72], scalar1=WSC, scalar2=None,
                                              op0=ALU.mult), r=[p.res], w=[kif[b].res])
            S.dma("sp", lambda q: q.dma_start(out=O["ki_p%d" % j][rows, :], in_=kif[b][:, 0:64]), reads=[kif[b].res])
            S.dma("pool", lambda q: q.dma_start(out=C.wi_d[rows, :], in_=kif[b][:, 64:72]), reads=[kif[b].res],
                  writes=[C.R_odd])
            for hf in range(2):
                p = PF()
                mm(p, p[:], [(xt[:, k, cs], w_in[:, k, 2120 + hf * 512:2120 + (hf + 1) * 512]) for k in range(8)],
                   [w_in.res, xt.res])
                A(lambda: nc.scalar.activation(out=zs[b][:, hf * 512:(hf + 1) * 512], in_=p[:], func=AF.Silu),
                  r=[p.res], w=[zs[b].res])
            S.dma("pool", lambda q: q.dma_start(out=C.zs_d[rows, :], in_=zs[b][:]), reads=[zs[b].res], writes=[C.R_odd])


def odd_pass_b(C, esl, j):
    nc, S, sb, PF, PB, mm, tr, V, A, G, I, O = C.nc, C.S, C.sb, C.PF, C.PB, C.mm, C.tr, C.V, C.A, C.G, C.I, C.O
    K = C.consts
    L, NTI, NMT = C.L, C.NTI, C.NMT
    es = esl
    TOPK = min(256, L // 4)
    NIT = 22
    kiT = sb(es, "kiT", [64, L], BF16)
    kT2 = sb(es, "kT2", [128, 2, L], BF16)
    V1 = sb(es, "V1", [128, NTI, 260], BF16)
    Isc = sb(es, "Isc", [128, L])
    msk = sb(es, "msk", [128, L], BF16)
    nmT = sb(es, "nmT", [128, NTI, 128], BF16)
    negc = sb(es, "negc", [128, 128])
    qT = [sb(es, "qT%d" % i, [128, 8, 128], BF16) for i in range(2)]
    qiT = [sb(es, "qiT%d" % i, [64, 8, 128], BF16) for i in range(2)]
    wi = [sb(es, "wi%d" % i, [128, 8]) for i in range(2)]
    zt = [sb(es, "zt%d" % i, [128, 1024], BF16) for i in range(2)]
    rl = [sb(es, "rl%d" % i, [128, 512]) for i in range(3)]
    E = [sb(es, "E%d" % i, [128, 4, 128], BF16) for i in range(3)]
    bs = sb(es, "bs", [128, 8])
    og = sb(es, "og", [128, 1024])
    ogb = sb(es, "ogb", [128, 1024], BF16)
    ogT = [sb(es, "ogT%d" % i, [128, 8, 128], BF16) for i in range(2)]
    rden = sb(es, "rden", [128, 4])
    G(lambda: nc.gpsimd.affine_select(out=negc[:], in_=K["zeros_f"][:], pattern=[[-1, 128]], compare_op=ALU.is_ge,
                                      fill=-1.0e30, base=0, channel_multiplier=1), r=[K["zeros_f"].res], w=[negc.res])
    S.dma("sp", lambda q: q.dma_start(out=kiT[:], in_=C.kiT_d[:, :]), reads=[C.R_odd], writes=[kiT.res])
    for h2 in range(2):
        S.dma("sp", lambda q: q.dma_start(out=kT2[:, h2, :], in_=C.kT_d[:, h2, :]), reads=[C.R_odd], writes=[kT2.res])
    S.dma("pool", lambda q: q.dma_start(out=V1[:], in_=C.V1_d.rearrange("(n p) c -> p n c", p=128)), reads=[C.R_odd],
          writes=[V1.res])
    ycv = C.ycat.rearrange("(k p) t -> p k t", p=128)
    def st_load(i):
        b = i % 2
        n = (i + 1) * 128
        ts = slice(i * 128, (i + 1) * 128)
        S.dma("sp", lambda q: q.dma_start(out=qT[b][:], in_=C.qT_d[:, :, ts]), reads=[C.R_odd], writes=[qT[b].res])
        S.dma("sp", lambda q: q.dma_start(out=qiT[b][:], in_=C.qiT_d[:, :, ts]), reads=[C.R_odd], writes=[qiT[b].res])
        S.dma("sp", lambda q: q.dma_start(out=wi[b][:], in_=C.wi_d[ts, :]), reads=[C.R_odd], writes=[wi[b].res])
        S.dma("sp", lambda q: q.dma_start(out=zt[b][:], in_=C.zs_d[ts, :]), reads=[C.R_odd], writes=[zt[b].res])

    def st_a1(i):
        b = i % 2
        n = (i + 1) * 128
        ts = slice(i * 128, (i + 1) * 128)
        nkb = (n + 511) // 512
        ri = 0
        for kb in range(nkb):
            w_ = min(512, n - kb * 512)
            ks = slice(kb * 512, kb * 512 + w_)
            for h in range(8):
                p = PF(h % 4)
                mm(p, p[:, 0:w_], [(qiT[b][:, h, :], kiT[:, ks])], [qiT[b].res, kiT.res])
                r_ = rl[ri % 3]
                ri += 1
                A(lambda: nc.scalar.activation(out=r_[:, 0:w_], in_=p[:, 0:w_], func=AF.Relu), r=[p.res], w=[r_.res])
                if h == 0:
                    V(lambda: nc.vector.tensor_scalar(out=Isc[:, ks], in0=r_[:, 0:w_], scalar1=wi[b][:, 0:1], scalar2=None,
                                                      op0=ALU.mult), r=[r_.res, wi[b].res], w=[Isc.res])
                else:
                    V(lambda: nc.vector.scalar_tensor_tensor(out=Isc[:, ks], in0=r_[:, 0:w_], scalar=wi[b][:, h:h + 1],
                                                             in1=Isc[:, ks], op0=ALU.mult, op1=ALU.add),
                      r=[r_.res, wi[b].res], w=[Isc.res])
        V(lambda: nc.vector.tensor_tensor(out=Isc[:, i * 128:n], in0=Isc[:, i * 128:n], in1=negc[:], op=ALU.add),
          r=[negc.res], w=[Isc.res])
        if n > TOPK:
            lo, hi, mid, cnt, ge, tmp = [bs[:, c:c + 1] for c in range(6)]
            V(lambda: nc.vector.tensor_reduce(out=lo, in_=Isc[:, 0:i * 128], axis=AX.X, op=ALU.min), r=[Isc.res],
              w=[bs.res])
            V(lambda: nc.vector.tensor_reduce(out=hi, in_=Isc[:, 0:n], axis=AX.X, op=ALU.max), r=[Isc.res], w=[bs.res])
            V(lambda: nc.vector.tensor_scalar(out=hi, in0=hi, scalar1=1.0, scalar2=None, op0=ALU.add), r=[], w=[bs.res])
            for it in range(NIT):
                V(lambda: nc.vector.tensor_tensor(out=mid, in0=lo, in1=hi, op=ALU.add), r=[], w=[bs.res])
                V(lambda: nc.vector.tensor_scalar(out=mid, in0=mid, scalar1=0.5, scalar2=None, op0=ALU.mult), r=[],
                  w=[bs.res])
                V(lambda: nc.vector.tensor_scalar(out=msk[:, 0:n], in0=Isc[:, 0:n], scalar1=mid, scalar2=0.0,
                                                  op0=ALU.is_ge, op1=ALU.add, accum_out=cnt), r=[Isc.res],
                  w=[bs.res, msk.res])
                V(lambda: nc.vector.tensor_scalar(out=ge, in0=cnt, scalar1=float(TOPK), scalar2=None, op0=ALU.is_ge),
                  r=[], w=[bs.res])
                V(lambda: nc.vector.tensor_tensor(out=tmp, in0=mid, in1=lo, op=ALU.subtract), r=[], w=[bs.res])
                V(lambda: nc.vector.scalar_tensor_tensor(out=lo, in0=tmp, scalar=ge, in1=lo, op0=ALU.mult, op1=ALU.add),
                  r=[], w=[bs.res])
                V(lambda: nc.vector.tensor_tensor(out=tmp, in0=hi, in1=mid, op=ALU.subtract), r=[], w=[bs.res])
                V(lambda: nc.vector.scalar_tensor_tensor(out=hi, in0=tmp, scalar=ge, in1=mid, op0=ALU.mult, op1=ALU.add),
                  r=[], w=[bs.res])
            V(lambda: nc.vector.tensor_scalar(out=msk[:, 0:n], in0=Isc[:, 0:n], scalar1=lo, scalar2=None, op0=ALU.is_ge),
              r=[Isc.res, bs.res], w=[msk.res])
        else:
            V(lambda: nc.vector.tensor_scalar(out=msk[:, 0:n], in0=Isc[:, 0:n], scalar1=-1.0e29, scalar2=None,
                                              op0=ALU.is_ge), r=[Isc.res], w=[msk.res])

    def st_a2(i):
        b = i % 2
        n = (i + 1) * 128
        ts = slice(i * 128, (i + 1) * 128)
        for b0 in range(0, i + 1, 8):
            nb = min(8, i + 1 - b0)
            pt = PB()
            for bb in range(nb):
                tr(pt, pt[:, bb * 128:(bb + 1) * 128], msk[:, (b0 + bb) * 128:(b0 + bb + 1) * 128], [msk.res],
                   first=(bb == 0))
            V(lambda: nc.vector.tensor_scalar(out=nmT[:, b0:b0 + nb, :],
                                              in0=pt[:, 0:nb * 128].rearrange("p (k t) -> p k t", t=128),
                                              scalar1=-1.0, scalar2=30000.0, op0=ALU.add, op1=ALU.mult), r=[pt.res],
              w=[nmT.res])

    def st_b(i):
        b = i % 2
        n = (i + 1) * 128
        ts = slice(i * 128, (i + 1) * 128)
        ei = 0
        for g in range(4):
            half = g // 2
            ps_ = slice(half * 64, half * 64 + 64)
            po = PF(2 + g)
            for sb_ in range(i + 1):
                ss = slice(sb_ * 128, (sb_ + 1) * 128)
                pl = PF(sb_ % 2)
                mm(pl, pl[:], [(kT2[ps_, g % 2, ss], qT[b][ps_, (g % 2) * 4:(g % 2) * 4 + 4, :]),
                               (K["ident_bf"][:], nmT[:, sb_, :].unsqueeze(1).to_broadcast([128, 4, 128]))],
                   [kT2.res, qT[b].res, K["ident_bf"].res, nmT.res])
                e_ = E[ei % 3]
                ei += 1
                A(lambda: nc.scalar.activation(out=e_[:], in_=pl[:].rearrange("p (r t) -> p r t", r=4), func=AF.Exp,
                                               scale=0.125), r=[pl.res], w=[e_.res])
                for r in range(4):
                    mm(po, po[:, r * 65:(r + 1) * 65], [(e_[:, r, :], V1[:, sb_, g * 65:(g + 1) * 65])],
                       [e_.res, V1.res], first=(sb_ == 0 and r == 0), start=(sb_ == 0 and r == 0), stop=(sb_ == i),
                       sgc=True)
            pov = po[:, 0:260].rearrange("p (r d) -> p r d", d=65)
            V(lambda: nc.vector.reciprocal(out=rden[:].unsqueeze(2), in_=pov[:, :, 64:65]), r=[po.res], w=[rden.res])
            V(lambda: nc.vector.tensor_tensor(out=og[:, g * 256:(g + 1) * 256].rearrange("p (r d) -> p r d", d=64),
                                              in0=pov[:, :, 0:64], in1=rden[:].unsqueeze(2).to_broadcast([128, 4, 64]),
                                              op=ALU.mult), r=[po.res, rden.res], w=[og.res])
        G(lambda: nc.gpsimd.tensor_tensor(out=ogb[:], in0=og[:], in1=zt[b][:], op=ALU.mult), r=[og.res, zt[b].res],
          w=[ogb.res])
        pt = PB()
        for f in range(8):
            tr(pt, pt[:, f * 128:(f + 1) * 128], ogb[:, f * 128:(f + 1) * 128], [ogb.res], first=(f == 0))
        A(lambda: nc.scalar.copy(out=ogT[b][:], in_=pt[:].rearrange("p (k t) -> p k t", k=8)), r=[pt.res], w=[ogT[b].res])
        S.dma("pool", lambda q: q.dma_start(out=ycv[:, 0:8, ts], in_=ogT[b][:]), reads=[ogT[b].res],
              writes=[C.R_ycat[i]])

    st_load(0)
    st_a1(0)
    st_a2(0)
    for i in range(NTI):
        if i + 1 < NTI:
            st_load(i + 1)
            st_a1(i + 1)
        st_b(i)
        if i + 1 < NTI:
            st_a2(i + 1)


def _unpack(C):
    return C.nc, C.S, C.sb, C.PF, C.PB, C.mm, C.tr, C.V, C.A, C.G, C.I, C.O


def sample_xT(C):
    nc, S, sb, PF, PB, mm, tr, V, A, G, I, O = _unpack(C)
    p = PF()
    for k in range(8):
        tr(p, p[:, k * 4:(k + 1) * 4], C.xs_s[0:4, k * 128:(k + 1) * 128], [C.xs_s.res], bf=False, first=(k == 0))
    A(lambda: nc.scalar.copy(out=C.xsT[:].rearrange("p k s -> p (k s)"), in_=p[:, 0:32]), r=[p.res], w=[C.xsT.res])


def sample_out(C, l, KT, w_out_sb, lng, lnb, is_last):
    nc, S, sb, PF, PB, mm, tr, V, A, G, I, O = _unpack(C)
    xs_s, ycT = C.xs_s, C.ycT
    with contextlib.ExitStack() as es:
        hb = sb(es, "so_hb", [4, D])
        st = sb(es, "so_st", [4, 2, 6])
        mv = sb(es, "so_mv", [4, 4])
        for hf in range(2):
            hs = slice(hf * 512, (hf + 1) * 512)
            p = PF()
            mm(p, p[0:4, :], [(ycT[:, k, :], w_out_sb[:, k, hs]) for k in range(KT)], [ycT.res, w_out_sb.res])
            V(lambda: nc.vector.scalar_tensor_tensor(out=hb[:, hs], in0=xs_s[:, hs], scalar=ALPHA, in1=p[0:4, :],
                                                     op0=ALU.mult, op1=ALU.add), r=[xs_s.res, p.res], w=[hb.res])
        for hf in range(2):
            V(lambda: nc.vector.bn_stats(out=st[:, hf, :], in_=hb[:, hf * 512:(hf + 1) * 512]), r=[hb.res], w=[st.res])
        V(lambda: nc.vector.bn_aggr(out=mv[:, 0:2], in_=st[:].rearrange("p a b -> p (a b)")), r=[st.res], w=[mv.res])
        V(lambda: nc.vector.tensor_scalar(out=mv[:, 2:3], in0=mv[:, 1:2], scalar1=EPS, scalar2=None, op0=ALU.add),
          r=[], w=[mv.res])
        A(lambda: nc.scalar.activation(out=mv[:, 2:3], in_=mv[:, 2:3], func=AF.Sqrt), r=[], w=[mv.res])
        V(lambda: nc.vector.reciprocal(out=mv[:, 3:4], in_=mv[:, 2:3]), r=[], w=[mv.res])
        V(lambda: nc.vector.tensor_scalar(out=xs_s[:], in0=hb[:], scalar1=mv[:, 0:1], scalar2=mv[:, 3:4],
                                          op0=ALU.subtract, op1=ALU.mult), r=[hb.res, mv.res], w=[xs_s.res])
        V(lambda: nc.vector.tensor_tensor(out=xs_s[:], in0=xs_s[:], in1=lng[0:4, :], op=ALU.mult), r=[lng.res],
          w=[xs_s.res])
        V(lambda: nc.vector.tensor_tensor(out=xs_s[:], in0=xs_s[:], in1=lnb[0:4, :], op=ALU.add), r=[lnb.res],
          w=[xs_s.res])
        if is_last:
            S.dma("sp", lambda q: q.dma_start(out=O["y_s"][:, :], in_=xs_s[:]), reads=[xs_s.res])
        else:
            sample_xT(C)
        S.barrier()


def sample_even(C, j, w_in, P):
    nc, S, sb, PF, PB, mm, tr, V, A, G, I, O = _unpack(C)
    K = C.consts
    xsT, ycT = C.xsT, C.ycT
    cw4, cb4, cw31, cb31, cfg, cfb, ngT, hp16 = (P[k] for k in ("cw4", "cb4", "cw31", "cb31", "cfg", "cfb", "ngT",
                                                                "hp16"))
    ones_f, ident_f, Esel = K["ones_f"], K["ident_f"], K["Esel"]
    with contextlib.ExitStack() as es:
        Esel2 = sb(es, "Esel2", [16, 8, 128])
        G(lambda: nc.gpsimd.memset(Esel2[:], 1.0), w=[Esel2.res])
        G(lambda: nc.gpsimd.affine_select(out=Esel2[:].rearrange("p f (e d) -> p f e d", e=2),
                                          in_=Esel2[:].rearrange("p f (e d) -> p f e d", e=2),
                                          pattern=[[-2, 8], [-1, 2], [0, 64]], compare_op=ALU.is_equal, fill=0.0,
                                          base=0, channel_multiplier=1), r=[], w=[Esel2.res])
        zaT = sb(es, "se_zaT", [128, 8, 4])
        zbT = sb(es, "se_zbT", [128, 8, 4])
        sgT = sb(es, "se_sgT", [128, 8, 4])
        xeT = sb(es, "se_xeT", [128, 12, 4, 4])
        ueT = sb(es, "se_ueT", [128, 8, 4, 31])
        dtv = sb(es, "se_dtv", [16, 9])
        ex = sb(es, "se_ex", [128, 8, 9])
        sst = sb(es, "se_sst", [12, 1536])
        cst = sb(es, "se_cst", [120, 1024])
        cvt = sb(es, "se_cvt", [128, 12, 4, 4])
        cv = sb(es, "se_cv", [128, 12, 4])
        xbcT = sb(es, "se_xbcT", [128, 12, 4])
        uct = sb(es, "se_uct", [128, 8, 4, 31])
        cu = sb(es, "se_cu", [128, 2, 8, 4])
        st2 = sb(es, "se_st2", [128, 2, 4])
        msq = sb(es, "se_msq", [128, 4])
        rstd = sb(es, "se_rstd", [128, 4])
        vn = sb(es, "se_vn", [128, 8, 4])
        bcT = sb(es, "se_bcT", [16, 128])
        BCb = sb(es, "se_BCb", [128, 16, 128])
        st = sb(es, "se_st", [128, 4, 8, 128])
        xd = sb(es, "se_xd", [128, 4, 8])
        coef = sb(es, "se_coef", [128, 4, 8])
        tmp = sb(es, "se_tmp", [128, 8, 128])
        yT = sb(es, "se_yT", [128, 4, 8])
        yv = sb(es, "se_yv", [128, 2, 8, 4])
        rs = sb(es, "se_rs", [128, 2, 4])
        rowx = sb(es, "se_rowx", [4, 1536])
        rowu = sb(es, "se_rowu", [4, 1024])

        S.dma("sp", lambda q: q.dma_start(out=sst[:], in_=I["state_ssdconv%d" % j].rearrange("s k c -> (s k) c")),
              writes=[sst.res])
        S.dma("sp", lambda q: q.dma_start(out=cst[:], in_=I["state_cfconv%d" % j].rearrange("s k c -> (s k) c")),
              writes=[cst.res])
        for s in range(4):
            S.dma("sp" if s % 2 == 0 else "pool", lambda q: q.dma_start(
                out=st[:, s], in_=I["state_ssm%d" % j][s].rearrange("(f q) n -> q f n", q=128)), writes=[st.res])
        S.dma("sp", lambda q: q.dma_start(out=O["sc_s%d" % j][:, 0:2, :], in_=I["state_ssdconv%d" % j][:, 1:3, :]))
        S.dma("sp", lambda q: q.dma_start(out=O["cc_s%d" % j][:, 0:29, :], in_=I["state_cfconv%d" % j][:, 1:30, :]))

        pj = PF()

        def fp(col0, M, off, first=False):
            mm(pj, pj[0:M, off:off + 4], [(w_in[:, k, col0:col0 + M], xsT[:, k, :]) for k in range(8)],
               [w_in.res, xsT.res], first=first)

        for f in range(8):
            fp(f * 128, 128, f * 4, first=(f == 0))
        for t in range(12):
            fp(1024 + t * 128, 128, 32 + t * 4)
        for f in range(8):
            fp(2576 + f * 128, 128, 80 + f * 4)
        for f in range(8):
            fp(3600 + f * 128, 128, 112 + f * 4)
        for f in range(8):
            fp(4624 + f * 128, 128, 144 + f * 4)
        fp(2560, 16, 176)
        A(lambda: nc.scalar.activation(out=zaT[:].rearrange("p f s -> p (f s)"), in_=pj[:, 0:32], func=AF.Silu),
          r=[pj.res], w=[zaT.res])
        V(lambda: nc.vector.tensor_copy(out=xeT[:, :, :, 3], in_=pj[:, 32:80].rearrange("p (t s) -> p t s", s=4)),
          r=[pj.res], w=[xeT.res])
        A(lambda: nc.scalar.activation(out=sgT[:].rearrange("p f s -> p (f s)"), in_=pj[:, 112:144], func=AF.Sigmoid),
          r=[pj.res], w=[sgT.res])
        V(lambda: nc.vector.tensor_tensor(out=ueT[:, :, :, 30], in0=pj[:, 80:112].rearrange("p (f s) -> p f s", s=4),
                                          in1=sgT[:], op=ALU.mult), r=[pj.res, sgT.res], w=[ueT.res])
        A(lambda: nc.scalar.activation(out=zbT[:].rearrange("p f s -> p (f s)"), in_=pj[:, 144:176], func=AF.Silu),
          r=[pj.res], w=[zbT.res])
        V(lambda: nc.vector.tensor_scalar(out=dtv[:, 0:4], in0=pj[0:16, 176:180], scalar1=hp16[:, 0:1], scalar2=None,
                                          op0=ALU.add), r=[pj.res, hp16.res], w=[dtv.res])
        A(lambda: nc.scalar.activation(out=dtv[:, 0:4], in_=dtv[:, 0:4], func=AF.Exp), r=[], w=[dtv.res])
        A(lambda: nc.scalar.activation(out=dtv[:, 0:4], in_=dtv[:, 0:4], func=AF.Ln, bias=1.0), r=[], w=[dtv.res])
        A(lambda: nc.scalar.activation(out=hp16[:, 1:2], in_=hp16[:, 1:2], func=AF.Exp), r=[], w=[hp16.res])
        V(lambda: nc.vector.tensor_scalar(out=hp16[:, 1:2], in0=hp16[:, 1:2], scalar1=-1.0, scalar2=None, op0=ALU.mult),
          r=[], w=[hp16.res])
        V(lambda: nc.vector.tensor_scalar(out=dtv[:, 4:8], in0=dtv[:, 0:4], scalar1=hp16[:, 1:2], scalar2=None,
                                          op0=ALU.mult), r=[hp16.res], w=[dtv.res])
        A(lambda: nc.scalar.activation(out=dtv[:, 4:8], in_=dtv[:, 4:8], func=AF.Exp), r=[], w=[dtv.res])
        V(lambda: nc.vector.tensor_copy(out=dtv[:, 8:9], in_=hp16[:, 2:3]), r=[hp16.res], w=[dtv.res])
        pe_ = PF()
        for f in range(8):
            mm(pe_, pe_[:, f * 9:(f + 1) * 9], [(Esel2[:, f, :], dtv[:, 0:9])], [Esel2.res, dtv.res], first=(f == 0))
        V(lambda: nc.vector.tensor_copy(out=ex[:].rearrange("p f c -> p (f c)"), in_=pe_[:, 0:72]), r=[pe_.res],
          w=[ex.res])

        p1 = PF()
        for t in range(12):
            tr(p1, p1[:, t * 12:(t + 1) * 12], sst[0:12, t * 128:(t + 1) * 128], [sst.res], bf=False, first=(t == 0))
        V(lambda: nc.vector.tensor_copy(out=xeT[:, :, :, 0:3],
                                        in_=p1[:, 0:144].rearrange("p (t s k) -> p t s k", t=12, s=4, k=3)),
          r=[p1.res], w=[xeT.res])
        for half in range(2):
            p2 = PF()
            for ff in range(4):
                f = half * 4 + ff
                tr(p2, p2[:, ff * 120:(ff + 1) * 120], cst[0:120, f * 128:(f + 1) * 128], [cst.res], bf=False,
                   first=(ff == 0))
            V(lambda: nc.vector.tensor_copy(out=ueT[:, half * 4:half * 4 + 4, :, 0:30],
                                            in_=p2[:, 0:480].rearrange("p (f s k) -> p f s k", f=4, s=4, k=30)),
              r=[p2.res], w=[ueT.res])

        for b3 in range(3):
            p = PF()
            for tt in range(4):
                t = b3 * 4 + tt
                tr(p, p[0:4, tt * 128:(tt + 1) * 128], xeT[:, t, :, 3], [xeT.res], bf=False, first=(tt == 0))
            A(lambda: nc.scalar.copy(out=rowx[:, b3 * 512:(b3 + 1) * 512], in_=p[0:4, :]), r=[p.res], w=[rowx.res])
        for b2 in range(2):
            p = PF()
            for ff in range(4):
                f = b2 * 4 + ff
                tr(p, p[0:4, ff * 128:(ff + 1) * 128], ueT[:, f, :, 30], [ueT.res], bf=False, first=(ff == 0))
            A(lambda: nc.scalar.copy(out=rowu[:, b2 * 512:(b2 + 1) * 512], in_=p[0:4, :]), r=[p.res], w=[rowu.res])
        S.dma("sp", lambda q: q.dma_start(out=O["sc_s%d" % j][:, 2, :], in_=rowx[:]), reads=[rowx.res])
        S.dma("sp", lambda q: q.dma_start(out=O["cc_s%d" % j][:, 29, :], in_=rowu[:]), reads=[rowu.res])

        V(lambda: nc.vector.tensor_tensor(out=cvt[:], in0=xeT[:], in1=cw4[:].unsqueeze(2).to_broadcast([128, 12, 4, 4]),
                                          op=ALU.mult), r=[xeT.res, cw4.res], w=[cvt.res])
        V(lambda: nc.vector.tensor_reduce(out=cv[:], in_=cvt[:], axis=AX.X, op=ALU.add), r=[cvt.res], w=[cv.res])
        V(lambda: nc.vector.tensor_tensor(out=cv[:], in0=cv[:], in1=cb4[:].unsqueeze(2).to_broadcast([128, 12, 4]),
                                          op=ALU.add), r=[cb4.res], w=[cv.res])
        A(lambda: nc.scalar.activation(out=xbcT[:], in_=cv[:], func=AF.Silu), r=[cv.res], w=[xbcT.res])

        V(lambda: nc.vector.tensor_tensor(out=uct[:], in0=ueT[:], in1=cw31[:].unsqueeze(2).to_broadcast([128, 8, 4, 31]),
                                          op=ALU.mult), r=[ueT.res, cw31.res], w=[uct.res])
        V(lambda: nc.vector.tensor_reduce(out=cu[:, 0], in_=uct[:], axis=AX.X, op=ALU.add), r=[uct.res], w=[cu.res])
        V(lambda: nc.vector.tensor_tensor(out=cu[:, 0], in0=cu[:, 0], in1=cb31[:].unsqueeze(2).to_broadcast([128, 8, 4]),
                                          op=ALU.add), r=[cb31.res], w=[cu.res])
        V(lambda: nc.vector.tensor_tensor(out=cu[:, 1], in0=cu[:, 0], in1=cu[:, 0], op=ALU.mult), r=[], w=[cu.res])
        pl = PF()
        mm(pl, pl[:, 0:64], [(ones_f[:], cu[:].rearrange("p a f s -> p (a f s)"))], [ones_f.res, cu.res])
        V(lambda: nc.vector.tensor_reduce(out=st2[:], in_=pl[:, 0:64].rearrange("p (a f s) -> p a s f", a=2, f=8, s=4),
                                          axis=AX.X, op=ALU.add), r=[pl.res], w=[st2.res])
        V(lambda: nc.vector.tensor_scalar(out=st2[:], in0=st2[:], scalar1=1.0 / 1024, scalar2=None, op0=ALU.mult),
          r=[], w=[st2.res])
        V(lambda: nc.vector.tensor_tensor(out=msq[:], in0=st2[:, 0], in1=st2[:, 0], op=ALU.mult), r=[st2.res],
          w=[msq.res])
        V(lambda: nc.vector.tensor_tensor(out=msq[:], in0=st2[:, 1], in1=msq[:], op=ALU.subtract), r=[st2.res],
          w=[msq.res])
        V(lambda: nc.vector.tensor_scalar(out=msq[:], in0=msq[:], scalar1=EPS, scalar2=None, op0=ALU.add), r=[],
          w=[msq.res])
        A(lambda: nc.scalar.activation(out=msq[:], in_=msq[:], func=AF.Sqrt), r=[], w=[msq.res])
        V(lambda: nc.vector.reciprocal(out=rstd[:], in_=msq[:]), r=[msq.res], w=[rstd.res])
        V(lambda: nc.vector.tensor_tensor(out=vn[:], in0=cu[:, 0], in1=st2[:, 0].unsqueeze(1).to_broadcast([128, 8, 4]),
                                          op=ALU.subtract), r=[cu.res, st2.res], w=[vn.res])
        V(lambda: nc.vector.tensor_tensor(out=vn[:], in0=vn[:], in1=rstd[:].unsqueeze(1).to_broadcast([128, 8, 4]),
                                          op=ALU.mult), r=[rstd.res], w=[vn.res])
        V(lambda: nc.vector.tensor_tensor(out=vn[:], in0=vn[:], in1=cfg[:].unsqueeze(2).to_broadcast([128, 8, 4]),
                                          op=ALU.mult), r=[cfg.res], w=[vn.res])
        V(lambda: nc.vector.tensor_tensor(out=vn[:], in0=vn[:], in1=cfb[:].unsqueeze(2).to_broadcast([128, 8, 4]),
                                          op=ALU.add), r=[cfb.res], w=[vn.res])
        A(lambda: nc.scalar.activation(out=vn[:], in_=vn[:], func=AF.Silu), r=[], w=[vn.res])
        V(lambda: nc.vector.tensor_tensor(out=ycT[:, 8:16, :], in0=vn[:], in1=zbT[:], op=ALU.mult), r=[vn.res, zbT.res],
          w=[ycT.res])

        pT = PF()
        tr(pT, pT[0:16, 0:128], xbcT[:, 8:12, :].rearrange("p t s -> p (t s)"), [xbcT.res], bf=False)
        V(lambda: nc.vector.tensor_copy(out=bcT[:], in_=pT[0:16, 0:128]), r=[pT.res], w=[bcT.res])
        for i4 in range(4):
            pb_ = PF()
            for ii in range(4):
                i = i4 * 4 + ii
                mm(pb_, pb_[:, ii * 128:(ii + 1) * 128], [(Esel[0:16, i, :], bcT[:])], [Esel.res, bcT.res],
                   first=(ii == 0))
            if i4 % 2 == 0:
                A(lambda: nc.scalar.copy(out=BCb[:, i4 * 4:(i4 + 1) * 4, :].rearrange("p a n -> p (a n)"), in_=pb_[:]),
                  r=[pb_.res], w=[BCb.res])
            else:
                V(lambda: nc.vector.tensor_copy(out=BCb[:, i4 * 4:(i4 + 1) * 4, :].rearrange("p a n -> p (a n)"),
                                                in_=pb_[:]), r=[pb_.res], w=[BCb.res])

        V(lambda: nc.vector.tensor_tensor(out=xd[:].rearrange("p s f -> p f s"), in0=xbcT[:, 0:8, :], in1=ex[:, :, 0:4],
                                          op=ALU.mult), r=[xbcT.res, ex.res], w=[xd.res])
        V(lambda: nc.vector.tensor_copy(out=coef[:].rearrange("p s f -> p f s"), in_=ex[:, :, 4:8]), r=[ex.res],
          w=[coef.res])
        for s in range(4):
            V(lambda: nc.vector.tensor_tensor(out=st[:, s], in0=st[:, s],
                                              in1=coef[:, s, :].unsqueeze(2).to_broadcast([128, 8, 128]), op=ALU.mult),
              r=[coef.res], w=[st.res])
            for g in range(2):
                fs = slice(4 * g, 4 * g + 4)
                V(lambda: nc.vector.tensor_tensor(out=tmp[:, fs, :],
                                                  in0=BCb[:, g * 4 + s, :].unsqueeze(1).to_broadcast([128, 4, 128]),
                                                  in1=xd[:, s, fs].unsqueeze(2).to_broadcast([128, 4, 128]),
                                                  op=ALU.mult), r=[BCb.res, xd.res], w=[tmp.res])
            V(lambda: nc.vector.tensor_tensor(out=st[:, s], in0=st[:, s], in1=tmp[:], op=ALU.add), r=[tmp.res],
              w=[st.res])
            for g in range(2):
                fs = slice(4 * g, 4 * g + 4)
                V(lambda: nc.vector.tensor_tensor(out=tmp[:, fs, :], in0=st[:, s, fs, :],
                                                  in1=BCb[:, (2 + g) * 4 + s, :].unsqueeze(1).to_broadcast([128, 4, 128]),
                                                  op=ALU.mult), r=[BCb.res, st.res], w=[tmp.res])
            V(lambda: nc.vector.tensor_reduce(out=yT[:, s, :], in_=tmp[:], axis=AX.X, op=ALU.add), r=[tmp.res],
              w=[yT.res])
            S.dma("sp" if s % 2 == 0 else "pool", lambda q: q.dma_start(
                out=O["ssm_s%d" % j][s].rearrange("(f q) n -> q f n", q=128), in_=st[:, s]), reads=[st.res])

        V(lambda: nc.vector.tensor_tensor(out=yv[:, 0], in0=xbcT[:, 0:8, :], in1=ex[:, :, 8:9].to_broadcast([128, 8, 4]),
                                          op=ALU.mult), r=[xbcT.res, ex.res], w=[yv.res])
        V(lambda: nc.vector.tensor_tensor(out=yv[:, 0], in0=yv[:, 0], in1=yT[:].rearrange("p s f -> p f s"), op=ALU.add),
          r=[yT.res], w=[yv.res])
        V(lambda: nc.vector.tensor_tensor(out=yv[:, 0], in0=yv[:, 0], in1=zaT[:], op=ALU.mult), r=[zaT.res], w=[yv.res])
        V(lambda: nc.vector.tensor_tensor(out=yv[:, 1], in0=yv[:, 0], in1=yv[:, 0], op=ALU.mult), r=[], w=[yv.res])
        pl = PF()
        mm(pl, pl[:, 0:32], [(ones_f[:], yv[:, 1].rearrange("p f s -> p (f s)"))], [ones_f.res, yv.res])
        V(lambda: nc.vector.tensor_reduce(out=rs[:], in_=pl[:, 0:32].rearrange("p (g f s) -> p g s f", g=2, f=4, s=4),
                                          axis=AX.X, op=ALU.add), r=[pl.res], w=[rs.res])
        V(lambda: nc.vector.tensor_scalar(out=rs[:], in0=rs[:], scalar1=1.0 / 512, scalar2=EPS, op0=ALU.mult,
                                          op1=ALU.add), r=[], w=[rs.res])
        A(lambda: nc.scalar.activation(out=rs[:], in_=rs[:], func=AF.Sqrt), r=[], w=[rs.res])
        V(lambda: nc.vector.reciprocal(out=rs[:], in_=rs[:]), r=[], w=[rs.res])
        for g in range(2):
            fs = slice(4 * g, 4 * g + 4)
            V(lambda: nc.vector.tensor_tensor(out=yv[:, 0, fs, :], in0=yv[:, 0, fs, :],
                                              in1=rs[:, g, :].unsqueeze(1).to_broadcast([128, 4, 4]), op=ALU.mult),
              r=[rs.res], w=[yv.res])
        V(lambda: nc.vector.tensor_tensor(out=ycT[:, 0:8, :], in0=yv[:, 0], in1=ngT[:].unsqueeze(2).to_broadcast([128, 8, 4]),
                                          op=ALU.mult), r=[yv.res, ngT.res], w=[ycT.res])
        S.barrier()


def sample_odd(C, j, w_in):
    nc, S, sb, PF, PB, mm, tr, V, A, G, I, O = _unpack(C)
    K = C.consts
    xsT, ycT = C.xsT, C.ycT
    ones_f, ones_bf, ident_f, Esel, bmk = K["ones_f"], K["ones_bf"], K["ident_f"], K["Esel"], K["bmk"]
    WSC = (8.0 ** -0.5) * (64.0 ** -0.5)
    NPG = 65
    NIT = 24
    jc = 0 if os.environ.get("SCACHE0") else j
    ck, cvv, cki = I["cache_k%d" % jc], I["cache_v%d" % jc], I["cache_ki%d" % jc]
    with contextlib.ExitStack() as es:
        sel0 = sb(es, "sel0", [4, 4, 128])
        G(lambda: nc.gpsimd.affine_select(out=sel0[:], in_=Esel[0:4, 0:4, :], pattern=[[0, 4], [1, 128]],
                                          compare_op=ALU.is_equal, fill=0.0, base=0, channel_multiplier=0),
          r=[Esel.res], w=[sel0.res])
        pt_tm = sb(es, "sd_pt", [4, 2120])
        zT = sb(es, "sd_zT", [128, 8, 4])
        qb = sb(es, "sd_qb", [128, 4, 1024])
        qib = sb(es, "sd_qib", [128, 4, 512])
        wib = sb(es, "sd_wib", [128, 4, 8])
        kin = sb(es, "sd_kin", [128, 4, 64])
        kvn = sb(es, "sd_kvn", [128, 4, 512])
        ptb = sb(es, "sd_ptb", [128, 256], I32)
        ptf = sb(es, "sd_ptf", [128, 256])
        pidx = sb(es, "sd_pidx", [128, 256], I32)
        Isc = sb(es, "sd_Isc", [128, 4, NPG])
        msk = sb(es, "sd_msk", [128, 4, NPG])
        cmpb = sb(es, "sd_cmpb", [128, 4, NPG])
        cntp = sb(es, "sd_cntp", [128, 4])
        sc_all = sb(es, "sd_scall", [128, NPG, 8])
        kip = [sb(es, "sd_kip%d" % i, [128, 64]) for i in range(4)]
        prod = [sb(es, "sd_prod%d" % i, [128, 8, 64]) for i in range(2)]
        kpg = [sb(es, "sd_kpg%d" % i, [128, 256]) for i in range(4)]
        vpg = [sb(es, "sd_vpg%d" % i, [128, 260]) for i in range(4)]
        vnw = sb(es, "sd_vnw", [128, 260])
        prod2 = [sb(es, "sd_prod2_%d" % i, [128, 16, 64]) for i in range(2)]
        lg = sb(es, "sd_lg", [128, NPG, 16])
        PT = sb(es, "sd_PT", [128, NPG, 16])
        pm = sb(es, "sd_pm", [128, 8])
        gm = sb(es, "sd_gm", [8, 1])
        dg = sb(es, "sd_dg", [8, 8])
        lo = sb(es, "sd_lo", [128, 4])
        hi = sb(es, "sd_hi", [128, 4])
        mid = sb(es, "sd_mid", [128, 4])
        ge = sb(es, "sd_ge", [128, 4])
        t1 = sb(es, "sd_t1", [128, 4])
        t2 = sb(es, "sd_t2", [128, 4])
        om = sb(es, "sd_om", [16, 4, 64])
        o1 = sb(es, "sd_o1", [16, 64])
        rd = sb(es, "sd_rd", [16, 1])
        od = sb(es, "sd_od", [16, 4, 128])

        S.dma("sp", lambda q: q.dma_start(out=ptb[:], in_=I["page_table"][0:1, :].to_broadcast([128, 256])),
              writes=[ptb.res])
        for t in vpg + [vnw]:
            V(lambda: nc.vector.memset(t[:, 256:260], 1.0), w=[t.res])

        for c in range(5):
            c0 = c * 512
            w_ = min(512, 2120 - c0)
            p = PF()
            mm(p, p[0:4, 0:w_], [(xsT[:, k, :], w_in[:, k, c0:c0 + w_]) for k in range(8)], [w_in.res, xsT.res])
            if c % 2 == 0:
                A(lambda: nc.scalar.copy(out=pt_tm[:, c0:c0 + w_], in_=p[0:4, 0:w_]), r=[p.res], w=[pt_tm.res])
            else:
                V(lambda: nc.vector.tensor_copy(out=pt_tm[:, c0:c0 + w_], in_=p[0:4, 0:w_]), r=[p.res], w=[pt_tm.res])
        S.dma("sp", lambda q: q.dma_start(out=O["k_s%d" % j][:, :], in_=pt_tm[:, 1024:1280]), reads=[pt_tm.res])
        S.dma("sp", lambda q: q.dma_start(out=O["v_s%d" % j][:, :], in_=pt_tm[:, 1280:1536]), reads=[pt_tm.res])
        S.dma("sp", lambda q: q.dma_start(out=O["ki_s%d" % j][:, :], in_=pt_tm[:, 2048:2112]), reads=[pt_tm.res])
        V(lambda: nc.vector.tensor_scalar(out=pt_tm[:, 2112:2120], in0=pt_tm[:, 2112:2120], scalar1=WSC, scalar2=None,
                                          op0=ALU.mult), r=[], w=[pt_tm.res])
        pz = PF()
        for f in range(8):
            mm(pz, pz[:, f * 4:(f + 1) * 4], [(w_in[:, k, 2120 + f * 128:2120 + (f + 1) * 128], xsT[:, k, :])
                                               for k in range(8)], [w_in.res, xsT.res], first=(f == 0))
        A(lambda: nc.scalar.activation(out=zT[:].rearrange("p f s -> p (f s)"), in_=pz[:, 0:32], func=AF.Silu),
          r=[pz.res], w=[zT.res])
        ci = 0

        def bc(lhs, c0, w_, dst):
            nonlocal ci
            p = PF()
            mm(p, p[:, 0:w_], [(lhs, pt_tm[0:4, c0:c0 + w_])], [Esel.res, sel0.res, pt_tm.res])
            ci += 1
            if ci % 2 == 0:
                A(lambda: nc.scalar.copy(out=dst, in_=p[:, 0:w_]), r=[p.res], w=[qb.res])
            else:
                V(lambda: nc.vector.tensor_copy(out=dst, in_=p[:, 0:w_]), r=[p.res], w=[qb.res])

        for s in range(4):
            bc(Esel[0:4, s, :], 0, 512, qb[:, s, 0:512])
            bc(Esel[0:4, s, :], 512, 512, qb[:, s, 512:1024])
            bc(Esel[0:4, s, :], 1536, 512, qib[:, s, :])
            bc(Esel[0:4, s, :], 2112, 8, wib[:, s, :])
            bc(sel0[0:4, s, :], 2048, 64, kin[:, s, :])
            bc(sel0[0:4, s, :], 1024, 512, kvn[:, s, :])
        V(lambda: nc.vector.tensor_copy(out=ptf[:], in_=ptb[:]), r=[ptb.res], w=[ptf.res])
        V(lambda: nc.vector.tensor_scalar(out=pidx[:], in0=ptf[:], scalar1=128.0, scalar2=K["iota_f"][:, 0:1],
                                          op0=ALU.mult, op1=ALU.add), r=[ptf.res, K["iota_f"].res], w=[pidx.res])

        if SODD < 99:
            S.dma("sp", lambda q: q.dma_start(out=O["dbg_i"][:, :], in_=pidx[:]), reads=[pidx.res])
        if SODD < 1:
            S.barrier()
            return

        def gather(dst_tile, dst_ap, cache, col):
            S.dma("pool", lambda q: q.indirect_dma_start(
                out=dst_ap, out_offset=None, in_=cache[:, :],
                in_offset=bass.IndirectOffsetOnAxis(ap=pidx[:, col:col + 1], axis=0)),
                reads=[pidx.res], writes=[dst_tile.res])

        for s in range(4):
            for pg in range(NPG):
                if pg < 64:
                    kb = kip[pg % 4]
                    gather(kb, kb[:], cki, s * 64 + pg)
                    src, srcr = kb[:], [kb.res]
                else:
                    src, srcr = kin[:, s, :], [qb.res]
                pr = prod[pg % 2]
                V(lambda: nc.vector.tensor_tensor(out=pr[:], in0=qib[:, s, :].rearrange("p (h d) -> p h d", d=64),
                                                  in1=src.unsqueeze(1).to_broadcast([128, 8, 64]), op=ALU.mult),
                  r=[qb.res] + srcr, w=[pr.res])
                V(lambda: nc.vector.tensor_reduce(out=sc_all[:, pg, :], in_=pr[:], axis=AX.X, op=ALU.add), r=[pr.res],
                  w=[sc_all.res])
            V(lambda: nc.vector.scalar_tensor_tensor(out=sc_all[:], in0=sc_all[:], scalar=0.0,
                                                     in1=wib[:, s, :].unsqueeze(1).to_broadcast([128, NPG, 8]),
                                                     op0=ALU.max, op1=ALU.mult), r=[qb.res], w=[sc_all.res])
            V(lambda: nc.vector.tensor_reduce(out=Isc[:, s, :], in_=sc_all[:], axis=AX.X, op=ALU.add), r=[sc_all.res],
              w=[Isc.res])
        V(lambda: nc.vector.tensor_tensor(out=Isc[:, :, 64], in0=Isc[:, :, 64],
                                          in1=K["negpage"][:, 0:1].to_broadcast([128, 4]), op=ALU.add),
          r=[K["negpage"].res], w=[Isc.res])

        if SODD < 99:
            S.dma("sp", lambda q: q.dma_start(out=O["dbg_f"][:, :], in_=Isc[:].rearrange("p s g -> p (s g)")),
                  reads=[Isc.res])
        if SODD < 2:
            S.barrier()
            return
        V(lambda: nc.vector.tensor_reduce(out=pm[:, 0:4], in_=Isc[:], axis=AX.X, op=ALU.max), r=[Isc.res], w=[pm.res])
        V(lambda: nc.vector.tensor_reduce(out=pm[:, 4:8], in_=Isc[:, :, 0:64], axis=AX.X, op=ALU.min), r=[Isc.res],
          w=[pm.res])
        V(lambda: nc.vector.tensor_scalar(out=pm[:, 4:8], in0=pm[:, 4:8], scalar1=-1.0, scalar2=None, op0=ALU.mult),
          r=[], w=[pm.res])
        pT = PF()
        tr(pT, pT[0:8, 0:128], pm[:, 0:8], [pm.res], bf=False)
        V(lambda: nc.vector.tensor_reduce(out=gm[:], in_=pT[0:8, 0:128], axis=AX.X, op=ALU.max), r=[pT.res], w=[gm.res])
        V(lambda: nc.vector.tensor_scalar(out=dg[:], in0=ident_f[0:8, 0:8], scalar1=gm[:, 0:1], scalar2=None,
                                          op0=ALU.mult), r=[ident_f.res, gm.res], w=[dg.res])
        pb_ = PF()
        mm(pb_, pb_[:, 0:8], [(ones_f[0:8, :], dg[:])], [ones_f.res, dg.res])
        V(lambda: nc.vector.tensor_scalar(out=hi[:], in0=pb_[:, 0:4], scalar1=1.0, scalar2=None, op0=ALU.add),
          r=[pb_.res], w=[hi.res])
        V(lambda: nc.vector.tensor_scalar(out=lo[:], in0=pb_[:, 4:8], scalar1=-1.0, scalar2=None, op0=ALU.mult),
          r=[pb_.res], w=[lo.res])
        for it in range(NIT):
            V(lambda: nc.vector.tensor_tensor(out=mid[:], in0=lo[:], in1=hi[:], op=ALU.add), r=[lo.res, hi.res],
              w=[mid.res])
            V(lambda: nc.vector.tensor_scalar(out=mid[:], in0=mid[:], scalar1=0.5, scalar2=None, op0=ALU.mult), r=[],
              w=[mid.res])
            V(lambda: nc.vector.tensor_tensor(out=cmpb[:], in0=Isc[:], in1=mid[:].unsqueeze(2).to_broadcast([128, 4, NPG]),
                                              op=ALU.is_ge), r=[Isc.res, mid.res], w=[cmpb.res])
            V(lambda: nc.vector.tensor_reduce(out=cntp[:], in_=cmpb[:], axis=AX.X, op=ALU.add), r=[cmpb.res],
              w=[cntp.res])
            pc = PF()
            mm(pc, pc[:, 0:4], [(ones_f[:], cntp[:])], [ones_f.res, cntp.res])
            V(lambda: nc.vector.tensor_scalar(out=ge[:], in0=pc[:, 0:4], scalar1=256.0, scalar2=None, op0=ALU.is_ge),
              r=[pc.res], w=[ge.res])
            V(lambda: nc.vector.tensor_tensor(out=t1[:], in0=mid[:], in1=lo[:], op=ALU.subtract), r=[], w=[t1.res])
            V(lambda: nc.vector.tensor_tensor(out=t1[:], in0=t1[:], in1=ge[:], op=ALU.mult), r=[], w=[t1.res])
            V(lambda: nc.vector.tensor_tensor(out=t2[:], in0=hi[:], in1=mid[:], op=ALU.subtract), r=[], w=[t2.res])
            V(lambda: nc.vector.tensor_tensor(out=t2[:], in0=t2[:], in1=ge[:], op=ALU.mult), r=[], w=[t2.res])
            V(lambda: nc.vector.tensor_tensor(out=lo[:], in0=lo[:], in1=t1[:], op=ALU.add), r=[], w=[lo.res])
            V(lambda: nc.vector.tensor_tensor(out=hi[:], in0=mid[:], in1=t2[:], op=ALU.add), r=[], w=[hi.res])
        V(lambda: nc.vector.tensor_tensor(out=msk[:], in0=Isc[:], in1=lo[:].unsqueeze(2).to_broadcast([128, 4, NPG]),
                                          op=ALU.is_ge), r=[Isc.res, lo.res], w=[msk.res])

        if SODD < 3:
            S.barrier()
            return
        pTo = PF()
        for s in range(4):
            for pg in range(NPG):
                if pg < 64:
                    kb = kpg[pg % 4]
                    gather(kb, kb[:], ck, s * 64 + pg)
                    src, srcr = kb[:], [kb.res]
                else:
                    src, srcr = kvn[:, s, 0:256], [qb.res]
                pr = prod2[pg % 2]
                V(lambda: nc.vector.tensor_tensor(
                    out=pr[:].rearrange("p (g r) d -> p g r d", g=4),
                    in0=qb[:, s, :].rearrange("p (g r d) -> p g r d", g=4, r=4),
                    in1=src.rearrange("p (g d) -> p g d", d=64).unsqueeze(2).to_broadcast([128, 4, 4, 64]),
                    op=ALU.mult), r=[qb.res] + srcr, w=[pr.res])
                V(lambda: nc.vector.tensor_reduce(out=lg[:, pg, :], in_=pr[:], axis=AX.X, op=ALU.add), r=[pr.res],
                  w=[lg.res])
            A(lambda: nc.scalar.activation(out=PT[:], in_=lg[:], func=AF.Exp, scale=0.125), r=[lg.res], w=[PT.res])
            V(lambda: nc.vector.tensor_tensor(out=PT[:], in0=PT[:], in1=msk[:, s, :].unsqueeze(2).to_broadcast([128, NPG, 16]),
                                              op=ALU.mult), r=[msk.res], w=[PT.res])
            if SODD < 3.5:
                continue
            po = PF()
            for pg in range(NPG):
                if pg < 64:
                    vb = vpg[pg % 4]
                    gather(vb, vb[:, 0:256], cvv, s * 64 + pg)
                else:
                    vb = vnw
                    V(lambda: nc.vector.tensor_copy(out=vnw[:, 0:256], in_=kvn[:, s, 256:512]), r=[qb.res], w=[vnw.res])
                mm(po, po[0:16, 0:260], [(PT[:, pg, :], vb[:])], [PT.res, vb.res], first=(pg == 0), start=(pg == 0),
                   stop=(pg == NPG - 1))
            if SODD < 3.7:
                continue
            V(lambda: nc.vector.tensor_tensor(out=om[:], in0=po[0:16, 0:256].rearrange("p (g d) -> p g d", d=64),
                                              in1=bmk[:].unsqueeze(2).to_broadcast([16, 4, 64]), op=ALU.mult),
              r=[po.res, bmk.res], w=[om.res])
            V(lambda: nc.vector.tensor_reduce(out=o1[:], in_=om[:].rearrange("p g d -> p d g"), axis=AX.X, op=ALU.add),
              r=[om.res], w=[o1.res])
            V(lambda: nc.vector.reciprocal(out=rd[:], in_=po[0:16, 256:257]), r=[po.res], w=[rd.res])
            V(lambda: nc.vector.tensor_scalar(out=od[:, s, 0:64], in0=o1[:], scalar1=rd[:, 0:1], scalar2=None,
                                              op0=ALU.mult), r=[o1.res, rd.res], w=[od.res])
            V(lambda: nc.vector.tensor_scalar(out=od[:, s, 64:128], in0=o1[:], scalar1=rd[:, 0:1], scalar2=None,
                                              op0=ALU.mult), r=[o1.res, rd.res], w=[od.res])
            tr(pTo, pTo[:, s * 16:(s + 1) * 16], od[0:16, s, :], [od.res], bf=False, first=(s == 0))
        if SODD < 3.8:
            S.barrier()
            return
        oT = sb(es, "sd_oT", [128, 64])
        V(lambda: nc.vector.tensor_copy(out=oT[:], in_=pTo[:, 0:64]), r=[pTo.res], w=[oT.res])
        yc32 = sb(es, "sd_yc32", [128, 8, 4])
        for e in range(2):
            ps_ = slice(e * 64, (e + 1) * 64)
            for s in range(4):
                V(lambda: nc.vector.tensor_tensor(
                    out=yc32[ps_, :, s],
                    in0=oT[ps_, s * 16:(s + 1) * 16].rearrange("p (kt e) -> p kt e", e=2)[:, :, e],
                    in1=zT[ps_, :, s], op=ALU.mult), r=[oT.res, zT.res], w=[yc32.res])
        V(lambda: nc.vector.tensor_copy(out=ycT[:, 0:8, :], in_=yc32[:]), r=[yc32.res], w=[ycT.res])
        S.barrier()


_L = 8192
_PROMPT_KEYS = ["w_in_even", "ssd_conv_w", "ssd_conv_b", "ssd_dt_bias", "ssd_a_log", "ssd_d", "ssd_norm_g", "cf_dw_w",
                "cf_dw_b", "cf_ln_g", "cf_ln_b", "w_out_even", "w_in_odd", "w_out_odd", "ln_g", "ln_b"]


def core_inputs(inp, c, xp):
    m = {}
    if xp is not None:
        m["x_prompt"] = xp
    for k in _PROMPT_KEYS:
        m[k] = inp[k]
    sl = slice(4 * c, 4 * c + 4)
    m["x_sample"] = np.ascontiguousarray(inp["x_sample"][sl].reshape(4, D))
    m["page_table"] = np.ascontiguousarray(inp["page_table"][sl].reshape(1, 256).astype(np.int32))
    st = [(inp["state_ssm_l0"], inp["state_ssdconv_l0"], inp["state_cfconv_l0"]),
          (inp["state_ssm_l2"], inp["state_ssdconv_l2"], inp["state_cfconv_l2"])]
    ch = [(inp["cache_k_l1"], inp["cache_v_l1"], inp["cache_kidx_l1"]),
          (inp["cache_k_l3"], inp["cache_v_l3"], inp["cache_kidx_l3"])]
    for j in range(2):
        m["state_ssm%d" % j] = np.ascontiguousarray(st[j][0][sl].reshape(4, 1024, 128))
        m["state_ssdconv%d" % j] = np.ascontiguousarray(st[j][1][sl])
        m["state_cfconv%d" % j] = np.ascontiguousarray(st[j][2][sl])
        m["cache_k%d" % j] = ch[j][0].reshape(-1, 256)
        m["cache_v%d" % j] = ch[j][1].reshape(-1, 256)
        m["cache_ki%d" % j] = ch[j][2].reshape(-1, 64)
    return m


def kernel(**inputs):
    inp = {k: np.ascontiguousarray(np.asarray(v)) for k, v in inputs.items()}
    n = 8
    nc = build(_L, layers=4)
    in_maps = [core_inputs(inp, c, inp["x_prompt"][c // 4]) for c in range(n)]
    res = run_bass_kernel_spmd(nc, in_maps, core_ids=list(range(n)))
    R = res.results

    def pr(name, shape):
        return np.stack([np.asarray(R[0][name], np.float32).reshape(shape),
                         np.asarray(R[4][name], np.float32).reshape(shape)])

    def sm(name, shape):
        return np.concatenate([np.asarray(R[c][name], np.float32).reshape((4,) + shape) for c in range(n)], axis=0)

    outs = [pr("y_prompt", (_L, D)), sm("y_s", (1, D))]
    for l in range(4):
        j = l // 2
        if l % 2 == 0:
            outs += [pr("ssm_p%d" % j, (16, 64, 128)), sm("ssm_s%d" % j, (16, 64, 128)),
                     pr("sc_p%d" % j, (3, 1536)), sm("sc_s%d" % j, (3, 1536)),
                     pr("cc_p%d" % j, (30, 1024)), sm("cc_s%d" % j, (30, 1024))]
        else:
            outs += [pr("k_p%d" % j, (_L, 4, 64)), sm("k_s%d" % j, (1, 4, 64)),
                     pr("v_p%d" % j, (_L, 4, 64)), sm("v_s%d" % j, (1, 4, 64)),
                     pr("ki_p%d" % j, (_L, 64)), sm("ki_s%d" % j, (1, 64))]
    return tuple(outs)
```

```python
import contextlib
import os
import numpy as np
STAGE = float(os.environ.get("KSTAGE", "99"))
SKIP = set(os.environ.get("KSKIP", "").split(","))
SODD = float(os.environ.get("SODD", "99"))
import concourse.bass as bass
import concourse.mybir as mybir
from concourse.bass_utils import run_bass_kernel_spmd

F32 = mybir.dt.float32
BF16 = mybir.dt.bfloat16
I32 = mybir.dt.int32
U32 = mybir.dt.uint32
ALU = mybir.AluOpType
AF = mybir.ActivationFunctionType
AX = mybir.AxisListType

NDS = 24
D = 1024
EVEN_IN = 5648
ODD_IN = 3144
ALPHA = 8.0 ** 0.25
EPS = 1e-5
NEG = -1.0e5


class Res:
    __slots__ = ("name", "w", "r", "ex")

    def __init__(self, name=""):
        self.name = name
        self.w = None
        self.r = {}
        self.ex = False


class Sched:
    def __init__(self, nc, es):
        self.nc = nc
        self.eng = {"pe": nc.tensor, "dve": nc.vector, "act": nc.scalar, "pool": nc.gpsimd, "sp": nc.sync}
        self.semh = {}
        self.cnt = {}
        for k in self.eng:
            self.semh[k] = es.enter_context(nc.semaphore("s_" + k))
            self.cnt[k] = 0
        self.known = {k: {} for k in self.eng}
        self.dq = {}
        for q in ("sp", "pool", "act"):
            keys = []
            for i in range(NDS):
                key = "d_%s_%d" % (q, i)
                self.semh[key] = es.enter_context(nc.semaphore(key))
                self.cnt[key] = 0
                keys.append(key)
            self.dq[q] = [keys, 0]
        self.ninst = 0

    def _waits(self, e, reads, writes):
        need = {}
        for r in reads:
            if r.w is not None:
                k, v = r.w
                if need.get(k, 0) < v:
                    need[k] = v
        for w in writes:
            if w.w is not None:
                k, v = w.w
                if need.get(k, 0) < v:
                    need[k] = v
            for k, v in w.r.items():
                if need.get(k, 0) < v:
                    need[k] = v
        known = self.known[e]
        for k, v in need.items():
            if known.get(k, 0) >= v:
                continue
            self.eng[e].wait_ge(self.semh[k], v)
            known[k] = v

    def _commit(self, tok, reads, writes):
        k, v = tok
        for r in reads:
            if r.r.get(k, 0) < v:
                r.r[k] = v
        for w in writes:
            w.w = tok
            w.r = {}

    def op(self, e, fn, reads=(), writes=(), acc_writes=()):
        if e != "pe":
            exr = [r for r in reads if r.ex]
            if exr:
                writes = list(writes) + exr
                reads = [r for r in reads if not r.ex]
        self._waits(e, reads, writes)
        inst = fn()
        self.cnt[e] += 1
        inst.then_inc(self.semh[e], 1)
        self._commit((e, self.cnt[e]), reads, list(writes) + list(acc_writes))
        self.ninst += 1
        return inst

    def dma(self, q, fn, reads=(), writes=()):
        keys, nxt = self.dq[q]
        key = keys[nxt]
        self.dq[q][1] = (nxt + 1) % NDS
        self._waits(q, reads, writes)
        known = self.known[q]
        if known.get(key, 0) < self.cnt[key]:
            self.eng[q].wait_ge(self.semh[key], self.cnt[key])
            known[key] = self.cnt[key]
        inst = fn(self.eng[q])
        self.cnt[key] += 16
        inst.then_inc(self.semh[key], 16)
        self._commit((key, self.cnt[key]), reads, writes)
        self.ninst += 1
        return inst

    def barrier(self):
        for e in self.eng:
            known = self.known[e]
            for k, h in self.semh.items():
                v = self.cnt[k]
                if v == 0 or known.get(k, 0) >= v:
                    continue
                self.eng[e].wait_ge(h, v)
                known[k] = v

    def final_wait(self, e="sp"):
        known = self.known[e]
        for k, h in self.semh.items():
            v = self.cnt[k]
            if v == 0 or known.get(k, 0) >= v:
                continue
            self.eng[e].wait_ge(h, v)
            known[k] = v


class TT:
    def __init__(self, t, name=""):
        self.t = t
        self.res = Res(name)

    def __getitem__(self, idx):
        return self.t[idx]


class Ctx:
    pass


def build(L, layers=4, NS=4, NPOOL=2560, do_prompt=True, do_sample=True):
    NTI = L // 128
    NMT = L // 512
    nc = bass.Bass("TRN2", target_bir_lowering=False)
    C = Ctx()
    C.nc = nc

    def din(name, shape, dt=F32):
        return nc.dram_tensor(name, list(shape), dt, kind="ExternalInput").ap()

    def dout(name, shape, dt=F32):
        return nc.dram_tensor(name, list(shape), dt, kind="ExternalOutput").ap()

    def dscr(name, shape, dt=F32):
        return nc.dram_tensor(name, list(shape), dt, kind="Internal").ap()

    I = {}
    if do_sample:
        for j in (1, 0):
            I["cache_ki%d" % j] = din("cache_ki%d" % j, [NPOOL * 128, 64])
            I["cache_k%d" % j] = din("cache_k%d" % j, [NPOOL * 128, 256])
            I["cache_v%d" % j] = din("cache_v%d" % j, [NPOOL * 128, 256])
    I["x_prompt"] = din("x_prompt", [L, D])
    I["w_in_even"] = din("w_in_even", [2, D, EVEN_IN])
    I["ssd_conv_w"] = din("ssd_conv_w", [2, 4, 1536])
    I["ssd_conv_b"] = din("ssd_conv_b", [2, 1536])
    I["ssd_dt_bias"] = din("ssd_dt_bias", [2, 16])
    I["ssd_a_log"] = din("ssd_a_log", [2, 16])
    I["ssd_d"] = din("ssd_d", [2, 16])
    I["ssd_norm_g"] = din("ssd_norm_g", [2, 1024])
    I["cf_dw_w"] = din("cf_dw_w", [2, 31, 1024])
    I["cf_dw_b"] = din("cf_dw_b", [2, 1024])
    I["cf_ln_g"] = din("cf_ln_g", [2, 1024])
    I["cf_ln_b"] = din("cf_ln_b", [2, 1024])
    I["w_out_even"] = din("w_out_even", [2, 2048, D])
    I["w_in_odd"] = din("w_in_odd", [2, D, ODD_IN])
    I["w_out_odd"] = din("w_out_odd", [2, D, D])
    I["ln_g"] = din("ln_g", [4, D])
    I["ln_b"] = din("ln_b", [4, D])
    if do_sample:
        I["x_sample"] = din("x_sample", [NS, D])
        I["page_table"] = din("page_table", [1, NS * 64], I32)
        for j in range(2):
            I["state_ssm%d" % j] = din("state_ssm%d" % j, [NS, 1024, 128])
            I["state_ssdconv%d" % j] = din("state_ssdconv%d" % j, [NS, 3, 1536])
            I["state_cfconv%d" % j] = din("state_cfconv%d" % j, [NS, 30, 1024])

    O = {}
    O["y_prompt"] = dout("y_prompt", [L, D])
    for j in range(2):
        O["ssm_p%d" % j] = dout("ssm_p%d" % j, [1024, 128])
        O["sc_p%d" % j] = dout("sc_p%d" % j, [3, 1536])
        O["cc_p%d" % j] = dout("cc_p%d" % j, [30, 1024])
        O["k_p%d" % j] = dout("k_p%d" % j, [L, 256])
        O["v_p%d" % j] = dout("v_p%d" % j, [L, 256])
        O["ki_p%d" % j] = dout("ki_p%d" % j, [L, 64])

    if do_sample:
        O["y_s"] = dout("y_s", [NS, D])
        if SODD < 99:
            O["dbg_i"] = dout("dbg_i", [128, 256], I32)
            O["dbg_f"] = dout("dbg_f", [128, 4 * 65])
        for j in range(2):
            O["ssm_s%d" % j] = dout("ssm_s%d" % j, [NS, 1024, 128])
            O["sc_s%d" % j] = dout("sc_s%d" % j, [NS, 3, 1536])
            O["cc_s%d" % j] = dout("cc_s%d" % j, [NS, 30, 1024])
            O["k_s%d" % j] = dout("k_s%d" % j, [NS, 256])
            O["v_s%d" % j] = dout("v_s%d" % j, [NS, 256])
            O["ki_s%d" % j] = dout("ki_s%d" % j, [NS, 64])

    xtm = dscr("xtm", [L, D])
    xT = dscr("xT", [D, L], BF16)
    ycat = dscr("ycat", [2048, L], BF16)
    R_xtm = [Res("xtm%d" % i) for i in range(NTI)]
    R_xT = [Res("xT%d" % i) for i in range(NTI)]
    R_ycat = [Res("ycat%d" % i) for i in range(NTI)]
    C.qT_d = dscr("qT_d", [128, 8, L], BF16)
    C.kT_d = dscr("kT_d", [128, 2, L], BF16)
    C.qiT_d = dscr("qiT_d", [64, 8, L], BF16)
    C.kiT_d = dscr("kiT_d", [64, L], BF16)
    C.V1_d = dscr("V1_d", [L, 260], BF16)
    C.wi_d = dscr("wi_d", [L, 8])
    C.zs_d = dscr("zs_d", [L, 1024], BF16)
    C.R_odd = Res("odd_scratch")

    with contextlib.ExitStack() as es:
        S = Sched(nc, es)
        C.S = S

        C.uid = 0

        def sb(es_, name, shape, dt=F32):
            C.uid += 1
            name = "%s_u%d" % (name, C.uid)
            return TT(es_.enter_context(nc.sbuf_tensor(name, list(shape), dt)), name)

        def ps(es_, name, shape, dt=F32):
            t = TT(es_.enter_context(nc.psum_tensor(name, list(shape), dt)), name)
            t.res.ex = True
            return t

        V = lambda fn, r=(), w=(): S.op("dve", fn, r, w)
        A = lambda fn, r=(), w=(): S.op("act", fn, r, w)
        G = lambda fn, r=(), w=(): S.op("pool", fn, r, w)

        ones_f = sb(es, "ones_f", [128, 128])
        zeros_f = sb(es, "zeros_f", [128, 128])
        ones_bf = sb(es, "ones_bf", [128, 128], BF16)
        ident_f = sb(es, "ident_f", [128, 128])
        ident_bf = sb(es, "ident_bf", [128, 128], BF16)
        U_f = sb(es, "U_f", [128, 128])
        negm = sb(es, "negm", [128, 128])
        Esel = sb(es, "Esel", [16, 16, 128])
        G(lambda: nc.gpsimd.memset(ones_f[:], 1.0), w=[ones_f.res])
        G(lambda: nc.gpsimd.memset(zeros_f[:], 0.0), w=[zeros_f.res])
        G(lambda: nc.gpsimd.memset(ones_bf[:], 1.0), w=[ones_bf.res])
        G(lambda: nc.gpsimd.memset(Esel[:], 1.0), w=[Esel.res])
        G(lambda: nc.gpsimd.affine_select(out=ident_f[:], in_=ones_f[:], pattern=[[-1, 128]], compare_op=ALU.is_equal,
                                          fill=0.0, base=0, channel_multiplier=1), r=[ones_f.res], w=[ident_f.res])
        G(lambda: nc.gpsimd.tensor_copy(out=ident_bf[:], in_=ident_f[:]), r=[ident_f.res], w=[ident_bf.res])
        G(lambda: nc.gpsimd.affine_select(out=U_f[:], in_=ones_f[:], pattern=[[1, 128]], compare_op=ALU.is_ge,
                                          fill=0.0, base=0, channel_multiplier=-1), r=[ones_f.res], w=[U_f.res])
        G(lambda: nc.gpsimd.affine_select(out=negm[:], in_=zeros_f[:], pattern=[[1, 128]], compare_op=ALU.is_ge,
                                          fill=NEG, base=0, channel_multiplier=-1), r=[zeros_f.res], w=[negm.res])
        G(lambda: nc.gpsimd.affine_select(out=Esel[:], in_=Esel[:], pattern=[[-1, 16], [0, 128]], compare_op=ALU.is_equal,
                                          fill=0.0, base=0, channel_multiplier=1), r=[], w=[Esel.res])

        C.do_prompt, C.do_sample, C.NS = do_prompt, do_sample, NS
        if do_sample:
            bmk = sb(es, "bmk", [16, 4])
            negpage = sb(es, "negpage", [128, 1])
            iota_i = sb(es, "iota_i", [128, 1], I32)
            iota_f = sb(es, "iota_f", [128, 1])
            G(lambda: nc.gpsimd.affine_select(out=bmk[:], in_=ones_f[0:16, 0:4], pattern=[[-4, 4]], compare_op=ALU.is_ge,
                                              fill=0.0, base=0, channel_multiplier=1), r=[ones_f.res], w=[bmk.res])
            G(lambda: nc.gpsimd.affine_select(out=bmk[:], in_=bmk[:], pattern=[[4, 4]], compare_op=ALU.is_ge,
                                              fill=0.0, base=3, channel_multiplier=-1), r=[], w=[bmk.res])
            G(lambda: nc.gpsimd.affine_select(out=negpage[:], in_=zeros_f[:, 0:1], pattern=[[0, 1]], compare_op=ALU.is_ge,
                                              fill=-1.0e30, base=0, channel_multiplier=-1), r=[zeros_f.res],
              w=[negpage.res])
            G(lambda: nc.gpsimd.iota(iota_i[:], pattern=[[0, 1]], base=0, channel_multiplier=1), w=[iota_i.res])
            G(lambda: nc.gpsimd.tensor_copy(out=iota_f[:], in_=iota_i[:]), r=[iota_i.res], w=[iota_f.res])
            C.xs_s = sb(es, "xs_s", [NS, D])
            C.xsT = sb(es, "xsT", [128, 8, NS], BF16)
            C.ycT = sb(es, "ycT", [128, 16, NS], BF16)

        pf = [ps(es, "pf%d" % i, [128, 512]) for i in range(6)]
        pb = [ps(es, "pb%d" % i, [128, 1024], BF16) for i in range(2)]
        C.pfi = 0
        C.pbi = 0

        def PF(i=None):
            if i is not None:
                return pf[i]
            t = pf[C.pfi]
            C.pfi = (C.pfi + 1) % len(pf)
            return t

        def PB():
            t = pb[C.pbi]
            C.pbi = (C.pbi + 1) % len(pb)
            return t

        def mm(pt, out_ap, pairs, reads, first=True, start=True, stop=True, sgc=False):
            n = len(pairs)
            for i, (l, r) in enumerate(pairs):
                st_ = start if i == 0 else False
                sp_ = stop if i == n - 1 else False
                if i == 0 and first:
                    S.op("pe", lambda: nc.tensor.matmul(out_ap, lhsT=l, rhs=r, start=st_, stop=sp_,
                                                        skip_group_check=sgc),
                         reads=reads, writes=[pt.res])
                else:
                    S.op("pe", lambda: nc.tensor.matmul(out_ap, lhsT=l, rhs=r, start=st_, stop=sp_,
                                                        skip_group_check=sgc),
                         reads=reads, acc_writes=[pt.res])

        def tr(pt, out_ap, in_ap, reads, bf=True, first=True):
            idn = ident_bf if bf else ident_f
            kp = in_ap.shape[0]
            if first:
                S.op("pe", lambda: nc.tensor.transpose(out=out_ap, in_=in_ap, identity=idn[0:kp, 0:kp]),
                     reads=list(reads) + [idn.res], writes=[pt.res])
            else:
                S.op("pe", lambda: nc.tensor.transpose(out=out_ap, in_=in_ap, identity=idn[0:kp, 0:kp]),
                     reads=list(reads) + [idn.res], acc_writes=[pt.res])


        def load_weight_bf16(esl, w_ap, K, N, dst, name):
            KT = K // 128
            with contextlib.ExitStack() as es2:
                CW = 4096 // KT
                stg = [sb(es2, "%s_stg%d" % (name, i), [128, KT, CW]) for i in range(2)]
                wv = w_ap.rearrange("(k p) n -> p k n", p=128)
                nchunk = (N + CW - 1) // CW
                for c in range(nchunk):
                    c0 = c * CW
                    cw = min(CW, N - c0)
                    st = stg[c % 2]
                    S.dma("sp" if c % 2 == 0 else "pool",
                          lambda q: q.dma_start(out=st[:, :, 0:cw], in_=wv[:, :, c0:c0 + cw]), writes=[st.res])
                    if c % 2 == 0:
                        A(lambda: nc.scalar.copy(out=dst[:, :, c0:c0 + cw], in_=st[:, :, 0:cw]), r=[st.res], w=[dst.res])
                    else:
                        V(lambda: nc.vector.tensor_copy(out=dst[:, :, c0:c0 + cw], in_=st[:, :, 0:cw]), r=[st.res],
                          w=[dst.res])
                S.barrier()

        def prep_xT(x_src):
            with contextlib.ExitStack() as es2:
                xs = [sb(es2, "px%d" % i, [128, D]) for i in range(2)]
                xb = [sb(es2, "pxb%d" % i, [128, D], BF16) for i in range(2)]
                xt = [sb(es2, "pxt%d" % i, [128, 8, 128], BF16) for i in range(2)]
                for i in range(NTI):
                    a, b, c = xs[i % 2], xb[i % 2], xt[i % 2]
                    S.dma("sp", lambda q: q.dma_start(out=a[:], in_=x_src[i * 128:(i + 1) * 128, :]), writes=[a.res])
                    V(lambda: nc.vector.tensor_copy(out=b[:], in_=a[:]), r=[a.res], w=[b.res])
                    p = PB()
                    for k in range(8):
                        tr(p, p[:, k * 128:(k + 1) * 128], b[:, k * 128:(k + 1) * 128], [b.res], first=(k == 0))
                    A(lambda: nc.scalar.copy(out=c[:], in_=p[:].rearrange("p (k t) -> p k t", k=8)), r=[p.res], w=[c.res])
                    S.dma("pool", lambda q: q.dma_start(
                        out=xT.rearrange("(k p) t -> p k t", p=128)[:, :, i * 128:(i + 1) * 128], in_=c[:]),
                        reads=[c.res], writes=[R_xT[i]])
                S.barrier()

        def out_pass(l, KT, w_out_sb, x_src, is_last):
            with contextlib.ExitStack() as es2:
                yt = [sb(es2, "oy%d" % i, [128, KT, 128], BF16) for i in range(2)]
                xs = [sb(es2, "ox%d" % i, [128, D]) for i in range(2)]
                hb = [sb(es2, "oh%d" % i, [128, D]) for i in range(2)]
                xn = [sb(es2, "oxn%d" % i, [128, D]) for i in range(2)]
                xb = [sb(es2, "oxb%d" % i, [128, D], BF16) for i in range(2)]
                xt = [sb(es2, "oxt%d" % i, [128, 8, 128], BF16) for i in range(2)]
                st = [sb(es2, "ost%d" % i, [128, 2, 6]) for i in range(2)]
                mv = [sb(es2, "omv%d" % i, [128, 4]) for i in range(2)]
                dst = O["y_prompt"] if is_last else xtm
                lng = sb(es2, "lng", [128, D])
                lnb = sb(es2, "lnb", [128, D])
                S.dma("sp", lambda q: q.dma_start(out=lng[:], in_=I["ln_g"][l:l + 1, :].to_broadcast([128, D])),
                      writes=[lng.res])
                S.dma("sp", lambda q: q.dma_start(out=lnb[:], in_=I["ln_b"][l:l + 1, :].to_broadcast([128, D])),
                      writes=[lnb.res])
                if do_sample:
                    sample_out(C, l, KT, w_out_sb, lng, lnb, is_last)
                for i in range(NTI if do_prompt else 0):
                    b = i % 2
                    S.dma("sp", lambda q: q.dma_start(
                        out=yt[b][:], in_=ycat.rearrange("(k p) t -> p k t", p=128)[:, 0:KT, i * 128:(i + 1) * 128]),
                        reads=[R_ycat[i]], writes=[yt[b].res])
                    S.dma("sp", lambda q: q.dma_start(out=xs[b][:], in_=x_src[i * 128:(i + 1) * 128, :]),
                          reads=[R_xtm[i]], writes=[xs[b].res])
                    for hf in range(2):
                        p = PF()
                        mm(p, p[:], [(yt[b][:, k, :], w_out_sb[:, k, hf * 512:(hf + 1) * 512]) for k in range(KT)],
                           [yt[b].res, w_out_sb.res])
                        V(lambda: nc.vector.scalar_tensor_tensor(out=hb[b][:, hf * 512:(hf + 1) * 512],
                                                                 in0=xs[b][:, hf * 512:(hf + 1) * 512], scalar=ALPHA,
                                                                 in1=p[:], op0=ALU.mult, op1=ALU.add),
                          r=[xs[b].res, p.res], w=[hb[b].res])
                    for hf in range(2):
                        V(lambda: nc.vector.bn_stats(out=st[b][:, hf, :], in_=hb[b][:, hf * 512:(hf + 1) * 512]),
                          r=[hb[b].res], w=[st[b].res])
                    V(lambda: nc.vector.bn_aggr(out=mv[b][:, 0:2], in_=st[b][:].rearrange("p a b -> p (a b)")),
                      r=[st[b].res], w=[mv[b].res])
                    V(lambda: nc.vector.tensor_scalar(out=mv[b][:, 2:3], in0=mv[b][:, 1:2], scalar1=EPS, scalar2=None,
                                                      op0=ALU.add), r=[mv[b].res], w=[mv[b].res])
                    A(lambda: nc.scalar.activation(out=mv[b][:, 2:3], in_=mv[b][:, 2:3], func=AF.Sqrt),
                      r=[mv[b].res], w=[mv[b].res])
                    V(lambda: nc.vector.reciprocal(out=mv[b][:, 3:4], in_=mv[b][:, 2:3]), r=[mv[b].res], w=[mv[b].res])
                    V(lambda: nc.vector.tensor_scalar(out=xn[b][:], in0=hb[b][:], scalar1=mv[b][:, 0:1],
                                                      scalar2=mv[b][:, 3:4], op0=ALU.subtract, op1=ALU.mult),
                      r=[hb[b].res, mv[b].res], w=[xn[b].res])
                    G(lambda: nc.gpsimd.tensor_tensor(out=xn[b][:], in0=xn[b][:], in1=lng[:], op=ALU.mult),
                      r=[lng.res], w=[xn[b].res])
                    G(lambda: nc.gpsimd.tensor_tensor(out=xn[b][:], in0=xn[b][:], in1=lnb[:], op=ALU.add),
                      r=[lnb.res], w=[xn[b].res])
                    S.dma("pool", lambda q: q.dma_start(out=dst[i * 128:(i + 1) * 128, :], in_=xn[b][:]),
                          reads=[xn[b].res], writes=[R_xtm[i]])
                    if not is_last:
                        A(lambda: nc.scalar.copy(out=xb[b][:], in_=xn[b][:]), r=[xn[b].res], w=[xb[b].res])
                        p = PB()
                        for k in range(8):
                            tr(p, p[:, k * 128:(k + 1) * 128], xb[b][:, k * 128:(k + 1) * 128], [xb[b].res],
                               first=(k == 0))
                        A(lambda: nc.scalar.copy(out=xt[b][:], in_=p[:].rearrange("p (k t) -> p k t", k=8)),
                          r=[p.res], w=[xt[b].res])
                        S.dma("pool", lambda q: q.dma_start(
                            out=xT.rearrange("(k p) t -> p k t", p=128)[:, :, i * 128:(i + 1) * 128], in_=xt[b][:]),
                            reads=[xt[b].res], writes=[R_xT[i]])
                S.barrier()

        C.sb = sb
        C.PF = PF
        C.PB = PB
        C.mm = mm
        C.tr = tr
        C.V, C.A, C.G = V, A, G
        C.I, C.O = I, O
        C.consts = dict(ones_f=ones_f, zeros_f=zeros_f, ones_bf=ones_bf, ident_f=ident_f, ident_bf=ident_bf, U_f=U_f,
                        negm=negm, Esel=Esel)
        if do_sample:
            C.consts.update(bmk=bmk, negpage=negpage, iota_f=iota_f)
        C.xT, C.xtm, C.ycat = xT, xtm, ycat
        C.R_xT, C.R_xtm, C.R_ycat = R_xT, R_xtm, R_ycat
        C.L, C.NTI, C.NMT = L, NTI, NMT
        C.load_weight_bf16 = load_weight_bf16

        if do_prompt:
            prep_xT(I["x_prompt"])
        if do_sample:
            S.dma("sp", lambda q: q.dma_start(out=C.xs_s[:], in_=I["x_sample"][:, :]), writes=[C.xs_s.res])
            sample_xT(C)
        x_src = I["x_prompt"]
        for l in range(layers if STAGE > 0 else 0):
            j = l // 2
            with contextlib.ExitStack() as esl:
                if l % 2 == 0:
                    with contextlib.ExitStack() as esa:
                        even_pass_a(C, esa, j)
                        S.barrier()
                else:
                    with contextlib.ExitStack() as esa:
                        odd_pass_a(C, esa, j)
                        S.barrier()
                    if do_prompt:
                        with contextlib.ExitStack() as esa:
                            odd_pass_b(C, esa, j)
                            S.barrier()
                if STAGE < 5:
                    continue
                with contextlib.ExitStack() as eso:
                    KT = 16 if l % 2 == 0 else 8
                    w_out_sb = sb(eso, "w_out_sb", [128, KT, D], BF16)
                    load_weight_bf16(eso, (I["w_out_even"] if l % 2 == 0 else I["w_out_odd"])[j], KT * 128, D,
                                     w_out_sb, "wo")
                    out_pass(l, KT, w_out_sb, x_src, is_last=(l == layers - 1))
                x_src = xtm
        S.final_wait("sp")
        print("instructions:", S.ninst)
    return nc


def even_pass_a(C, esl, j):
    nc, S, sb, PF, PB, mm, tr, V, A, G, I, O = C.nc, C.S, C.sb, C.PF, C.PB, C.mm, C.tr, C.V, C.A, C.G, C.I, C.O
    K = C.consts
    L, NTI, NMT = C.L, C.NTI, C.NMT
    xT, ycat = C.xT, C.ycat
    w_in = sb(esl, "w_in_e", [128, 8, EVEN_IN], BF16)
    C.load_weight_bf16(esl, I["w_in_even"][j], D, EVEN_IN, w_in, "wi")
    es = esl
    cw4 = sb(es, "cw4", [128, 12, 4])
    cb4 = sb(es, "cb4", [128, 12])
    cw31 = sb(es, "cw31", [128, 8, 31])
    cb31 = sb(es, "cb31", [128, 8])
    cfg = sb(es, "cfg", [128, 8])
    cfb = sb(es, "cfb", [128, 8])
    dtb = sb(es, "dtb", [128, 16])
    a_bc = sb(es, "a_bc", [128, 16])
    d_bc = sb(es, "d_bc", [128, 16])
    ng_bc = sb(es, "ng_bc", [128, 1024])
    ngT = sb(es, "ngT", [128, 8])
    hp16 = sb(es, "hp16", [16, 3])
    with contextlib.ExitStack() as esp:
        pr1 = sb(esp, "pr1", [8, 1536])
        pr2 = sb(esp, "pr2", [35, 1024])
        S.dma("sp", lambda q: q.dma_start(out=pr2[34:35, :], in_=I["ssd_norm_g"][j:j + 1, :]), writes=[pr2.res])
        S.dma("sp", lambda q: q.dma_start(out=hp16[:, 0:1], in_=I["ssd_dt_bias"][j:j + 1, :].rearrange("o h -> h o")),
              writes=[hp16.res])
        S.dma("sp", lambda q: q.dma_start(out=hp16[:, 1:2], in_=I["ssd_a_log"][j:j + 1, :].rearrange("o h -> h o")),
              writes=[hp16.res])
        S.dma("sp", lambda q: q.dma_start(out=hp16[:, 2:3], in_=I["ssd_d"][j:j + 1, :].rearrange("o h -> h o")),
              writes=[hp16.res])
        S.dma("sp", lambda q: q.dma_start(out=pr1[0:4, :], in_=I["ssd_conv_w"][j]), writes=[pr1.res])
        S.dma("sp", lambda q: q.dma_start(out=pr1[4:5, :], in_=I["ssd_conv_b"][j:j + 1, :]), writes=[pr1.res])
        S.dma("sp", lambda q: q.dma_start(out=pr2[0:31, :], in_=I["cf_dw_w"][j]), writes=[pr2.res])
        S.dma("sp", lambda q: q.dma_start(out=pr2[31:32, :], in_=I["cf_dw_b"][j:j + 1, :]), writes=[pr2.res])
        S.dma("sp", lambda q: q.dma_start(out=pr2[32:33, :], in_=I["cf_ln_g"][j:j + 1, :]), writes=[pr2.res])
        S.dma("sp", lambda q: q.dma_start(out=pr2[33:34, :], in_=I["cf_ln_b"][j:j + 1, :]), writes=[pr2.res])
        for c in range(12):
            p = PF()
            tr(p, p[:, 0:5], pr1[0:5, c * 128:(c + 1) * 128], [pr1.res], bf=False)
            V(lambda: nc.vector.tensor_copy(out=cw4[:, c, :], in_=p[:, 0:4]), r=[p.res], w=[cw4.res])
            V(lambda: nc.vector.tensor_copy(out=cb4[:, c:c + 1], in_=p[:, 4:5]), r=[p.res], w=[cb4.res])
        for c in range(8):
            p = PF()
            tr(p, p[:, 0:35], pr2[0:35, c * 128:(c + 1) * 128], [pr2.res], bf=False)
            V(lambda: nc.vector.tensor_copy(out=ngT[:, c:c + 1], in_=p[:, 34:35]), r=[p.res], w=[ngT.res])
            V(lambda: nc.vector.tensor_copy(out=cw31[:, c, :], in_=p[:, 0:31]), r=[p.res], w=[cw31.res])
            V(lambda: nc.vector.tensor_copy(out=cb31[:, c:c + 1], in_=p[:, 31:32]), r=[p.res], w=[cb31.res])
            V(lambda: nc.vector.tensor_copy(out=cfg[:, c:c + 1], in_=p[:, 32:33]), r=[p.res], w=[cfg.res])
            V(lambda: nc.vector.tensor_copy(out=cfb[:, c:c + 1], in_=p[:, 33:34]), r=[p.res], w=[cfb.res])
        S.barrier()
    S.dma("sp", lambda q: q.dma_start(out=dtb[:], in_=I["ssd_dt_bias"][j:j + 1, :].to_broadcast([128, 16])),
          writes=[dtb.res])
    S.dma("sp", lambda q: q.dma_start(out=a_bc[:], in_=I["ssd_a_log"][j:j + 1, :].to_broadcast([128, 16])),
          writes=[a_bc.res])
    S.dma("sp", lambda q: q.dma_start(out=d_bc[:], in_=I["ssd_d"][j:j + 1, :].to_broadcast([128, 16])),
          writes=[d_bc.res])
    S.dma("sp", lambda q: q.dma_start(out=ng_bc[:], in_=I["ssd_norm_g"][j:j + 1, :].to_broadcast([128, 1024])),
          writes=[ng_bc.res])
    A(lambda: nc.scalar.activation(out=a_bc[:], in_=a_bc[:], func=AF.Exp), r=[], w=[a_bc.res])
    V(lambda: nc.vector.tensor_scalar(out=a_bc[:], in0=a_bc[:], scalar1=-1.0, scalar2=None, op0=ALU.mult), r=[],
      w=[a_bc.res])

    if C.do_sample:
        sample_even(C, j, w_in, dict(cw4=cw4, cb4=cb4, cw31=cw31, cb31=cb31, cfg=cfg, cfb=cfb, ngT=ngT, hp16=hp16))
    if STAGE < 2 or not C.do_prompt:
        return
    ST = sb(es, "ST", [128, 1024])
    sraw = sb(es, "sraw", [128, 12, 3])
    uraw = sb(es, "uraw", [128, 8, 30])
    esw = contextlib.ExitStack()
    es = esw
    xt_mt = [sb(es, "xt_mt%d" % i, [128, 8, 512], BF16) for i in range(1)]
    xroll = sb(es, "xroll", [128, 12, 515], BF16)
    uroll = sb(es, "uroll", [128, 8, 542], BF16)
    xbcT = sb(es, "xbcT", [128, 12, 512], BF16)
    zb = [sb(es, "zb%d" % i, [128, 512], BF16) for i in range(2)]
    ybT = [sb(es, "ybT%d" % i, [128, 512], BF16) for i in range(2)]
    yaT = [sb(es, "yaT%d" % i, [128, 8, 128], BF16) for i in range(2)]
    acc = [sb(es, "acc%d" % i, [128, 512]) for i in range(2)]
    sg = [sb(es, "sg%d" % i, [128, 512]) for i in range(2)]
    cvq = [sb(es, "cvq%d" % i, [128, 512], BF16) for i in range(2)]
    cvf = sb(es, "cvf", [128, 8, 512], BF16)
    mean_bc = sb(es, "mean_bc", [128, 512])
    rstd_bc = sb(es, "rstd_bc", [128, 512])
    tmp512 = sg
    STb = sb(es, "STb", [128, 1024], BF16)
    za = sb(es, "za", [128, 1024], BF16)
    dt = sb(es, "dt", [128, 16])
    dta = sb(es, "dta", [128, 16])
    acum = sb(es, "acum", [128, 16])
    nacum = sb(es, "nacum", [128, 16])
    ea = sb(es, "ea", [128, 16])
    toend = sb(es, "toend", [128, 16])
    cdec = sb(es, "cdec", [128, 16])
    acT = sb(es, "acT", [16, 128])
    xs_tm = sb(es, "xs_tm", [128, 1024], BF16)
    X = sb(es, "X", [128, 1024], BF16)
    Xw = sb(es, "Xw", [128, 1024], BF16)
    B_tm = sb(es, "B_tm", [128, 256], BF16)
    cbT = sb(es, "cbT", [128, 2, 128])
    decT = [sb(es, "decT%d" % i, [128, 128]) for i in range(2)]
    MT = [sb(es, "MT%d" % i, [128, 128], BF16) for i in range(4)]
    yv = sb(es, "yv", [128, 1024])
    ysq = mean_bc
    rms = sb(es, "rms", [128, 4])
    yab = sb(es, "yab", [128, 1024], BF16)

    V(lambda: nc.vector.memset(xroll[:], 0.0), w=[xroll.res])
    V(lambda: nc.vector.memset(uroll[:], 0.0), w=[uroll.res])
    V(lambda: nc.vector.memset(ST[:], 0.0), w=[ST.res])
    V(lambda: nc.vector.memset(STb[:], 0.0), w=[STb.res])

    xTv = xT.rearrange("(k p) t -> p k t", p=128)
    ycv = ycat.rearrange("(k p) t -> p k t", p=128)

    for m in range(NMT):
        last = (m == NMT - 1)
        xt = xt_mt[0]
        S.dma("sp", lambda q: q.dma_start(out=xt[:], in_=xTv[:, :, m * 512:(m + 1) * 512]),
              reads=[C.R_xT[4 * m + i] for i in range(4)], writes=[xt.res])

        def fproj(col0):
            p = PF()
            mm(p, p[:], [(w_in[:, k, col0:col0 + 128], xt[:, k, :]) for k in range(8)], [w_in.res, xt.res])
            return p

        if STAGE < 2.1:
            continue
        for c in range(12):
            p = fproj(1024 + c * 128)
            if "xcopy" not in SKIP:
                A(lambda: nc.scalar.copy(out=xroll[:, c, 3:515], in_=p[:]), r=[p.res], w=[xroll.res])
            if last and "sraw" not in SKIP:
                V(lambda: nc.vector.tensor_copy(out=sraw[:, c, :], in_=p[:, 509:512]), r=[p.res], w=[sraw.res])
        if STAGE < 2.2:
            continue
        for c in range(8):
            pv = fproj(2576 + c * 128)
            pg = fproj(3600 + c * 128)
            s_ = sg[c % 2]
            A(lambda: nc.scalar.activation(out=s_[:], in_=pg[:], func=AF.Sigmoid), r=[pg.res], w=[s_.res])
            V(lambda: nc.vector.tensor_tensor(out=uroll[:, c, 30:542], in0=pv[:], in1=s_[:], op=ALU.mult),
              r=[pv.res, s_.res], w=[uroll.res])
            if last:
                V(lambda: nc.vector.tensor_tensor(out=uraw[:, c, :], in0=pv[:, 482:512], in1=s_[:, 482:512],
                                                  op=ALU.mult), r=[pv.res, s_.res], w=[uraw.res])
        if STAGE < 2.3:
            continue
        for c in range(12):
            a_ = acc[c % 2]
            V(lambda: nc.vector.tensor_scalar(out=a_[:], in0=xroll[:, c, 0:512], scalar1=cw4[:, c, 0:1],
                                              scalar2=cb4[:, c:c + 1], op0=ALU.mult, op1=ALU.add),
              r=[xroll.res, cw4.res, cb4.res], w=[a_.res])
            for k in range(1, 4):
                V(lambda: nc.vector.scalar_tensor_tensor(out=a_[:], in0=xroll[:, c, k:k + 512], scalar=cw4[:, c, k:k + 1],
                                                         in1=a_[:], op0=ALU.mult, op1=ALU.add),
                  r=[xroll.res], w=[a_.res])
            A(lambda: nc.scalar.activation(out=xbcT[:, c, :], in_=a_[:], func=AF.Silu), r=[a_.res], w=[xbcT.res])
        G(lambda: nc.gpsimd.tensor_copy(out=xroll[:, :, 0:3], in_=xroll[:, :, 512:515]), r=[], w=[xroll.res])
        if STAGE < 2.4:
            continue
        p1 = PF(4)
        p2 = PF(5)

        def cf_taps():
            for c in range(8):
                a_ = acc[c % 2]
                V(lambda: nc.vector.tensor_scalar(out=a_[:], in0=uroll[:, c, 0:512], scalar1=cw31[:, c, 0:1],
                                                  scalar2=cb31[:, c:c + 1], op0=ALU.mult, op1=ALU.add),
                  r=[uroll.res, cw31.res, cb31.res], w=[a_.res])
                yield
                for k in range(1, 31):
                    V(lambda: nc.vector.scalar_tensor_tensor(out=a_[:] if k < 30 else cvf[:, c, :],
                                                             in0=uroll[:, c, k:k + 512], scalar=cw31[:, c, k:k + 1],
                                                             in1=a_[:], op0=ALU.mult, op1=ALU.add),
                      r=[uroll.res, a_.res], w=[a_.res] if k < 30 else [cvf.res])
                    yield

        taps = cf_taps()

        def pump(n_):
            for _ in range(n_):
                if next(taps, "done") == "done":
                    break

        if STAGE < 2.5:
            continue
        for cc in range(4 if STAGE > 3 else 0):
            cs = slice(cc * 128, (cc + 1) * 128)
            for hf in range(2):
                p = PF()
                mm(p, p[:], [(xt[:, k, cs], w_in[:, k, hf * 512:(hf + 1) * 512]) for k in range(8)], [w_in.res, xt.res])
                A(lambda: nc.scalar.activation(out=za[:, hf * 512:(hf + 1) * 512], in_=p[:], func=AF.Silu), r=[p.res],
                  w=[za.res])
            p = PF()
            mm(p, p[:, 0:16], [(xt[:, k, cs], w_in[:, k, 2560:2576]) for k in range(8)], [w_in.res, xt.res])
            V(lambda: nc.vector.tensor_tensor(out=dt[:], in0=p[:, 0:16], in1=dtb[:], op=ALU.add), r=[p.res, dtb.res],
              w=[dt.res])
            A(lambda: nc.scalar.activation(out=dt[:], in_=dt[:], func=AF.Exp), r=[], w=[dt.res])
            A(lambda: nc.scalar.activation(out=dt[:], in_=dt[:], func=AF.Ln, bias=1.0), r=[], w=[dt.res])
            V(lambda: nc.vector.tensor_tensor(out=dta[:], in0=dt[:], in1=a_bc[:], op=ALU.mult), r=[dt.res, a_bc.res],
              w=[dta.res])
            p = PF()
            mm(p, p[:, 0:16], [(K["U_f"][:], dta[:])], [K["U_f"].res, dta.res])
            mm(p, p[:, 16:32], [(K["ones_f"][:], dta[:])], [K["ones_f"].res, dta.res], first=False)
            V(lambda: nc.vector.tensor_copy(out=acum[:], in_=p[:, 0:16]), r=[p.res], w=[acum.res])
            V(lambda: nc.vector.tensor_scalar(out=nacum[:], in0=p[:, 0:16], scalar1=-1.0, scalar2=None, op0=ALU.mult),
              r=[p.res], w=[nacum.res])
            A(lambda: nc.scalar.activation(out=ea[:], in_=p[:, 0:16], func=AF.Exp), r=[p.res], w=[ea.res])
            A(lambda: nc.scalar.activation(out=cdec[:], in_=p[:, 16:32], func=AF.Exp), r=[p.res], w=[cdec.res])
            V(lambda: nc.vector.tensor_tensor(out=toend[:], in0=p[:, 16:32], in1=acum[:], op=ALU.subtract),
              r=[p.res, acum.res], w=[toend.res])
            A(lambda: nc.scalar.activation(out=toend[:], in_=toend[:], func=AF.Exp), r=[], w=[toend.res])
            p = PF()
            tr(p, p[0:16, 0:128], acum[:], [acum.res], bf=False)
            V(lambda: nc.vector.tensor_copy(out=acT[:], in_=p[0:16, 0:128]), r=[p.res], w=[acT.res])
            pbt = PB()
            for f in range(8):
                tr(pbt, pbt[:, f * 128:(f + 1) * 128], xbcT[:, f, cs], [xbcT.res], first=(f == 0))
            A(lambda: nc.scalar.copy(out=xs_tm[:], in_=pbt[:]), r=[pbt.res], w=[xs_tm.res])
            pbt = PB()
            for g in range(2):
                tr(pbt, pbt[:, g * 128:(g + 1) * 128], xbcT[:, 8 + g, cs], [xbcT.res], first=(g == 0))
            A(lambda: nc.scalar.copy(out=B_tm[:], in_=pbt[:, 0:256]), r=[pbt.res], w=[B_tm.res])
            V(lambda: nc.vector.tensor_tensor(out=X[:].rearrange("p (h d) -> p h d", d=64),
                                              in0=xs_tm[:].rearrange("p (h d) -> p h d", d=64),
                                              in1=dt[:].unsqueeze(2).to_broadcast([128, 16, 64]), op=ALU.mult),
              r=[xs_tm.res, dt.res], w=[X.res])
            V(lambda: nc.vector.tensor_tensor(out=Xw[:].rearrange("p (h d) -> p h d", d=64),
                                              in0=X[:].rearrange("p (h d) -> p h d", d=64),
                                              in1=toend[:].unsqueeze(2).to_broadcast([128, 16, 64]), op=ALU.mult),
              r=[X.res, toend.res], w=[Xw.res])
            p = PF()
            for g in range(2):
                mm(p, p[:, g * 128:(g + 1) * 128], [(xbcT[:, 8 + g, cs], xbcT[:, 10 + g, cs])], [xbcT.res],
                   first=(g == 0))
            V(lambda: nc.vector.tensor_copy(out=cbT[:], in_=p[:, 0:256].rearrange("p (g t) -> p g t", g=2)), r=[p.res],
              w=[cbT.res])
            pyo = [PF(0), PF(1)]
            for g in range(2):
                mm(pyo[g], pyo[g][:], [(xbcT[:, 10 + g, cs], STb[:, g * 512:(g + 1) * 512])], [xbcT.res, STb.res])
            pyd = [PF(2), PF(3)]
            for h in range(16):
                g = h // 8
                pd = PF(4 + h % 2)
                mm(pd, pd[:, 0:128], [(K["Esel"][:, h, :], acT[:]), (K["ident_f"][:], K["negm"][:])],
                   [K["Esel"].res, acT.res, K["ident_f"].res, K["negm"].res])
                dT = decT[h % 2]
                A(lambda: nc.scalar.activation(out=dT[:], in_=pd[:, 0:128], func=AF.Exp, bias=nacum[:, h:h + 1]),
                  r=[pd.res, nacum.res], w=[dT.res])
                mt = MT[h % 4]
                V(lambda: nc.vector.tensor_tensor(out=mt[:], in0=dT[:], in1=cbT[:, g, :], op=ALU.mult),
                  r=[dT.res, cbT.res], w=[mt.res])
                pump(4)
                hh = h % 8
                mm(pyd[g], pyd[g][:, hh * 64:(hh + 1) * 64], [(mt[:], X[:, h * 64:(h + 1) * 64])], [mt.res, X.res],
                   first=(hh == 0))
            for g in range(2):
                gs = slice(g * 512, (g + 1) * 512)
                V(lambda: nc.vector.tensor_tensor(out=yv[:, gs].rearrange("p (h d) -> p h d", d=64),
                                                  in0=pyo[g][:].rearrange("p (h d) -> p h d", d=64),
                                                  in1=ea[:, g * 8:(g + 1) * 8].unsqueeze(2).to_broadcast([128, 8, 64]),
                                                  op=ALU.mult), r=[pyo[g].res, ea.res], w=[yv.res])
                V(lambda: nc.vector.tensor_tensor(out=yv[:, gs], in0=yv[:, gs], in1=pyd[g][:], op=ALU.add),
                  r=[pyd[g].res], w=[yv.res])
                t_ = tmp512[g]
                G(lambda: nc.gpsimd.tensor_tensor(out=t_[:].rearrange("p (h d) -> p h d", d=64),
                                                  in0=xs_tm[:, gs].rearrange("p (h d) -> p h d", d=64),
                                                  in1=d_bc[:, g * 8:(g + 1) * 8].unsqueeze(2).to_broadcast([128, 8, 64]),
                                                  op=ALU.mult), r=[xs_tm.res, d_bc.res], w=[t_.res])
                V(lambda: nc.vector.tensor_tensor(out=yv[:, gs], in0=yv[:, gs], in1=t_[:], op=ALU.add), r=[t_.res],
                  w=[yv.res])
                V(lambda: nc.vector.tensor_tensor(out=yv[:, gs], in0=yv[:, gs], in1=za[:, gs], op=ALU.mult),
                  r=[za.res], w=[yv.res])
                A(lambda: nc.scalar.activation(out=ysq[:], in_=yv[:, gs], func=AF.Square, accum_out=rms[:, g:g + 1]),
                  r=[yv.res], w=[ysq.res, rms.res])
            V(lambda: nc.vector.tensor_scalar(out=rms[:, 0:2], in0=rms[:, 0:2], scalar1=1.0 / 512, scalar2=EPS,
                                              op0=ALU.mult, op1=ALU.add), r=[], w=[rms.res])
            A(lambda: nc.scalar.activation(out=rms[:, 0:2], in_=rms[:, 0:2], func=AF.Sqrt), r=[], w=[rms.res])
            V(lambda: nc.vector.reciprocal(out=rms[:, 2:4], in_=rms[:, 0:2]), r=[], w=[rms.res])
            for g in range(2):
                gs = slice(g * 512, (g + 1) * 512)
                V(lambda: nc.vector.scalar_tensor_tensor(out=yab[:, gs], in0=yv[:, gs], scalar=rms[:, 2 + g:3 + g],
                                                         in1=ng_bc[:, gs], op0=ALU.mult, op1=ALU.mult),
                  r=[yv.res, rms.res, ng_bc.res], w=[yab.res])
            pbt = PB()
            for f in range(8):
                tr(pbt, pbt[:, f * 128:(f + 1) * 128], yab[:, f * 128:(f + 1) * 128], [yab.res], first=(f == 0))
            ya_ = yaT[cc % 2]
            A(lambda: nc.scalar.copy(out=ya_[:], in_=pbt[:].rearrange("p (k t) -> p k t", k=8)), r=[pbt.res],
              w=[ya_.res])
            S.dma("pool", lambda q: q.dma_start(out=ycv[:, 0:8, m * 512 + cc * 128:m * 512 + (cc + 1) * 128], in_=ya_[:]),
                  reads=[ya_.res], writes=[C.R_ycat[4 * m + cc]])
            for g in range(2):
                gs = slice(g * 512, (g + 1) * 512)
                p = PF()
                mm(p, p[:], [(B_tm[:, g * 128:(g + 1) * 128], Xw[:, gs])], [B_tm.res, Xw.res])
                V(lambda: nc.vector.tensor_tensor(out=ST[:, gs].rearrange("p (h d) -> p h d", d=64),
                                                  in0=ST[:, gs].rearrange("p (h d) -> p h d", d=64),
                                                  in1=cdec[:, g * 8:(g + 1) * 8].unsqueeze(2).to_broadcast([128, 8, 64]),
                                                  op=ALU.mult), r=[cdec.res], w=[ST.res])
                V(lambda: nc.vector.tensor_tensor(out=ST[:, gs], in0=ST[:, gs], in1=p[:], op=ALU.add), r=[p.res],
                  w=[ST.res])
            A(lambda: nc.scalar.copy(out=STb[:], in_=ST[:]), r=[ST.res], w=[STb.res])
        pump(10 ** 6)
        for c in range(8):
            cq_ = cvq[c % 2]
            A(lambda: nc.scalar.activation(out=cq_[:], in_=cvf[:, c, :], func=AF.Square), r=[cvf.res],
              w=[cq_.res])
            mm(p1, p1[:], [(K["ones_bf"][:], cvf[:, c, :])], [cvf.res, K["ones_bf"].res], first=(c == 0),
               start=(c == 0), stop=(c == 7))
            mm(p2, p2[:], [(K["ones_bf"][:], cq_[:])], [cq_.res, K["ones_bf"].res], first=(c == 0),
               start=(c == 0), stop=(c == 7))
        G(lambda: nc.gpsimd.tensor_copy(out=uroll[:, :, 0:30], in_=uroll[:, :, 512:542]), r=[], w=[uroll.res])

        t0, t1 = tmp512
        A(lambda: nc.scalar.activation(out=mean_bc[:], in_=p1[:], func=AF.Copy, scale=1.0 / 1024), r=[p1.res], w=[mean_bc.res])
        V(lambda: nc.vector.tensor_tensor(out=t0[:], in0=mean_bc[:], in1=mean_bc[:], op=ALU.mult), r=[mean_bc.res],
          w=[t0.res])
        V(lambda: nc.vector.scalar_tensor_tensor(out=t1[:], in0=p2[:], scalar=1.0 / 1024, in1=t0[:], op0=ALU.mult,
                                                 op1=ALU.subtract), r=[p2.res, t0.res], w=[t1.res])
        V(lambda: nc.vector.tensor_scalar(out=t1[:], in0=t1[:], scalar1=EPS, scalar2=None, op0=ALU.add), r=[],
          w=[t1.res])
        A(lambda: nc.scalar.activation(out=t1[:], in_=t1[:], func=AF.Sqrt), r=[], w=[t1.res])
        V(lambda: nc.vector.reciprocal(out=rstd_bc[:], in_=t1[:]), r=[t1.res], w=[rstd_bc.res])
        for c in range(8 if STAGE >= 2.6 else 0):
            a_ = acc[c % 2]
            V(lambda: nc.vector.tensor_tensor(out=a_[:], in0=cvf[:, c, :], in1=mean_bc[:], op=ALU.subtract),
              r=[cvf.res, mean_bc.res], w=[a_.res])
            G(lambda: nc.gpsimd.tensor_tensor(out=a_[:], in0=a_[:], in1=rstd_bc[:], op=ALU.mult), r=[rstd_bc.res],
              w=[a_.res])
            A(lambda: nc.scalar.activation(out=a_[:], in_=a_[:], func=AF.Silu, bias=cfb[:, c:c + 1],
                                           scale=cfg[:, c:c + 1]), r=[cfg.res, cfb.res], w=[a_.res])
            pz = fproj(4624 + c * 128)
            zb_, yb_ = zb[c % 2], ybT[c % 2]
            A(lambda: nc.scalar.activation(out=zb_[:], in_=pz[:], func=AF.Silu), r=[pz.res], w=[zb_.res])
            G(lambda: nc.gpsimd.tensor_tensor(out=yb_[:], in0=a_[:], in1=zb_[:], op=ALU.mult),
              r=[a_.res, zb_.res], w=[yb_.res])
            S.dma("pool", lambda q: q.dma_start(out=ycv[:, 8 + c, m * 512:(m + 1) * 512], in_=yb_[:]),
                  reads=[yb_.res], writes=[])

    S.barrier()
    esw.close()
    es = esl
    if "outs" in SKIP:
        return
    sT_out = sb(es, "sT_out", [128, 8, 128])
    sc_o = sb(es, "sc_o", [3, 1536])
    cc_o = sb(es, "cc_o", [30, 1024])
    for f in range(8):
        p = PF()
        tr(p, p[:, 0:128], ST[:, f * 128:(f + 1) * 128], [ST.res], bf=False)
        V(lambda: nc.vector.tensor_copy(out=sT_out[:, f, :], in_=p[:, 0:128]), r=[p.res], w=[sT_out.res])
    S.dma("sp", lambda q: q.dma_start(out=O["ssm_p%d" % j].rearrange("(f p) n -> p f n", p=128), in_=sT_out[:]),
          reads=[sT_out.res])
    for c in range(12):
        p = PF()
        tr(p, p[0:3, 0:128], sraw[:, c, :], [sraw.res], bf=False)
        V(lambda: nc.vector.tensor_copy(out=sc_o[0:3, c * 128:(c + 1) * 128], in_=p[0:3, 0:128]), r=[p.res],
          w=[sc_o.res])
    for c in range(8):
        p = PF()
        tr(p, p[0:30, 0:128], uraw[:, c, :], [uraw.res], bf=False)
        V(lambda: nc.vector.tensor_copy(out=cc_o[0:30, c * 128:(c + 1) * 128], in_=p[0:30, 0:128]), r=[p.res],
          w=[cc_o.res])
    S.dma("sp", lambda q: q.dma_start(out=O["sc_p%d" % j], in_=sc_o[0:3, :]), reads=[sc_o.res])
    S.dma("sp", lambda q: q.dma_start(out=O["cc_p%d" % j], in_=cc_o[0:30, :]), reads=[cc_o.res])


def odd_pass_a(C, esl, j):
    nc, S, sb, PF, PB, mm, tr, V, A, G, I, O = C.nc, C.S, C.sb, C.PF, C.PB, C.mm, C.tr, C.V, C.A, C.G, C.I, C.O
    L, NTI, NMT = C.L, C.NTI, C.NMT
    es = esl
    w_in = sb(es, "w_in_o", [128, 8, ODD_IN], BF16)
    C.load_weight_bf16(esl, I["w_in_odd"][j], D, ODD_IN, w_in, "wio")
    if C.do_sample:
        sample_odd(C, j, w_in)
    if not C.do_prompt:
        return
    xt_mt = [sb(es, "oxt_mt%d" % i, [128, 8, 512], BF16) for i in range(2)]
    stg = [sb(es, "ostg%d" % i, [128, 512], BF16) for i in range(4)]
    kvf = [sb(es, "kvf%d" % i, [128, 512]) for i in range(2)]
    kif = [sb(es, "kif%d" % i, [128, 72]) for i in range(2)]
    v1 = [sb(es, "v1_%d" % i, [128, 4, 65], BF16) for i in range(2)]
    zs = [sb(es, "zs%d" % i, [128, 1024], BF16) for i in range(2)]
    for t in v1:
        V(lambda: nc.vector.memset(t[:], 1.0), w=[t.res])
    xTv = C.xT.rearrange("(k p) t -> p k t", p=128)
    WSC = (8.0 ** -0.5) * (64.0 ** -0.5)
    si = 0
    for m in range(NMT):
        xt = xt_mt[m % 2]
        ts = slice(m * 512, (m + 1) * 512)
        S.dma("sp", lambda q: q.dma_start(out=xt[:], in_=xTv[:, :, ts]),
              reads=[C.R_xT[4 * m + i] for i in range(4)], writes=[xt.res])

        def fproj(col0, M=128):
            p = PF()
            mm(p, p[0:M, :], [(w_in[:, k, col0:col0 + M], xt[:, k, :]) for k in range(8)], [w_in.res, xt.res])
            return p

        def evac(p, M=128):
            nonlocal si
            s_ = stg[si % 4]
            si += 1
            if si % 2 == 0:
                A(lambda: nc.scalar.copy(out=s_[0:M, :], in_=p[0:M, :]), r=[p.res], w=[s_.res])
            else:
                V(lambda: nc.vector.tensor_copy(out=s_[0:M, :], in_=p[0:M, :]), r=[p.res], w=[s_.res])
            return s_

        for c in range(8):
            s_ = evac(fproj(c * 128))
            for e in range(2):
                h = 2 * c + e
                S.dma("pool", lambda q: q.dma_start(out=C.qT_d[(h // 8) * 64:(h // 8) * 64 + 64, h % 8, ts],
                                                    in_=s_[e * 64:(e + 1) * 64, :]), reads=[s_.res], writes=[])
        for c in range(2):
            s_ = evac(fproj(1024 + c * 128))
            for e in range(2):
                S.dma("pool", lambda q: q.dma_start(out=C.kT_d[c * 64:(c + 1) * 64, e, ts], in_=s_[e * 64:(e + 1) * 64, :]),
                      reads=[s_.res], writes=[])
        for c in range(4):
            s_ = evac(fproj(1536 + c * 128))
            for e in range(2):
                S.dma("pool", lambda q: q.dma_start(out=C.qiT_d[:, 2 * c + e, ts], in_=s_[e * 64:(e + 1) * 64, :]),
                      reads=[s_.res], writes=[])
        s_ = evac(fproj(2048, 64), 64)
        S.dma("pool", lambda q: q.dma_start(out=C.kiT_d[:, ts], in_=s_[0:64, :]), reads=[s_.res], writes=[])
        for cc in range(4):
            cs = slice(cc * 128, (cc + 1) * 128)
            t0 = m * 512 + cc * 128
            rows = slice(t0, t0 + 128)
            b = cc % 2
            p = PF()
            mm(p, p[:], [(xt[:, k, cs], w_in[:, k, 1024:1536]) for k in range(8)], [w_in.res, xt.res])
            V(lambda: nc.vector.tensor_copy(out=kvf[b][:], in_=p[:]), r=[p.res], w=[kvf[b].res])
            A(lambda: nc.scalar.copy(out=v1[b][:, :, 0:64], in_=p[:, 256:512].rearrange("p (g d) -> p g d", d=64)),
              r=[p.res], w=[v1[b].res])
            S.dma("sp", lambda q: q.dma_start(out=O["k_p%d" % j][rows, :], in_=kvf[b][:, 0:256]), reads=[kvf[b].res])
            S.dma("sp", lambda q: q.dma_start(out=O["v_p%d" % j][rows, :], in_=kvf[b][:, 256:512]), reads=[kvf[b].res])
            S.dma("pool", lambda q: q.dma_start(out=C.V1_d[rows, :], in_=v1[b][:].rearrange("p g d -> p (g d)")),
                  reads=[v1[b].res], writes=[])
            p = PF()
            mm(p, p[:, 0:72], [(xt[:, k, cs], w_in[:, k, 2048:2120]) for k in range(8)], [w_in.res, xt.res])
            V(lambda: nc.vector.tensor_copy(out=kif[b][:, 0:64], in_=p[:, 0:64]), r=[p.res], w=[kif[b].res])
            V(lambda: nc.vector.tensor_scalar(out=kif[b][:, 64:72], in0=p[:, 64:72], scalar1=WSC, scalar2=None,
                                              op0=ALU.mult), r=[p.res], w=[kif[b].res])
            S.dma("sp", lambda q: q.dma_start(out=O["ki_p%d" % j][rows, :], in_=kif[b][:, 0:64]), reads=[kif[b].res])
            S.dma("pool", lambda q: q.dma_start(out=C.wi_d[rows, :], in_=kif[b][:, 64:72]), reads=[kif[b].res],
                  writes=[])
            for hf in range(2):
                p = PF()
                mm(p, p[:], [(xt[:, k, cs], w_in[:, k, 2120 + hf * 512:2120 + (hf + 1) * 512]) for k in range(8)],
                   [w_in.res, xt.res])
                A(lambda: nc.scalar.activation(out=zs[b][:, hf * 512:(hf + 1) * 512], in_=p[:], func=AF.Silu),
                  r=[p.res], w=[zs[b].res])
            S.dma("pool", lambda q: q.dma_start(out=C.zs_d[rows, :], in_=zs[b][:]), reads=[zs[b].res], writes=[])


def odd_pass_b(C, esl, j):
    nc, S, sb, PF, PB, mm, tr, V, A, G, I, O = C.nc, C.S, C.sb, C.PF, C.PB, C.mm, C.tr, C.V, C.A, C.G, C.I, C.O
    K = C.consts
    L, NTI, NMT = C.L, C.NTI, C.NMT
    es = esl
    TOPK = min(256, L // 4)
    NIT = 22
    kiT = sb(es, "kiT", [64, L], BF16)
    kT2 = sb(es, "kT2", [128, 2, L], BF16)
    V1 = sb(es, "V1", [128, NTI, 260], BF16)
    Isc = sb(es, "Isc", [128, L])
    msk = sb(es, "msk", [128, L], BF16)
    nmT = sb(es, "nmT", [128, NTI, 128], BF16)
    negc = sb(es, "negc", [128, 128])
    qT = [sb(es, "qT%d" % i, [128, 8, 128], BF16) for i in range(2)]
    qiT = [sb(es, "qiT%d" % i, [64, 8, 128], BF16) for i in range(2)]
    wi = [sb(es, "wi%d" % i, [128, 8]) for i in range(2)]
    zt = [sb(es, "zt%d" % i, [128, 1024], BF16) for i in range(2)]
    rl = [sb(es, "rl%d" % i, [128, 512]) for i in range(3)]
    E = [sb(es, "E%d" % i, [128, 4, 128], BF16) for i in range(3)]
    bs = sb(es, "bs", [128, 8])
    og = sb(es, "og", [128, 1024])
    ogb = sb(es, "ogb", [128, 1024], BF16)
    ogT = [sb(es, "ogT%d" % i, [128, 8, 128], BF16) for i in range(2)]
    rden = sb(es, "rden", [128, 4])
    G(lambda: nc.gpsimd.affine_select(out=negc[:], in_=K["zeros_f"][:], pattern=[[-1, 128]], compare_op=ALU.is_ge,
                                      fill=-1.0e30, base=0, channel_multiplier=1), r=[K["zeros_f"].res], w=[negc.res])
    S.dma("sp", lambda q: q.dma_start(out=kiT[:], in_=C.kiT_d[:, :]), reads=[C.R_odd], writes=[kiT.res])
    for h2 in range(2):
        S.dma("sp", lambda q: q.dma_start(out=kT2[:, h2, :], in_=C.kT_d[:, h2, :]), reads=[C.R_odd], writes=[kT2.res])
    S.dma("pool", lambda q: q.dma_start(out=V1[:], in_=C.V1_d.rearrange("(n p) c -> p n c", p=128)), reads=[C.R_odd],
          writes=[V1.res])
    ycv = C.ycat.rearrange("(k p) t -> p k t", p=128)
    def st_load(i):
        b = i % 2
        n = (i + 1) * 128
        ts = slice(i * 128, (i + 1) * 128)
        S.dma("sp", lambda q: q.dma_start(out=qT[b][:], in_=C.qT_d[:, :, ts]), reads=[C.R_odd], writes=[qT[b].res])
        S.dma("sp", lambda q: q.dma_start(out=qiT[b][:], in_=C.qiT_d[:, :, ts]), reads=[C.R_odd], writes=[qiT[b].res])
        S.dma("sp", lambda q: q.dma_start(out=wi[b][:], in_=C.wi_d[ts, :]), reads=[C.R_odd], writes=[wi[b].res])
        S.dma("sp", lambda q: q.dma_start(out=zt[b][:], in_=C.zs_d[ts, :]), reads=[C.R_odd], writes=[zt[b].res])

    def st_a1(i):
        b = i % 2
        n = (i + 1) * 128
        ts = slice(i * 128, (i + 1) * 128)
        nkb = (n + 511) // 512
        ri = 0
        for kb in range(nkb):
            w_ = min(512, n - kb * 512)
            ks = slice(kb * 512, kb * 512 + w_)
            for h in range(8):
                p = PF(h % 4)
                mm(p, p[:, 0:w_], [(qiT[b][:, h, :], kiT[:, ks])], [qiT[b].res, kiT.res])
                r_ = rl[ri % 3]
                ri += 1
                A(lambda: nc.scalar.activation(out=r_[:, 0:w_], in_=p[:, 0:w_], func=AF.Relu), r=[p.res], w=[r_.res])
                if h == 0:
                    V(lambda: nc.vector.tensor_scalar(out=Isc[:, ks], in0=r_[:, 0:w_], scalar1=wi[b][:, 0:1], scalar2=None,
                                                      op0=ALU.mult), r=[r_.res, wi[b].res], w=[Isc.res])
                else:
                    V(lambda: nc.vector.scalar_tensor_tensor(out=Isc[:, ks], in0=r_[:, 0:w_], scalar=wi[b][:, h:h + 1],
                                                             in1=Isc[:, ks], op0=ALU.mult, op1=ALU.add),
                      r=[r_.res, wi[b].res], w=[Isc.res])
        V(lambda: nc.vector.tensor_tensor(out=Isc[:, i * 128:n], in0=Isc[:, i * 128:n], in1=negc[:], op=ALU.add),
          r=[negc.res], w=[Isc.res])
        if n > TOPK:
            lo, hi, mid, cnt, ge, tmp = [bs[:, c:c + 1] for c in range(6)]
            V(lambda: nc.vector.tensor_reduce(out=lo, in_=Isc[:, 0:i * 128], axis=AX.X, op=ALU.min), r=[Isc.res],
              w=[bs.res])
            V(lambda: nc.vector.tensor_reduce(out=hi, in_=Isc[:, 0:n], axis=AX.X, op=ALU.max), r=[Isc.res], w=[bs.res])
            V(lambda: nc.vector.tensor_scalar(out=hi, in0=hi, scalar1=1.0, scalar2=None, op0=ALU.add), r=[], w=[bs.res])
            for it in range(NIT):
                V(lambda: nc.vector.tensor_tensor(out=mid, in0=lo, in1=hi, op=ALU.add), r=[], w=[bs.res])
                V(lambda: nc.vector.tensor_scalar(out=mid, in0=mid, scalar1=0.5, scalar2=None, op0=ALU.mult), r=[],
                  w=[bs.res])
                V(lambda: nc.vector.tensor_scalar(out=msk[:, 0:n], in0=Isc[:, 0:n], scalar1=mid, scalar2=0.0,
                                                  op0=ALU.is_ge, op1=ALU.add, accum_out=cnt), r=[Isc.res],
                  w=[bs.res, msk.res])
                V(lambda: nc.vector.tensor_scalar(out=ge, in0=cnt, scalar1=float(TOPK), scalar2=None, op0=ALU.is_ge),
                  r=[], w=[bs.res])
                V(lambda: nc.vector.tensor_tensor(out=tmp, in0=mid, in1=lo, op=ALU.subtract), r=[], w=[bs.res])
                V(lambda: nc.vector.scalar_tensor_tensor(out=lo, in0=tmp, scalar=ge, in1=lo, op0=ALU.mult, op1=ALU.add),
                  r=[], w=[bs.res])
                V(lambda: nc.vector.tensor_tensor(out=tmp, in0=hi, in1=mid, op=ALU.subtract), r=[], w=[bs.res])
                V(lambda: nc.vector.scalar_tensor_tensor(out=hi, in0=tmp, scalar=ge, in1=mid, op0=ALU.mult, op1=ALU.add),
                  r=[], w=[bs.res])
            V(lambda: nc.vector.tensor_scalar(out=msk[:, 0:n], in0=Isc[:, 0:n], scalar1=lo, scalar2=None, op0=ALU.is_ge),
              r=[Isc.res, bs.res], w=[msk.res])
        else:
            V(lambda: nc.vector.tensor_scalar(out=msk[:, 0:n], in0=Isc[:, 0:n], scalar1=-1.0e29, scalar2=None,
                                              op0=ALU.is_ge), r=[Isc.res], w=[msk.res])

    def st_a2(i):
        b = i % 2
        n = (i + 1) * 128
        ts = slice(i * 128, (i + 1) * 128)
        for b0 in range(0, i + 1, 8):
            nb = min(8, i + 1 - b0)
            pt = PB()
            for bb in range(nb):
                tr(pt, pt[:, bb * 128:(bb + 1) * 128], msk[:, (b0 + bb) * 128:(b0 + bb + 1) * 128], [msk.res],
                   first=(bb == 0))
            V(lambda: nc.vector.tensor_scalar(out=nmT[:, b0:b0 + nb, :],
                                              in0=pt[:, 0:nb * 128].rearrange("p (k t) -> p k t", t=128),
                                              scalar1=-1.0, scalar2=30000.0, op0=ALU.add, op1=ALU.mult), r=[pt.res],
              w=[nmT.res])

    def st_b(i):
        b = i % 2
        n = (i + 1) * 128
        ts = slice(i * 128, (i + 1) * 128)
        ei = 0
        for g in range(4):
            half = g // 2
            ps_ = slice(half * 64, half * 64 + 64)
            po = PF(2 + g)
            for sb_ in range(i + 1):
                ss = slice(sb_ * 128, (sb_ + 1) * 128)
                pl = PF(sb_ % 2)
                mm(pl, pl[:], [(kT2[ps_, g % 2, ss], qT[b][ps_, (g % 2) * 4:(g % 2) * 4 + 4, :]),
                               (K["ident_bf"][:], nmT[:, sb_, :].unsqueeze(1).to_broadcast([128, 4, 128]))],
                   [kT2.res, qT[b].res, K["ident_bf"].res, nmT.res])
                e_ = E[ei % 3]
                ei += 1
                A(lambda: nc.scalar.activation(out=e_[:], in_=pl[:].rearrange("p (r t) -> p r t", r=4), func=AF.Exp,
                                               scale=0.125), r=[pl.res], w=[e_.res])
                for r in range(4):
                    mm(po, po[:, r * 65:(r + 1) * 65], [(e_[:, r, :], V1[:, sb_, g * 65:(g + 1) * 65])],
                       [e_.res, V1.res], first=(sb_ == 0 and r == 0), start=(sb_ == 0 and r == 0), stop=(sb_ == i),
                       sgc=True)
            pov = po[:, 0:260].rearrange("p (r d) -> p r d", d=65)
            V(lambda: nc.vector.reciprocal(out=rden[:].unsqueeze(2), in_=pov[:, :, 64:65]), r=[po.res], w=[rden.res])
            V(lambda: nc.vector.tensor_tensor(out=og[:, g * 256:(g + 1) * 256].rearrange("p (r d) -> p r d", d=64),
                                              in0=pov[:, :, 0:64], in1=rden[:].unsqueeze(2).to_broadcast([128, 4, 64]),
                                              op=ALU.mult), r=[po.res, rden.res], w=[og.res])
        G(lambda: nc.gpsimd.tensor_tensor(out=ogb[:], in0=og[:], in1=zt[b][:], op=ALU.mult), r=[og.res, zt[b].res],
          w=[ogb.res])
        pt = PB()
        for f in range(8):
            tr(pt, pt[:, f * 128:(f + 1) * 128], ogb[:, f * 128:(f + 1) * 128], [ogb.res], first=(f == 0))
        A(lambda: nc.scalar.copy(out=ogT[b][:], in_=pt[:].rearrange("p (k t) -> p k t", k=8)), r=[pt.res], w=[ogT[b].res])
        S.dma("pool", lambda q: q.dma_start(out=ycv[:, 0:8, ts], in_=ogT[b][:]), reads=[ogT[b].res],
              writes=[C.R_ycat[i]])

    st_load(0)
    st_a1(0)
    st_a2(0)
    for i in range(NTI):
        if i + 1 < NTI:
            st_load(i + 1)
            st_a1(i + 1)
        st_b(i)
        if i + 1 < NTI:
            st_a2(i + 1)


def _unpack(C):
    return C.nc, C.S, C.sb, C.PF, C.PB, C.mm, C.tr, C.V, C.A, C.G, C.I, C.O


def sample_xT(C):
    nc, S, sb, PF, PB, mm, tr, V, A, G, I, O = _unpack(C)
    p = PF()
    for k in range(8):
        tr(p, p[:, k * 4:(k + 1) * 4], C.xs_s[0:4, k * 128:(k + 1) * 128], [C.xs_s.res], bf=False, first=(k == 0))
    A(lambda: nc.scalar.copy(out=C.xsT[:].rearrange("p k s -> p (k s)"), in_=p[:, 0:32]), r=[p.res], w=[C.xsT.res])


def sample_out(C, l, KT, w_out_sb, lng, lnb, is_last):
    nc, S, sb, PF, PB, mm, tr, V, A, G, I, O = _unpack(C)
    xs_s, ycT = C.xs_s, C.ycT
    with contextlib.ExitStack() as es:
        hb = sb(es, "so_hb", [4, D])
        st = sb(es, "so_st", [4, 2, 6])
        mv = sb(es, "so_mv", [4, 4])
        for hf in range(2):
            hs = slice(hf * 512, (hf + 1) * 512)
            p = PF()
            mm(p, p[0:4, :], [(ycT[:, k, :], w_out_sb[:, k, hs]) for k in range(KT)], [ycT.res, w_out_sb.res])
            V(lambda: nc.vector.scalar_tensor_tensor(out=hb[:, hs], in0=xs_s[:, hs], scalar=ALPHA, in1=p[0:4, :],
                                                     op0=ALU.mult, op1=ALU.add), r=[xs_s.res, p.res], w=[hb.res])
        for hf in range(2):
            V(lambda: nc.vector.bn_stats(out=st[:, hf, :], in_=hb[:, hf * 512:(hf + 1) * 512]), r=[hb.res], w=[st.res])
        V(lambda: nc.vector.bn_aggr(out=mv[:, 0:2], in_=st[:].rearrange("p a b -> p (a b)")), r=[st.res], w=[mv.res])
        V(lambda: nc.vector.tensor_scalar(out=mv[:, 2:3], in0=mv[:, 1:2], scalar1=EPS, scalar2=None, op0=ALU.add),
          r=[], w=[mv.res])
        A(lambda: nc.scalar.activation(out=mv[:, 2:3], in_=mv[:, 2:3], func=AF.Sqrt), r=[], w=[mv.res])
        V(lambda: nc.vector.reciprocal(out=mv[:, 3:4], in_=mv[:, 2:3]), r=[], w=[mv.res])
        V(lambda: nc.vector.tensor_scalar(out=xs_s[:], in0=hb[:], scalar1=mv[:, 0:1], scalar2=mv[:, 3:4],
                                          op0=ALU.subtract, op1=ALU.mult), r=[hb.res, mv.res], w=[xs_s.res])
        V(lambda: nc.vector.tensor_tensor(out=xs_s[:], in0=xs_s[:], in1=lng[0:4, :], op=ALU.mult), r=[lng.res],
          w=[xs_s.res])
        V(lambda: nc.vector.tensor_tensor(out=xs_s[:], in0=xs_s[:], in1=lnb[0:4, :], op=ALU.add), r=[lnb.res],
          w=[xs_s.res])
        if is_last:
            S.dma("sp", lambda q: q.dma_start(out=O["y_s"][:, :], in_=xs_s[:]), reads=[xs_s.res])
        else:
            sample_xT(C)
        S.barrier()


def sample_even(C, j, w_in, P):
    nc, S, sb, PF, PB, mm, tr, V, A, G, I, O = _unpack(C)
    K = C.consts
    xsT, ycT = C.xsT, C.ycT
    cw4, cb4, cw31, cb31, cfg, cfb, ngT, hp16 = (P[k] for k in ("cw4", "cb4", "cw31", "cb31", "cfg", "cfb", "ngT",
                                                                "hp16"))
    ones_f, ident_f, Esel = K["ones_f"], K["ident_f"], K["Esel"]
    with contextlib.ExitStack() as es:
        Esel2 = sb(es, "Esel2", [16, 8, 128])
        G(lambda: nc.gpsimd.memset(Esel2[:], 1.0), w=[Esel2.res])
        G(lambda: nc.gpsimd.affine_select(out=Esel2[:].rearrange("p f (e d) -> p f e d", e=2),
                                          in_=Esel2[:].rearrange("p f (e d) -> p f e d", e=2),
                                          pattern=[[-2, 8], [-1, 2], [0, 64]], compare_op=ALU.is_equal, fill=0.0,
                                          base=0, channel_multiplier=1), r=[], w=[Esel2.res])
        zaT = sb(es, "se_zaT", [128, 8, 4])
        zbT = sb(es, "se_zbT", [128, 8, 4])
        sgT = sb(es, "se_sgT", [128, 8, 4])
        xeT = sb(es, "se_xeT", [128, 12, 4, 4])
        ueT = sb(es, "se_ueT", [128, 8, 4, 31])
        dtv = sb(es, "se_dtv", [16, 9])
        ex = sb(es, "se_ex", [128, 8, 9])
        sst = sb(es, "se_sst", [12, 1536])
        cst = sb(es, "se_cst", [120, 1024])
        cvt = sb(es, "se_cvt", [128, 12, 4, 4])
        cv = sb(es, "se_cv", [128, 12, 4])
        xbcT = sb(es, "se_xbcT", [128, 12, 4])
        uct = sb(es, "se_uct", [128, 8, 4, 31])
        cu = sb(es, "se_cu", [128, 2, 8, 4])
        st2 = sb(es, "se_st2", [128, 2, 4])
        msq = sb(es, "se_msq", [128, 4])
        rstd = sb(es, "se_rstd", [128, 4])
        vn = sb(es, "se_vn", [128, 8, 4])
        bcT = sb(es, "se_bcT", [16, 128])
        BCb = sb(es, "se_BCb", [128, 16, 128])
        st = sb(es, "se_st", [128, 4, 8, 128])
        xd = sb(es, "se_xd", [128, 4, 8])
        coef = sb(es, "se_coef", [128, 4, 8])
        tmp = sb(es, "se_tmp", [128, 8, 128])
        yT = sb(es, "se_yT", [128, 4, 8])
        yv = sb(es, "se_yv", [128, 2, 8, 4])
        rs = sb(es, "se_rs", [128, 2, 4])
        rowx = sb(es, "se_rowx", [4, 1536])
        rowu = sb(es, "se_rowu", [4, 1024])

        S.dma("sp", lambda q: q.dma_start(out=sst[:], in_=I["state_ssdconv%d" % j].rearrange("s k c -> (s k) c")),
              writes=[sst.res])
        S.dma("sp", lambda q: q.dma_start(out=cst[:], in_=I["state_cfconv%d" % j].rearrange("s k c -> (s k) c")),
              writes=[cst.res])
        for s in range(4):
            S.dma("sp" if s % 2 == 0 else "pool", lambda q: q.dma_start(
                out=st[:, s], in_=I["state_ssm%d" % j][s].rearrange("(f q) n -> q f n", q=128)), writes=[st.res])
        S.dma("sp", lambda q: q.dma_start(out=O["sc_s%d" % j][:, 0:2, :], in_=I["state_ssdconv%d" % j][:, 1:3, :]))
        S.dma("sp", lambda q: q.dma_start(out=O["cc_s%d" % j][:, 0:29, :], in_=I["state_cfconv%d" % j][:, 1:30, :]))

        pj = PF()

        def fp(col0, M, off, first=False):
            mm(pj, pj[0:M, off:off + 4], [(w_in[:, k, col0:col0 + M], xsT[:, k, :]) for k in range(8)],
               [w_in.res, xsT.res], first=first)

        for f in range(8):
            fp(f * 128, 128, f * 4, first=(f == 0))
        for t in range(12):
            fp(1024 + t * 128, 128, 32 + t * 4)
        for f in range(8):
            fp(2576 + f * 128, 128, 80 + f * 4)
        for f in range(8):
            fp(3600 + f * 128, 128, 112 + f * 4)
        for f in range(8):
            fp(4624 + f * 128, 128, 144 + f * 4)
        fp(2560, 16, 176)
        A(lambda: nc.scalar.activation(out=zaT[:].rearrange("p f s -> p (f s)"), in_=pj[:, 0:32], func=AF.Silu),
          r=[pj.res], w=[zaT.res])
        V(lambda: nc.vector.tensor_copy(out=xeT[:, :, :, 3], in_=pj[:, 32:80].rearrange("p (t s) -> p t s", s=4)),
          r=[pj.res], w=[xeT.res])
        A(lambda: nc.scalar.activation(out=sgT[:].rearrange("p f s -> p (f s)"), in_=pj[:, 112:144], func=AF.Sigmoid),
          r=[pj.res], w=[sgT.res])
        V(lambda: nc.vector.tensor_tensor(out=ueT[:, :, :, 30], in0=pj[:, 80:112].rearrange("p (f s) -> p f s", s=4),
                                          in1=sgT[:], op=ALU.mult), r=[pj.res, sgT.res], w=[ueT.res])
        A(lambda: nc.scalar.activation(out=zbT[:].rearrange("p f s -> p (f s)"), in_=pj[:, 144:176], func=AF.Silu),
          r=[pj.res], w=[zbT.res])
        V(lambda: nc.vector.tensor_scalar(out=dtv[:, 0:4], in0=pj[0:16, 176:180], scalar1=hp16[:, 0:1], scalar2=None,
                                          op0=ALU.add), r=[pj.res, hp16.res], w=[dtv.res])
        A(lambda: nc.scalar.activation(out=dtv[:, 0:4], in_=dtv[:, 0:4], func=AF.Exp), r=[], w=[dtv.res])
        A(lambda: nc.scalar.activation(out=dtv[:, 0:4], in_=dtv[:, 0:4], func=AF.Ln, bias=1.0), r=[], w=[dtv.res])
        A(lambda: nc.scalar.activation(out=hp16[:, 1:2], in_=hp16[:, 1:2], func=AF.Exp), r=[], w=[hp16.res])
        V(lambda: nc.vector.tensor_scalar(out=hp16[:, 1:2], in0=hp16[:, 1:2], scalar1=-1.0, scalar2=None, op0=ALU.mult),
          r=[], w=[hp16.res])
        V(lambda: nc.vector.tensor_scalar(out=dtv[:, 4:8], in0=dtv[:, 0:4], scalar1=hp16[:, 1:2], scalar2=None,
                                          op0=ALU.mult), r=[hp16.res], w=[dtv.res])
        A(lambda: nc.scalar.activation(out=dtv[:, 4:8], in_=dtv[:, 4:8], func=AF.Exp), r=[], w=[dtv.res])
        V(lambda: nc.vector.tensor_copy(out=dtv[:, 8:9], in_=hp16[:, 2:3]), r=[hp16.res], w=[dtv.res])
        pe_ = PF()
        for f in range(8):
            mm(pe_, pe_[:, f * 9:(f + 1) * 9], [(Esel2[:, f, :], dtv[:, 0:9])], [Esel2.res, dtv.res], first=(f == 0))
        V(lambda: nc.vector.tensor_copy(out=ex[:].rearrange("p f c -> p (f c)"), in_=pe_[:, 0:72]), r=[pe_.res],
          w=[ex.res])

        p1 = PF()
        for t in range(12):
            tr(p1, p1[:, t * 12:(t + 1) * 12], sst[0:12, t * 128:(t + 1) * 128], [sst.res], bf=False, first=(t == 0))
        V(lambda: nc.vector.tensor_copy(out=xeT[:, :, :, 0:3],
                                        in_=p1[:, 0:144].rearrange("p (t s k) -> p t s k", t=12, s=4, k=3)),
          r=[p1.res], w=[xeT.res])
        for half in range(2):
            p2 = PF()
            for ff in range(4):
                f = half * 4 + ff
                tr(p2, p2[:, ff * 120:(ff + 1) * 120], cst[0:120, f * 128:(f + 1) * 128], [cst.res], bf=False,
                   first=(ff == 0))
            V(lambda: nc.vector.tensor_copy(out=ueT[:, half * 4:half * 4 + 4, :, 0:30],
                                            in_=p2[:, 0:480].rearrange("p (f s k) -> p f s k", f=4, s=4, k=30)),
              r=[p2.res], w=[ueT.res])

        for b3 in range(3):
            p = PF()
            for tt in range(4):
                t = b3 * 4 + tt
                tr(p, p[0:4, tt * 128:(tt + 1) * 128], xeT[:, t, :, 3], [xeT.res], bf=False, first=(tt == 0))
            A(lambda: nc.scalar.copy(out=rowx[:, b3 * 512:(b3 + 1) * 512], in_=p[0:4, :]), r=[p.res], w=[rowx.res])
        for b2 in range(2):
            p = PF()
            for ff in range(4):
                f = b2 * 4 + ff
                tr(p, p[0:4, ff * 128:(ff + 1) * 128], ueT[:, f, :, 30], [ueT.res], bf=False, first=(ff == 0))
            A(lambda: nc.scalar.copy(out=rowu[:, b2 * 512:(b2 + 1) * 512], in_=p[0:4, :]), r=[p.res], w=[rowu.res])
        S.dma("sp", lambda q: q.dma_start(out=O["sc_s%d" % j][:, 2, :], in_=rowx[:]), reads=[rowx.res])
        S.dma("sp", lambda q: q.dma_start(out=O["cc_s%d" % j][:, 29, :], in_=rowu[:]), reads=[rowu.res])

        V(lambda: nc.vector.tensor_tensor(out=cvt[:], in0=xeT[:], in1=cw4[:].unsqueeze(2).to_broadcast([128, 12, 4, 4]),
                                          op=ALU.mult), r=[xeT.res, cw4.res], w=[cvt.res])
        V(lambda: nc.vector.tensor_reduce(out=cv[:], in_=cvt[:], axis=AX.X, op=ALU.add), r=[cvt.res], w=[cv.res])
        V(lambda: nc.vector.tensor_tensor(out=cv[:], in0=cv[:], in1=cb4[:].unsqueeze(2).to_broadcast([128, 12, 4]),
                                          op=ALU.add), r=[cb4.res], w=[cv.res])
        A(lambda: nc.scalar.activation(out=xbcT[:], in_=cv[:], func=AF.Silu), r=[cv.res], w=[xbcT.res])

        V(lambda: nc.vector.tensor_tensor(out=uct[:], in0=ueT[:], in1=cw31[:].unsqueeze(2).to_broadcast([128, 8, 4, 31]),
                                          op=ALU.mult), r=[ueT.res, cw31.res], w=[uct.res])
        V(lambda: nc.vector.tensor_reduce(out=cu[:, 0], in_=uct[:], axis=AX.X, op=ALU.add), r=[uct.res], w=[cu.res])
        V(lambda: nc.vector.tensor_tensor(out=cu[:, 0], in0=cu[:, 0], in1=cb31[:].unsqueeze(2).to_broadcast([128, 8, 4]),
                                          op=ALU.add), r=[cb31.res], w=[cu.res])
        V(lambda: nc.vector.tensor_tensor(out=cu[:, 1], in0=cu[:, 0], in1=cu[:, 0], op=ALU.mult), r=[], w=[cu.res])
        pl = PF()
        mm(pl, pl[:, 0:64], [(ones_f[:], cu[:].rearrange("p a f s -> p (a f s)"))], [ones_f.res, cu.res])
        V(lambda: nc.vector.tensor_reduce(out=st2[:], in_=pl[:, 0:64].rearrange("p (a f s) -> p a s f", a=2, f=8, s=4),
                                          axis=AX.X, op=ALU.add), r=[pl.res], w=[st2.res])
        V(lambda: nc.vector.tensor_scalar(out=st2[:], in0=st2[:], scalar1=1.0 / 1024, scalar2=None, op0=ALU.mult),
          r=[], w=[st2.res])
        V(lambda: nc.vector.tensor_tensor(out=msq[:], in0=st2[:, 0], in1=st2[:, 0], op=ALU.mult), r=[st2.res],
          w=[msq.res])
        V(lambda: nc.vector.tensor_tensor(out=msq[:], in0=st2[:, 1], in1=msq[:], op=ALU.subtract), r=[st2.res],
          w=[msq.res])
        V(lambda: nc.vector.tensor_scalar(out=msq[:], in0=msq[:], scalar1=EPS, scalar2=None, op0=ALU.add), r=[],
          w=[msq.res])
        A(lambda: nc.scalar.activation(out=msq[:], in_=msq[:], func=AF.Sqrt), r=[], w=[msq.res])
        V(lambda: nc.vector.reciprocal(out=rstd[:], in_=msq[:]), r=[msq.res], w=[rstd.res])
        V(lambda: nc.vector.tensor_tensor(out=vn[:], in0=cu[:, 0], in1=st2[:, 0].unsqueeze(1).to_broadcast([128, 8, 4]),
                                          op=ALU.subtract), r=[cu.res, st2.res], w=[vn.res])
        V(lambda: nc.vector.tensor_tensor(out=vn[:], in0=vn[:], in1=rstd[:].unsqueeze(1).to_broadcast([128, 8, 4]),
                                          op=ALU.mult), r=[rstd.res], w=[vn.res])
        V(lambda: nc.vector.tensor_tensor(out=vn[:], in0=vn[:], in1=cfg[:].unsqueeze(2).to_broadcast([128, 8, 4]),
                                          op=ALU.mult), r=[cfg.res], w=[vn.res])
        V(lambda: nc.vector.tensor_tensor(out=vn[:], in0=vn[:], in1=cfb[:].unsqueeze(2).to_broadcast([128, 8, 4]),
                                          op=ALU.add), r=[cfb.res], w=[vn.res])
        A(lambda: nc.scalar.activation(out=vn[:], in_=vn[:], func=AF.Silu), r=[], w=[vn.res])
        V(lambda: nc.vector.tensor_tensor(out=ycT[:, 8:16, :], in0=vn[:], in1=zbT[:], op=ALU.mult), r=[vn.res, zbT.res],
          w=[ycT.res])

        pT = PF()
        tr(pT, pT[0:16, 0:128], xbcT[:, 8:12, :].rearrange("p t s -> p (t s)"), [xbcT.res], bf=False)
        V(lambda: nc.vector.tensor_copy(out=bcT[:], in_=pT[0:16, 0:128]), r=[pT.res], w=[bcT.res])
        for i4 in range(4):
            pb_ = PF()
            for ii in range(4):
                i = i4 * 4 + ii
                mm(pb_, pb_[:, ii * 128:(ii + 1) * 128], [(Esel[0:16, i, :], bcT[:])], [Esel.res, bcT.res],
                   first=(ii == 0))
            if i4 % 2 == 0:
                A(lambda: nc.scalar.copy(out=BCb[:, i4 * 4:(i4 + 1) * 4, :].rearrange("p a n -> p (a n)"), in_=pb_[:]),
                  r=[pb_.res], w=[BCb.res])
            else:
                V(lambda: nc.vector.tensor_copy(out=BCb[:, i4 * 4:(i4 + 1) * 4, :].rearrange("p a n -> p (a n)"),
                                                in_=pb_[:]), r=[pb_.res], w=[BCb.res])

        V(lambda: nc.vector.tensor_tensor(out=xd[:].rearrange("p s f -> p f s"), in0=xbcT[:, 0:8, :], in1=ex[:, :, 0:4],
                                          op=ALU.mult), r=[xbcT.res, ex.res], w=[xd.res])
        V(lambda: nc.vector.tensor_copy(out=coef[:].rearrange("p s f -> p f s"), in_=ex[:, :, 4:8]), r=[ex.res],
          w=[coef.res])
        for s in range(4):
            V(lambda: nc.vector.tensor_tensor(out=st[:, s], in0=st[:, s],
                                              in1=coef[:, s, :].unsqueeze(2).to_broadcast([128, 8, 128]), op=ALU.mult),
              r=[coef.res], w=[st.res])
            for g in range(2):
                fs = slice(4 * g, 4 * g + 4)
                V(lambda: nc.vector.tensor_tensor(out=tmp[:, fs, :],
                                                  in0=BCb[:, g * 4 + s, :].unsqueeze(1).to_broadcast([128, 4, 128]),
                                                  in1=xd[:, s, fs].unsqueeze(2).to_broadcast([128, 4, 128]),
                                                  op=ALU.mult), r=[BCb.res, xd.res], w=[tmp.res])
            V(lambda: nc.vector.tensor_tensor(out=st[:, s], in0=st[:, s], in1=tmp[:], op=ALU.add), r=[tmp.res],
              w=[st.res])
            for g in range(2):
                fs = slice(4 * g, 4 * g + 4)
                V(lambda: nc.vector.tensor_tensor(out=tmp[:, fs, :], in0=st[:, s, fs, :],
                                                  in1=BCb[:, (2 + g) * 4 + s, :].unsqueeze(1).to_broadcast([128, 4, 128]),
                                                  op=ALU.mult), r=[BCb.res, st.res], w=[tmp.res])
            V(lambda: nc.vector.tensor_reduce(out=yT[:, s, :], in_=tmp[:], axis=AX.X, op=ALU.add), r=[tmp.res],
              w=[yT.res])
            S.dma("sp" if s % 2 == 0 else "pool", lambda q: q.dma_start(
                out=O["ssm_s%d" % j][s].rearrange("(f q) n -> q f n", q=128), in_=st[:, s]), reads=[st.res])

        V(lambda: nc.vector.tensor_tensor(out=yv[:, 0], in0=xbcT[:, 0:8, :], in1=ex[:, :, 8:9].to_broadcast([128, 8, 4]),
                                          op=ALU.mult), r=[xbcT.res, ex.res], w=[yv.res])
        V(lambda: nc.vector.tensor_tensor(out=yv[:, 0], in0=yv[:, 0], in1=yT[:].rearrange("p s f -> p f s"), op=ALU.add),
          r=[yT.res], w=[yv.res])
        V(lambda: nc.vector.tensor_tensor(out=yv[:, 0], in0=yv[:, 0], in1=zaT[:], op=ALU.mult), r=[zaT.res], w=[yv.res])
        V(lambda: nc.vector.tensor_tensor(out=yv[:, 1], in0=yv[:, 0], in1=yv[:, 0], op=ALU.mult), r=[], w=[yv.res])
        pl = PF()
        mm(pl, pl[:, 0:32], [(ones_f[:], yv[:, 1].rearrange("p f s -> p (f s)"))], [ones_f.res, yv.res])
        V(lambda: nc.vector.tensor_reduce(out=rs[:], in_=pl[:, 0:32].rearrange("p (g f s) -> p g s f", g=2, f=4, s=4),
                                          axis=AX.X, op=ALU.add), r=[pl.res], w=[rs.res])
        V(lambda: nc.vector.tensor_scalar(out=rs[:], in0=rs[:], scalar1=1.0 / 512, scalar2=EPS, op0=ALU.mult,
                                          op1=ALU.add), r=[], w=[rs.res])
        A(lambda: nc.scalar.activation(out=rs[:], in_=rs[:], func=AF.Sqrt), r=[], w=[rs.res])
        V(lambda: nc.vector.reciprocal(out=rs[:], in_=rs[:]), r=[], w=[rs.res])
        for g in range(2):
            fs = slice(4 * g, 4 * g + 4)
            V(lambda: nc.vector.tensor_tensor(out=yv[:, 0, fs, :], in0=yv[:, 0, fs, :],
                                              in1=rs[:, g, :].unsqueeze(1).to_broadcast([128, 4, 4]), op=ALU.mult),
              r=[rs.res], w=[yv.res])
        V(lambda: nc.vector.tensor_tensor(out=ycT[:, 0:8, :], in0=yv[:, 0], in1=ngT[:].unsqueeze(2).to_broadcast([128, 8, 4]),
                                          op=ALU.mult), r=[yv.res, ngT.res], w=[ycT.res])
        S.barrier()


def sample_odd(C, j, w_in):
    nc, S, sb, PF, PB, mm, tr, V, A, G, I, O = _unpack(C)
    K = C.consts
    xsT, ycT = C.xsT, C.ycT
    ones_f, ones_bf, ident_f, Esel, bmk = K["ones_f"], K["ones_bf"], K["ident_f"], K["Esel"], K["bmk"]
    WSC = (8.0 ** -0.5) * (64.0 ** -0.5)
    NPG = 65
    NIT = 24
    jc = 0 if os.environ.get("SCACHE0") else j
    ck, cvv, cki = I["cache_k%d" % jc], I["cache_v%d" % jc], I["cache_ki%d" % jc]
    with contextlib.ExitStack() as es:
        sel0 = sb(es, "sel0", [4, 4, 128])
        G(lambda: nc.gpsimd.affine_select(out=sel0[:], in_=Esel[0:4, 0:4, :], pattern=[[0, 4], [1, 128]],
                                          compare_op=ALU.is_equal, fill=0.0, base=0, channel_multiplier=0),
          r=[Esel.res], w=[sel0.res])
        pt_tm = sb(es, "sd_pt", [4, 2120])
        zT = sb(es, "sd_zT", [128, 8, 4])
        qb = sb(es, "sd_qb", [128, 4, 1024])
        qib = sb(es, "sd_qib", [128, 4, 512])
        wib = sb(es, "sd_wib", [128, 4, 8])
        kin = sb(es, "sd_kin", [128, 4, 64])
        kvn = sb(es, "sd_kvn", [128, 4, 512])
        ptb = sb(es, "sd_ptb", [128, 256], I32)
        ptf = sb(es, "sd_ptf", [128, 256])
        pidx = sb(es, "sd_pidx", [128, 256], I32)
        Isc = sb(es, "sd_Isc", [128, 4, NPG])
        msk = sb(es, "sd_msk", [128, 4, NPG])
        cmpb = sb(es, "sd_cmpb", [128, 4, NPG])
        cntp = sb(es, "sd_cntp", [128, 4])
        sc_all = sb(es, "sd_scall", [128, NPG, 8])
        kip = [sb(es, "sd_kip%d" % i, [128, 64]) for i in range(4)]
        prod = [sb(es, "sd_prod%d" % i, [128, 8, 64]) for i in range(2)]
        kpg = [sb(es, "sd_kpg%d" % i, [128, 256]) for i in range(4)]
        vpg = [sb(es, "sd_vpg%d" % i, [128, 260]) for i in range(4)]
        vnw = sb(es, "sd_vnw", [128, 260])
        prod2 = [sb(es, "sd_prod2_%d" % i, [128, 16, 64]) for i in range(2)]
        lg = sb(es, "sd_lg", [128, NPG, 16])
        PT = sb(es, "sd_PT", [128, NPG, 16])
        pm = sb(es, "sd_pm", [128, 8])
        gm = sb(es, "sd_gm", [8, 1])
        dg = sb(es, "sd_dg", [8, 8])
        lo = sb(es, "sd_lo", [128, 4])
        hi = sb(es, "sd_hi", [128, 4])
        mid = sb(es, "sd_mid", [128, 4])
        ge = sb(es, "sd_ge", [128, 4])
        t1 = sb(es, "sd_t1", [128, 4])
        t2 = sb(es, "sd_t2", [128, 4])
        om = sb(es, "sd_om", [16, 4, 64])
        o1 = sb(es, "sd_o1", [16, 64])
        rd = sb(es, "sd_rd", [16, 1])
        od = sb(es, "sd_od", [16, 4, 128])

        S.dma("sp", lambda q: q.dma_start(out=ptb[:], in_=I["page_table"][0:1, :].to_broadcast([128, 256])),
              writes=[ptb.res])
        for t in vpg + [vnw]:
            V(lambda: nc.vector.memset(t[:, 256:260], 1.0), w=[t.res])

        for c in range(5):
            c0 = c * 512
            w_ = min(512, 2120 - c0)
            p = PF()
            mm(p, p[0:4, 0:w_], [(xsT[:, k, :], w_in[:, k, c0:c0 + w_]) for k in range(8)], [w_in.res, xsT.res])
            if c % 2 == 0:
                A(lambda: nc.scalar.copy(out=pt_tm[:, c0:c0 + w_], in_=p[0:4, 0:w_]), r=[p.res], w=[pt_tm.res])
            else:
                V(lambda: nc.vector.tensor_copy(out=pt_tm[:, c0:c0 + w_], in_=p[0:4, 0:w_]), r=[p.res], w=[pt_tm.res])
        S.dma("sp", lambda q: q.dma_start(out=O["k_s%d" % j][:, :], in_=pt_tm[:, 1024:1280]), reads=[pt_tm.res])
        S.dma("sp", lambda q: q.dma_start(out=O["v_s%d" % j][:, :], in_=pt_tm[:, 1280:1536]), reads=[pt_tm.res])
        S.dma("sp", lambda q: q.dma_start(out=O["ki_s%d" % j][:, :], in_=pt_tm[:, 2048:2112]), reads=[pt_tm.res])
        V(lambda: nc.vector.tensor_scalar(out=pt_tm[:, 2112:2120], in0=pt_tm[:, 2112:2120], scalar1=WSC, scalar2=None,
                                          op0=ALU.mult), r=[], w=[pt_tm.res])
        pz = PF()
        for f in range(8):
            mm(pz, pz[:, f * 4:(f + 1) * 4], [(w_in[:, k, 2120 + f * 128:2120 + (f + 1) * 128], xsT[:, k, :])
                                               for k in range(8)], [w_in.res, xsT.res], first=(f == 0))
        A(lambda: nc.scalar.activation(out=zT[:].rearrange("p f s -> p (f s)"), in_=pz[:, 0:32], func=AF.Silu),
          r=[pz.res], w=[zT.res])
        ci = 0

        def bc(lhs, c0, w_, dst):
            nonlocal ci
            p = PF()
            mm(p, p[:, 0:w_], [(lhs, pt_tm[0:4, c0:c0 + w_])], [Esel.res, sel0.res, pt_tm.res])
            ci += 1
            if ci % 2 == 0:
                A(lambda: nc.scalar.copy(out=dst, in_=p[:, 0:w_]), r=[p.res], w=[qb.res])
            else:
                V(lambda: nc.vector.tensor_copy(out=dst, in_=p[:, 0:w_]), r=[p.res], w=[qb.res])

        for s in range(4):
            bc(Esel[0:4, s, :], 0, 512, qb[:, s, 0:512])
            bc(Esel[0:4, s, :], 512, 512, qb[:, s, 512:1024])
            bc(Esel[0:4, s, :], 1536, 512, qib[:, s, :])
            bc(Esel[0:4, s, :], 2112, 8, wib[:, s, :])
            bc(sel0[0:4, s, :], 2048, 64, kin[:, s, :])
            bc(sel0[0:4, s, :], 1024, 512, kvn[:, s, :])
        V(lambda: nc.vector.tensor_copy(out=ptf[:], in_=ptb[:]), r=[ptb.res], w=[ptf.res])
        V(lambda: nc.vector.tensor_scalar(out=pidx[:], in0=ptf[:], scalar1=128.0, scalar2=K["iota_f"][:, 0:1],
                                          op0=ALU.mult, op1=ALU.add), r=[ptf.res, K["iota_f"].res], w=[pidx.res])

        if SODD < 99:
            S.dma("sp", lambda q: q.dma_start(out=O["dbg_i"][:, :], in_=pidx[:]), reads=[pidx.res])
        if SODD < 1:
            S.barrier()
            return

        def gather(dst_tile, dst_ap, cache, col):
            S.dma("pool", lambda q: q.indirect_dma_start(
                out=dst_ap, out_offset=None, in_=cache[:, :],
                in_offset=bass.IndirectOffsetOnAxis(ap=pidx[:, col:col + 1], axis=0)),
                reads=[pidx.res], writes=[dst_tile.res])

        for s in range(4):
            for pg in range(NPG):
                if pg < 64:
                    kb = kip[pg % 4]
                    gather(kb, kb[:], cki, s * 64 + pg)
                    src, srcr = kb[:], [kb.res]
                else:
                    src, srcr = kin[:, s, :], [qb.res]
                pr = prod[pg % 2]
                V(lambda: nc.vector.tensor_tensor(out=pr[:], in0=qib[:, s, :].rearrange("p (h d) -> p h d", d=64),
                                                  in1=src.unsqueeze(1).to_broadcast([128, 8, 64]), op=ALU.mult),
                  r=[qb.res] + srcr, w=[pr.res])
                V(lambda: nc.vector.tensor_reduce(out=sc_all[:, pg, :], in_=pr[:], axis=AX.X, op=ALU.add), r=[pr.res],
                  w=[sc_all.res])
            V(lambda: nc.vector.scalar_tensor_tensor(out=sc_all[:], in0=sc_all[:], scalar=0.0,
                                                     in1=wib[:, s, :].unsqueeze(1).to_broadcast([128, NPG, 8]),
                                                     op0=ALU.max, op1=ALU.mult), r=[qb.res], w=[sc_all.res])
            V(lambda: nc.vector.tensor_reduce(out=Isc[:, s, :], in_=sc_all[:], axis=AX.X, op=ALU.add), r=[sc_all.res],
              w=[Isc.res])
        V(lambda: nc.vector.tensor_tensor(out=Isc[:, :, 64], in0=Isc[:, :, 64],
                                          in1=K["negpage"][:, 0:1].to_broadcast([128, 4]), op=ALU.add),
          r=[K["negpage"].res], w=[Isc.res])

        if SODD < 99:
            S.dma("sp", lambda q: q.dma_start(out=O["dbg_f"][:, :], in_=Isc[:].rearrange("p s g -> p (s g)")),
                  reads=[Isc.res])
        if SODD < 2:
            S.barrier()
            return
        V(lambda: nc.vector.tensor_reduce(out=pm[:, 0:4], in_=Isc[:], axis=AX.X, op=ALU.max), r=[Isc.res], w=[pm.res])
        V(lambda: nc.vector.tensor_reduce(out=pm[:, 4:8], in_=Isc[:, :, 0:64], axis=AX.X, op=ALU.min), r=[Isc.res],
          w=[pm.res])
        V(lambda: nc.vector.tensor_scalar(out=pm[:, 4:8], in0=pm[:, 4:8], scalar1=-1.0, scalar2=None, op0=ALU.mult),
          r=[], w=[pm.res])
        pT = PF()
        tr(pT, pT[0:8, 0:128], pm[:, 0:8], [pm.res], bf=False)
        V(lambda: nc.vector.tensor_reduce(out=gm[:], in_=pT[0:8, 0:128], axis=AX.X, op=ALU.max), r=[pT.res], w=[gm.res])
        V(lambda: nc.vector.tensor_scalar(out=dg[:], in0=ident_f[0:8, 0:8], scalar1=gm[:, 0:1], scalar2=None,
                                          op0=ALU.mult), r=[ident_f.res, gm.res], w=[dg.res])
        pb_ = PF()
        mm(pb_, pb_[:, 0:8], [(ones_f[0:8, :], dg[:])], [ones_f.res, dg.res])
        V(lambda: nc.vector.tensor_scalar(out=hi[:], in0=pb_[:, 0:4], scalar1=1.0, scalar2=None, op0=ALU.add),
          r=[pb_.res], w=[hi.res])
        V(lambda: nc.vector.tensor_scalar(out=lo[:], in0=pb_[:, 4:8], scalar1=-1.0, scalar2=None, op0=ALU.mult),
          r=[pb_.res], w=[lo.res])
        for it in range(NIT):
            V(lambda: nc.vector.tensor_tensor(out=mid[:], in0=lo[:], in1=hi[:], op=ALU.add), r=[lo.res, hi.res],
              w=[mid.res])
            V(lambda: nc.vector.tensor_scalar(out=mid[:], in0=mid[:], scalar1=0.5, scalar2=None, op0=ALU.mult), r=[],
              w=[mid.res])
            V(lambda: nc.vector.tensor_tensor(out=cmpb[:], in0=Isc[:], in1=mid[:].unsqueeze(2).to_broadcast([128, 4, NPG]),
                                              op=ALU.is_ge), r=[Isc.res, mid.res], w=[cmpb.res])
            V(lambda: nc.vector.tensor_reduce(out=cntp[:], in_=cmpb[:], axis=AX.X, op=ALU.add), r=[cmpb.res],
              w=[cntp.res])
            pc = PF()
            mm(pc, pc[:, 0:4], [(ones_f[:], cntp[:])], [ones_f.res, cntp.res])
            V(lambda: nc.vector.tensor_scalar(out=ge[:], in0=pc[:, 0:4], scalar1=256.0, scalar2=None, op0=ALU.is_ge),
              r=[pc.res], w=[ge.res])
            V(lambda: nc.vector.tensor_tensor(out=t1[:], in0=mid[:], in1=lo[:], op=ALU.subtract), r=[], w=[t1.res])
            V(lambda: nc.vector.tensor_tensor(out=t1[:], in0=t1[:], in1=ge[:], op=ALU.mult), r=[], w=[t1.res])
            V(lambda: nc.vector.tensor_tensor(out=t2[:], in0=hi[:], in1=mid[:], op=ALU.subtract), r=[], w=[t2.res])
            V(lambda: nc.vector.tensor_tensor(out=t2[:], in0=t2[:], in1=ge[:], op=ALU.mult), r=[], w=[t2.res])
            V(lambda: nc.vector.tensor_tensor(out=lo[:], in0=lo[:], in1=t1[:], op=ALU.add), r=[], w=[lo.res])
            V(lambda: nc.vector.tensor_tensor(out=hi[:], in0=mid[:], in1=t2[:], op=ALU.add), r=[], w=[hi.res])
        V(lambda: nc.vector.tensor_tensor(out=msk[:], in0=Isc[:], in1=lo[:].unsqueeze(2).to_broadcast([128, 4, NPG]),
                                          op=ALU.is_ge), r=[Isc.res, lo.res], w=[msk.res])

        if SODD < 3:
            S.barrier()
            return
        pTo = PF()
        for s in range(4):
            for pg in range(NPG):
                if pg < 64:
                    kb = kpg[pg % 4]
                    gather(kb, kb[:], ck, s * 64 + pg)
                    src, srcr = kb[:], [kb.res]
                else:
                    src, srcr = kvn[:, s, 0:256], [qb.res]
                pr = prod2[pg % 2]
                V(lambda: nc.vector.tensor_tensor(
                    out=pr[:].rearrange("p (g r) d -> p g r d", g=4),
                    in0=qb[:, s, :].rearrange("p (g r d) -> p g r d", g=4, r=4),
                    in1=src.rearrange("p (g d) -> p g d", d=64).unsqueeze(2).to_broadcast([128, 4, 4, 64]),
                    op=ALU.mult), r=[qb.res] + srcr, w=[pr.res])
                V(lambda: nc.vector.tensor_reduce(out=lg[:, pg, :], in_=pr[:], axis=AX.X, op=ALU.add), r=[pr.res],
                  w=[lg.res])
            A(lambda: nc.scalar.activation(out=PT[:], in_=lg[:], func=AF.Exp, scale=0.125), r=[lg.res], w=[PT.res])
            V(lambda: nc.vector.tensor_tensor(out=PT[:], in0=PT[:], in1=msk[:, s, :].unsqueeze(2).to_broadcast([128, NPG, 16]),
                                              op=ALU.mult), r=[msk.res], w=[PT.res])
            if SODD < 3.5:
                continue
            po = PF()
            for pg in range(NPG):
                if pg < 64:
                    vb = vpg[pg % 4]
                    gather(vb, vb[:, 0:256], cvv, s * 64 + pg)
                else:
                    vb = vnw
                    V(lambda: nc.vector.tensor_copy(out=vnw[:, 0:256], in_=kvn[:, s, 256:512]), r=[qb.res], w=[vnw.res])
                mm(po, po[0:16, 0:260], [(PT[:, pg, :], vb[:])], [PT.res, vb.res], first=(pg == 0), start=(pg == 0),
                   stop=(pg == NPG - 1))
            if SODD < 3.7:
                continue
            V(lambda: nc.vector.tensor_tensor(out=om[:], in0=po[0:16, 0:256].rearrange("p (g d) -> p g d", d=64),
                                              in1=bmk[:].unsqueeze(2).to_broadcast([16, 4, 64]), op=ALU.mult),
              r=[po.res, bmk.res], w=[om.res])
            V(lambda: nc.vector.tensor_reduce(out=o1[:], in_=om[:].rearrange("p g d -> p d g"), axis=AX.X, op=ALU.add),
              r=[om.res], w=[o1.res])
            V(lambda: nc.vector.reciprocal(out=rd[:], in_=po[0:16, 256:257]), r=[po.res], w=[rd.res])
            V(lambda: nc.vector.tensor_scalar(out=od[:, s, 0:64], in0=o1[:], scalar1=rd[:, 0:1], scalar2=None,
                                              op0=ALU.mult), r=[o1.res, rd.res], w=[od.res])
            V(lambda: nc.vector.tensor_scalar(out=od[:, s, 64:128], in0=o1[:], scalar1=rd[:, 0:1], scalar2=None,
                                              op0=ALU.mult), r=[o1.res, rd.res], w=[od.res])
            tr(pTo, pTo[:, s * 16:(s + 1) * 16], od[0:16, s, :], [od.res], bf=False, first=(s == 0))
        if SODD < 3.8:
            S.barrier()
            return
        oT = sb(es, "sd_oT", [128, 64])
        V(lambda: nc.vector.tensor_copy(out=oT[:], in_=pTo[:, 0:64]), r=[pTo.res], w=[oT.res])
        yc32 = sb(es, "sd_yc32", [128, 8, 4])
        for e in range(2):
            ps_ = slice(e * 64, (e + 1) * 64)
            for s in range(4):
                V(lambda: nc.vector.tensor_tensor(
                    out=yc32[ps_, :, s],
                    in0=oT[ps_, s * 16:(s + 1) * 16].rearrange("p (kt e) -> p kt e", e=2)[:, :, e],
                    in1=zT[ps_, :, s], op=ALU.mult), r=[oT.res, zT.res], w=[yc32.res])
        V(lambda: nc.vector.tensor_copy(out=ycT[:, 0:8, :], in_=yc32[:]), r=[yc32.res], w=[ycT.res])
        S.barrier()


_L = 8192
_PROMPT_KEYS = ["w_in_even", "ssd_conv_w", "ssd_conv_b", "ssd_dt_bias", "ssd_a_log", "ssd_d", "ssd_norm_g", "cf_dw_w",
                "cf_dw_b", "cf_ln_g", "cf_ln_b", "w_out_even", "w_in_odd", "w_out_odd", "ln_g", "ln_b"]


def core_inputs(inp, c, xp):
    m = {}
    if xp is not None:
        m["x_prompt"] = xp
    for k in _PROMPT_KEYS:
        m[k] = inp[k]
    sl = slice(4 * c, 4 * c + 4)
    m["x_sample"] = np.ascontiguousarray(inp["x_sample"][sl].reshape(4, D))
    m["page_table"] = np.ascontiguousarray(inp["page_table"][sl].reshape(1, 256).astype(np.int32))
    st = [(inp["state_ssm_l0"], inp["state_ssdconv_l0"], inp["state_cfconv_l0"]),
          (inp["state_ssm_l2"], inp["state_ssdconv_l2"], inp["state_cfconv_l2"])]
    ch = [(inp["cache_k_l1"], inp["cache_v_l1"], inp["cache_kidx_l1"]),
          (inp["cache_k_l3"], inp["cache_v_l3"], inp["cache_kidx_l3"])]
    for j in range(2):
        m["state_ssm%d" % j] = np.ascontiguousarray(st[j][0][sl].reshape(4, 1024, 128))
        m["state_ssdconv%d" % j] = np.ascontiguousarray(st[j][1][sl])
        m["state_cfconv%d" % j] = np.ascontiguousarray(st[j][2][sl])
        m["cache_k%d" % j] = ch[j][0].reshape(-1, 256)
        m["cache_v%d" % j] = ch[j][1].reshape(-1, 256)
        m["cache_ki%d" % j] = ch[j][2].reshape(-1, 64)
    return m


def kernel(**inputs):
    inp = {k: np.ascontiguousarray(np.asarray(v)) for k, v in inputs.items()}
    n = 8
    nc = build(_L, layers=4)
    in_maps = [core_inputs(inp, c, inp["x_prompt"][c // 4]) for c in range(n)]
    res = run_bass_kernel_spmd(nc, in_maps, core_ids=list(range(n)))
    R = res.results

    def pr(name, shape):
        return np.stack([np.asarray(R[0][name], np.float32).reshape(shape),
                         np.asarray(R[4][name], np.float32).reshape(shape)])

    def sm(name, shape):
        return np.concatenate([np.asarray(R[c][name], np.float32).reshape((4,) + shape) for c in range(n)], axis=0)

    outs = [pr("y_prompt", (_L, D)), sm("y_s", (1, D))]
    for l in range(4):
        j = l // 2
        if l % 2 == 0:
            outs += [pr("ssm_p%d" % j, (16, 64, 128)), sm("ssm_s%d" % j, (16, 64, 128)),
                     pr("sc_p%d" % j, (3, 1536)), sm("sc_s%d" % j, (3, 1536)),
                     pr("cc_p%d" % j, (30, 1024)), sm("cc_s%d" % j, (30, 1024))]
        else:
            outs += [pr("k_p%d" % j, (_L, 4, 64)), sm("k_s%d" % j, (1, 4, 64)),
                     pr("v_p%d" % j, (_L, 4, 64)), sm("v_s%d" % j, (1, 4, 64)),
                     pr("ki_p%d" % j, (_L, 64)), sm("ki_s%d" % j, (1, 64))]
    return tuple(outs)
```
